# Optimizing a Trainium2 kernel written in Bass

```python
import math
import jax, jax.numpy as jnp
from jax import lax
import numpy as np

D_MODEL = 1024
BATCH = 8
SEQ = 2048
DEPTH = 1
DEC_BATCH = 128
DEC_SEQ = 1
PAST_LEN = 16384
PAGE_SIZE = 128

GDN_HEADS = 4
GDN_HEAD_DIM = 128
GDN_WIDTH = GDN_HEADS * GDN_HEAD_DIM
CONV_W = 4
GDN_CHUNK = 64
S5_WIDTH = D_MODEL - GDN_WIDTH
S5_GROUP = 16
S5_GROUPS = S5_WIDTH // S5_GROUP
S5_STATE = 64
MIX_WIDTH = GDN_WIDTH + S5_WIDTH
D_FF = 4 * D_MODEL
N_PROJ = 3 * GDN_WIDTH + GDN_WIDTH + 2 * GDN_HEADS + S5_WIDTH
DN_ALPHA = (2.0 * DEPTH) ** 0.25
DN_BETA = (8.0 * DEPTH) ** -0.25
NORM_EPS = 1e-6

kernel_name = "hymba_gdn_s5_deepnorm_step"


def layernorm(x, g, b):
    xf = x.astype(jnp.float32)
    mu = jnp.mean(xf, axis=-1, keepdims=True)
    var = jnp.mean(jnp.square(xf - mu), axis=-1, keepdims=True)
    y = (xf - mu) * lax.rsqrt(var + NORM_EPS) * g.astype(jnp.float32) + b.astype(jnp.float32)
    return y.astype(x.dtype)


def l2norm(a):
    af = a.astype(jnp.float32)
    return af * lax.rsqrt(jnp.sum(af * af, axis=-1, keepdims=True) + NORM_EPS)


def causal_short_conv(x, buf, w):
    xc = jnp.concatenate([buf.astype(x.dtype), x], axis=1)
    y = lax.conv_general_dilated(xc, w[:, None, :].astype(x.dtype), window_strides=(1,), padding='VALID',
                                 dimension_numbers=('NWC', 'WIO', 'NWC'), feature_group_count=x.shape[-1])
    return jax.nn.silu(y), xc[:, xc.shape[1] - (CONV_W - 1):]


def gated_delta_chunked(q, k, v, g, beta, s0):
    f32 = jnp.float32
    bsz, T = q.shape[0], q.shape[1]
    C = min(GDN_CHUNK, T)
    n = -(-T // C)
    pad = n * C - T

    def prep(a):
        a = a.astype(f32)
        if pad:
            a = jnp.pad(a, [(0, 0), (0, pad)] + [(0, 0)] * (a.ndim - 2))
        a = a.reshape((bsz, n, C) + a.shape[2:])
        return jnp.moveaxis(a, (1, 3), (0, 2))

    q, k, v, g, beta = prep(q), prep(k), prep(v), prep(g), prep(beta)
    q = q * (GDN_HEAD_DIM ** -0.5)
    gc = jnp.cumsum(g, axis=-1)
    tri = jnp.tril(jnp.ones((C, C), bool))
    strict = jnp.tril(jnp.ones((C, C), bool), -1)
    diff = gc[..., :, None] - gc[..., None, :]
    decay = jnp.where(tri, jnp.exp(jnp.where(tri, diff, 0.0)), 0.0)
    k_beta = k * beta[..., None]
    v_beta = v * beta[..., None]
    L = jnp.where(strict, jnp.einsum('nbhid,nbhjd->nbhij', k_beta, k) * decay, 0.0)
    rhs = jnp.concatenate([v_beta, k_beta * jnp.exp(gc)[..., None]], axis=-1)
    sol = lax.linalg.triangular_solve(L + jnp.eye(C, dtype=f32), rhs, left_side=True, lower=True,
                                      unit_diagonal=True)
    u = sol[..., :GDN_HEAD_DIM]
    w = sol[..., GDN_HEAD_DIM:]
    qk = jnp.where(tri, jnp.einsum('nbhid,nbhjd->nbhij', q, k) * decay, 0.0)

    def step(S, xs):
        q_c, k_c, u_c, w_c, qk_c, gc_c = xs
        v_new = u_c - jnp.einsum('bhcd,bhde->bhce', w_c, S)
        o = (jnp.einsum('bhcd,bhde->bhce', q_c * jnp.exp(gc_c)[..., None], S)
             + jnp.einsum('bhij,bhje->bhie', qk_c, v_new))
        g_last = gc_c[..., -1]
        S = (S * jnp.exp(g_last)[..., None, None]
             + jnp.einsum('bhcd,bhce->bhde', k_c * jnp.exp(g_last[..., None] - gc_c)[..., None], v_new))
        return S, o

    S, o = lax.scan(step, s0.astype(f32), (q, k, u, w, qk, gc))
    o = jnp.moveaxis(o, (0, 2), (1, 3)).reshape(bsz, n * C, GDN_HEADS, GDN_HEAD_DIM)[:, :T]
    return o, S


def s5_scan(u, h0_re, h0_im, a_re, a_im, b_re, b_im, c_re, c_im, d_skip, log_dt):
    f32 = jnp.float32
    bsz, T = u.shape[0], u.shape[1]
    uf = u.astype(f32)
    lam = lax.complex(a_re.astype(f32), a_im.astype(f32))
    dt = jnp.exp(log_dt.astype(f32))[:, None]
    lam_bar = jnp.exp(lam * dt)
    b_bar = ((lam_bar - 1.0) / lam)[..., None] * lax.complex(b_re.astype(f32), b_im.astype(f32))
    bu = jnp.einsum('gpc,btgc->btgp', b_bar, uf.reshape(bsz, T, S5_GROUPS, S5_GROUP).astype(jnp.complex64))
    h0 = lax.complex(h0_re.astype(f32), h0_im.astype(f32))
    bu = bu.at[:, 0].add(lam_bar * h0)
    a = jnp.broadcast_to(lam_bar, bu.shape)

    def combine(left, right):
        a1, b1 = left
        a2, b2 = right
        return a1 * a2, a2 * b1 + b2

    _, h = lax.associative_scan(combine, (a, bu), axis=1)
    c = lax.complex(c_re.astype(f32), c_im.astype(f32))
    y = jnp.real(jnp.einsum('gcp,btgp->btgc', c, h)).reshape(bsz, T, S5_WIDTH) + d_skip.astype(f32) * uf
    h_last = h[:, -1]
    return y, jnp.real(h_last), jnp.imag(h_last)


def hybrid_layer(x, conv_buf, s_gdn, h_re, h_im, w_in, w_conv, gdn_a_log, gdn_dt_bias, gdn_norm_w,
                 s5_a_re, s5_a_im, s5_b_re, s5_b_im, s5_c_re, s5_c_im, s5_d, s5_log_dt, w_glu, b_glu,
                 w_out, ln1_g, ln1_b, w_ff1, w_ff2, ln2_g, ln2_b):
    f32 = jnp.float32
    bsz, T, _ = x.shape
    proj = jnp.einsum('btd,dn->btn', x, w_in)
    i0 = 3 * GDN_WIDTH
    i1 = i0 + GDN_WIDTH
    i2 = i1 + GDN_HEADS
    i3 = i2 + GDN_HEADS
    qkv_raw, z, b_logit, a_logit, u = (proj[..., :i0], proj[..., i0:i1], proj[..., i1:i2],
                                        proj[..., i2:i3], proj[..., i3:])
    qkv, new_conv = causal_short_conv(qkv_raw, conv_buf, w_conv)
    q = l2norm(qkv[..., :GDN_WIDTH].reshape(bsz, T, GDN_HEADS, GDN_HEAD_DIM))
    k = l2norm(qkv[..., GDN_WIDTH:2 * GDN_WIDTH].reshape(bsz, T, GDN_HEADS, GDN_HEAD_DIM))
    v = qkv[..., 2 * GDN_WIDTH:].reshape(bsz, T, GDN_HEADS, GDN_HEAD_DIM)
    beta = jax.nn.sigmoid(b_logit.astype(f32))
    g = -jnp.exp(gdn_a_log.astype(f32)) * jax.nn.softplus(a_logit.astype(f32) + gdn_dt_bias.astype(f32))
    o, s_new = gated_delta_chunked(q, k, v, g, beta, s_gdn)
    o = (o * lax.rsqrt(jnp.mean(o * o, axis=-1, keepdims=True) + NORM_EPS) * gdn_norm_w.astype(f32)
         * jax.nn.silu(z.astype(f32).reshape(bsz, T, GDN_HEADS, GDN_HEAD_DIM)))
    o = o.reshape(bsz, T, GDN_WIDTH).astype(x.dtype)
    ys, hr, hi = s5_scan(u, h_re, h_im, s5_a_re, s5_a_im, s5_b_re, s5_b_im, s5_c_re, s5_c_im, s5_d, s5_log_dt)
    ys = jax.nn.gelu(ys).astype(x.dtype)
    ys = ys * jax.nn.sigmoid(jnp.einsum('btc,ce->bte', ys, w_glu) + b_glu)
    mix = jnp.einsum('btc,cd->btd', jnp.concatenate([o, ys], axis=-1), w_out)
    x = layernorm(DN_ALPHA * x + mix, ln1_g, ln1_b)
    h = jnp.square(jax.nn.relu(jnp.einsum('btd,df->btf', x, w_ff1)))
    x = layernorm(DN_ALPHA * x + jnp.einsum('btf,fd->btd', h, w_ff2), ln2_g, ln2_b)
    return x, new_conv, s_new, hr, hi


def setup_inputs(seed: int = 0) -> dict:
    key = jax.random.key(seed)
    ks = jax.random.split(key, 32)
    f32 = jnp.float32

    def nrm(k, shape, s):
        return s * jax.random.normal(k, shape, f32)

    x_prompt = nrm(ks[0], (BATCH, SEQ, D_MODEL), 1.0)
    x_sample = nrm(ks[1], (DEC_BATCH, DEC_SEQ, D_MODEL), 1.0)
    state_gdn = nrm(ks[2], (DEPTH, DEC_BATCH, GDN_HEADS, GDN_HEAD_DIM, GDN_HEAD_DIM), 0.1)
    state_conv = nrm(ks[3], (DEPTH, DEC_BATCH, CONV_W - 1, 3 * GDN_WIDTH), 1.0)
    state_ssm_re = nrm(ks[4], (DEPTH, DEC_BATCH, S5_GROUPS, S5_STATE), 0.1)
    state_ssm_im = nrm(ks[5], (DEPTH, DEC_BATCH, S5_GROUPS, S5_STATE), 0.1)
    col_scale = jnp.concatenate([jnp.ones((2 * GDN_WIDTH,), f32), jnp.full((GDN_WIDTH,), DN_BETA, f32),
                                 jnp.ones((GDN_WIDTH + 2 * GDN_HEADS,), f32), jnp.full((S5_WIDTH,), DN_BETA, f32)])
    w_in = nrm(ks[6], (DEPTH, D_MODEL, N_PROJ), D_MODEL ** -0.5) * col_scale
    w_conv = nrm(ks[7], (DEPTH, CONV_W, 3 * GDN_WIDTH), CONV_W ** -0.5)
    gdn_a_log = jnp.log(jax.random.uniform(ks[8], (DEPTH, GDN_HEADS), f32, 1.0, 16.0))
    dt = jnp.exp(jax.random.uniform(ks[9], (DEPTH, GDN_HEADS), f32, math.log(1e-3), math.log(1e-1)))
    gdn_dt_bias = dt + jnp.log(-jnp.expm1(-dt))
    gdn_norm_w = 1.0 + nrm(ks[10], (DEPTH, GDN_HEAD_DIM), 0.02)
    s5_a_re = -0.5 + nrm(ks[11], (DEPTH, S5_GROUPS, S5_STATE), 0.01)
    s5_a_im = jnp.pi * jnp.arange(S5_STATE, dtype=f32) + nrm(ks[12], (DEPTH, S5_GROUPS, S5_STATE), 0.01)
    s5_b_re = nrm(ks[13], (DEPTH, S5_GROUPS, S5_STATE, S5_GROUP), (2 * S5_GROUP) ** -0.5)
    s5_b_im = nrm(ks[14], (DEPTH, S5_GROUPS, S5_STATE, S5_GROUP), (2 * S5_GROUP) ** -0.5)
    s5_c_re = nrm(ks[15], (DEPTH, S5_GROUPS, S5_GROUP, S5_STATE), S5_STATE ** -0.5)
    s5_c_im = nrm(ks[16], (DEPTH, S5_GROUPS, S5_GROUP, S5_STATE), S5_STATE ** -0.5)
    s5_d = nrm(ks[17], (DEPTH, S5_WIDTH), 1.0)
    s5_log_dt = jax.random.uniform(ks[18], (DEPTH, S5_GROUPS), f32, math.log(1e-3), math.log(1e-1))
    w_glu = nrm(ks[19], (DEPTH, S5_WIDTH, S5_WIDTH), S5_WIDTH ** -0.5)
    b_glu = nrm(ks[20], (DEPTH, S5_WIDTH), 0.01)
    w_out = nrm(ks[21], (DEPTH, MIX_WIDTH, D_MODEL), MIX_WIDTH ** -0.5 * DN_BETA)
    ln1_g = 1.0 + nrm(ks[22], (DEPTH, D_MODEL), 0.02)
    ln1_b = nrm(ks[23], (DEPTH, D_MODEL), 0.01)
    w_ff1 = nrm(ks[24], (DEPTH, D_MODEL, D_FF), D_MODEL ** -0.5)
    w_ff2 = nrm(ks[25], (DEPTH, D_FF, D_MODEL), D_FF ** -0.5 * DN_BETA)
    ln2_g = 1.0 + nrm(ks[26], (DEPTH, D_MODEL), 0.02)
    ln2_b = nrm(ks[27], (DEPTH, D_MODEL), 0.01)
    return {"x_prompt": x_prompt, "x_sample": x_sample, "state_gdn": state_gdn, "state_conv": state_conv,
            "state_ssm_re": state_ssm_re, "state_ssm_im": state_ssm_im, "w_in": w_in, "w_conv": w_conv,
            "gdn_a_log": gdn_a_log, "gdn_dt_bias": gdn_dt_bias, "gdn_norm_w": gdn_norm_w,
            "s5_a_re": s5_a_re, "s5_a_im": s5_a_im, "s5_b_re": s5_b_re, "s5_b_im": s5_b_im,
            "s5_c_re": s5_c_re, "s5_c_im": s5_c_im, "s5_d": s5_d, "s5_log_dt": s5_log_dt,
            "w_glu": w_glu, "b_glu": b_glu, "w_out": w_out, "ln1_g": ln1_g, "ln1_b": ln1_b,
            "w_ff1": w_ff1, "w_ff2": w_ff2, "ln2_g": ln2_g, "ln2_b": ln2_b}


def reference(x_prompt, x_sample, state_gdn, state_conv, state_ssm_re, state_ssm_im, w_in, w_conv,
              gdn_a_log, gdn_dt_bias, gdn_norm_w, s5_a_re, s5_a_im, s5_b_re, s5_b_im, s5_c_re, s5_c_im,
              s5_d, s5_log_dt, w_glu, b_glu, w_out, ln1_g, ln1_b, w_ff1, w_ff2, ln2_g, ln2_b):
    f32 = jnp.float32
    bp = x_prompt.shape[0]
    yp, ys = x_prompt, x_sample
    p_gdn, p_conv, p_re, p_im = [], [], [], []
    s_gdn, s_conv, s_re, s_im = [], [], [], []
    for l in range(DEPTH):
        params = (w_in[l], w_conv[l], gdn_a_log[l], gdn_dt_bias[l], gdn_norm_w[l], s5_a_re[l], s5_a_im[l],
                  s5_b_re[l], s5_b_im[l], s5_c_re[l], s5_c_im[l], s5_d[l], s5_log_dt[l], w_glu[l], b_glu[l],
                  w_out[l], ln1_g[l], ln1_b[l], w_ff1[l], w_ff2[l], ln2_g[l], ln2_b[l])
        yp, c, s, hr, hi = hybrid_layer(
            yp, jnp.zeros((bp, CONV_W - 1, 3 * GDN_WIDTH), x_prompt.dtype),
            jnp.zeros((bp, GDN_HEADS, GDN_HEAD_DIM, GDN_HEAD_DIM), f32),
            jnp.zeros((bp, S5_GROUPS, S5_STATE), f32), jnp.zeros((bp, S5_GROUPS, S5_STATE), f32), *params)
        p_gdn.append(s)
        p_conv.append(c)
        p_re.append(hr)
        p_im.append(hi)
        ys, c, s, hr, hi = hybrid_layer(ys, state_conv[l], state_gdn[l], state_ssm_re[l], state_ssm_im[l], *params)
        s_gdn.append(s)
        s_conv.append(c)
        s_re.append(hr)
        s_im.append(hi)
    return (yp, ys, jnp.stack(p_gdn), jnp.stack(p_conv), jnp.stack(p_re), jnp.stack(p_im),
            jnp.stack(s_gdn), jnp.stack(s_conv), jnp.stack(s_re), jnp.stack(s_im))
```

```python
import contextlib
import math
import os
import numpy as np
import concourse.bass as bass
import concourse.mybir as mybir
from concourse.bass_utils import run_bass_kernel_spmd

F32 = mybir.dt.float32
BF16 = mybir.dt.bfloat16
AF = mybir.ActivationFunctionType
ALU = mybir.AluOpType
AX = mybir.AxisListType


class Rg:
    __slots__ = ("w", "r", "name")

    def __init__(self, name=""):
        self.w = None
        self.r = {}
        self.name = name


def rgs(n, name=""):
    return [Rg(f"{name}{i}") for i in range(n)]


class Prog:
    COMPUTE = ("pe", "act", "dve", "pool")
    NDMASEM = 6

    def __init__(self):
        self.ops = {e: [] for e in ("pe", "act", "dve", "pool", "sp")}
        self.dma_count = {}
        self.floor = set()
        self.nobar = set()

    def barrier(self):
        fl = set()
        for e in self.COMPUTE:
            if self.ops[e]:
                for i in range(len(self.ops[e]) - 1, -1, -1):
                    if not self.ops[e][i]["dma"]:
                        fl.add((e, i))
                        break
        for q, n in self.dma_count.items():
            for k in range(max(0, n - self.NDMASEM), n):
                if ("dma:" + q, k) not in self.nobar:
                    fl.add(("dma:" + q, k))
        self.floor = fl

    def op(self, eng, fn, reads=(), writes=(), dma=False):
        deps = set(self.floor)
        for r in reads:
            if r.w is not None:
                deps.add(r.w)
        for w in writes:
            if w.w is not None:
                deps.add(w.w)
            for e, i in w.r.items():
                deps.add((e, i))
        idx = len(self.ops[eng])
        if dma:
            k = self.dma_count.get(eng, 0)
            self.dma_count[eng] = k + 1
            me = ("dma:" + eng, k)
        else:
            me = (eng, idx)
            nd = set()
            for d in deps:
                if d[0] == eng:
                    if eng == "pe":
                        continue
                    isr = any((r.w == d) for r in reads) or any((w.w == d) for w in writes)
                    if not isr:
                        continue
                nd.add(d)
            deps = nd
        self.ops[eng].append(dict(fn=fn, deps=deps, dma=dma, me=me))
        for r in reads:
            r.r[me[0]] = me[1]
        for w in writes:
            w.w = me
            w.r = {}
        return me

    def emit(self, nc, stack):
        sems = {e: stack.enter_context(nc.semaphore("s_" + e)) for e in self.COMPUTE}
        dsems = {}
        for q in self.dma_count:
            dsems[q] = [stack.enter_context(nc.semaphore(f"d_{q}{i}")) for i in range(self.NDMASEM)]
        sig = {e: set() for e in self.COMPUTE}
        for e, lst in self.ops.items():
            for o in lst:
                for d in o["deps"]:
                    if d[0] in sig:
                        sig[d[0]].add(d[1])
        rank = {}
        for e in self.COMPUTE:
            rank[e] = {idx: i + 1 for i, idx in enumerate(sorted(sig[e]))}
        K = self.NDMASEM

        def dep_sem(d):
            s, i = d
            if s.startswith("dma:"):
                q = s[4:]
                return dsems[q][i % K], 16 * (i // K + 1), ("d", q, i % K)
            return sems[s], rank[s][i], ("c", s)

        block = stack.enter_context(nc.Block())
        getters = {"pe": block.tensor, "act": block.scalar, "dve": block.vector, "pool": block.gpsimd,
                   "sp": block.sync}

        def make(eng):
            lst = self.ops[eng]

            def body(E):
                waited = {}
                for o in lst:
                    need = {}
                    for d in o["deps"]:
                        sh, val, key = dep_sem(d)
                        if need.get(key, (None, 0))[1] < val:
                            need[key] = (sh, val)
                    if o["dma"]:
                        q, k = o["me"][0][4:], o["me"][1]
                        if k >= K:
                            key = ("d", q, k % K)
                            val = 16 * (k // K)
                            if need.get(key, (None, 0))[1] < val:
                                need[key] = (dsems[q][k % K], val)
                    for key, (sh, val) in need.items():
                        if waited.get(key, 0) >= val:
                            continue
                        E.wait_ge(sh, val)
                        waited[key] = val
                    ins = o["fn"](E)
                    if o["dma"]:
                        q, k = o["me"][0][4:], o["me"][1]
                        ins.then_inc(dsems[q][k % K], 16)
                    elif o["me"][1] in sig.get(eng, ()):
                        ins.then_inc(sems[eng], 1)
                n = self.dma_count.get(eng, 0)
                for j in range(min(K, n)):
                    cnt = (n - 1 - j) // K + 1
                    E.wait_ge(dsems[eng][j], 16 * cnt)
            return body

        for eng in ("sp", "pe", "act", "dve", "pool"):
            if self.ops[eng] or self.dma_count.get(eng, 0):
                getters[eng](make(eng))


NCORES = 8
D = 1024
TP = 2048
NS = 16
NT = TP + NS
NPROJ = 2568
DFF = 4096
ALPHA = 2.0 ** 0.25
EPS = 1e-6


class KB:
    def __init__(self, nc, stack):
        self.nc = nc
        self.st = stack
        self.p = Prog()
        self.ps = stack.enter_context(nc.psum_tensor("ps_all", [128, 4096], F32))
        self.bank_rg = rgs(8, "bank")
        self.bank_i = 0
        self.ins = {}
        self.outs = {}

    def sb(self, name, shape, dtype=F32):
        return self.st.enter_context(self.nc.sbuf_tensor(name, list(shape), dtype))

    def din(self, name, shape, dtype=F32):
        t = self.nc.dram_tensor(name, list(shape), dtype, kind="ExternalInput").ap()
        self.ins[name] = t
        return t

    def dout(self, name, shape, dtype=F32):
        t = self.nc.dram_tensor(name, list(shape), dtype, kind="ExternalOutput").ap()
        self.outs[name] = t
        return t

    def dtmp(self, name, shape, dtype):
        return self.nc.dram_tensor(name, list(shape), dtype, kind="Internal").ap()

    def bank(self):
        lst = getattr(self, "bank_list", None) or list(range(8))
        i = lst[self.bank_i % len(lst)]
        self.bank_i += 1
        return self.ps[:, i * 512:(i + 1) * 512], self.bank_rg[i]

    def fixed_bank(self, i):
        return self.ps[:, i * 512:(i + 1) * 512], self.bank_rg[i]

    def mm(self, out, lhsT, rhs, start, stop, rd, wr):
        return self.p.op("pe", lambda E: E.matmul(out, lhsT=lhsT, rhs=rhs, start=start, stop=stop), rd, wr)

    def tr(self, out, in_, ident, rd, wr):
        return self.p.op("pe", lambda E: E.transpose(out=out, in_=in_, identity=ident), rd, wr)

    def act(self, out, in_, func, rd, wr, bias=None, scale=None):
        kw = {}
        if bias is not None:
            kw["bias"] = bias
        if scale is not None:
            kw["scale"] = scale
        return self.p.op("act", lambda E: E.activation(out=out, in_=in_, func=func, **kw), rd, wr)

    def ts(self, eng, out, in0, s1, s2, op0, op1, rd, wr):
        if s2 is None:
            return self.p.op(eng, lambda E: E.tensor_scalar(out=out, in0=in0, scalar1=s1, scalar2=None, op0=op0),
                             rd, wr)
        return self.p.op(eng, lambda E: E.tensor_scalar(out=out, in0=in0, scalar1=s1, scalar2=s2, op0=op0, op1=op1),
                         rd, wr)

    def tt(self, eng, out, in0, in1, op, rd, wr):
        return self.p.op(eng, lambda E: E.tensor_tensor(out=out, in0=in0, in1=in1, op=op), rd, wr)

    def stt(self, eng, out, in0, scalar, in1, op0, op1, rd, wr):
        return self.p.op(eng, lambda E: E.scalar_tensor_tensor(out=out, in0=in0, scalar=scalar, in1=in1, op0=op0,
                                                               op1=op1), rd, wr)

    def cp(self, eng, out, in_, rd, wr):
        if eng == "act":
            return self.p.op("act", lambda E: E.copy(out=out, in_=in_), rd, wr)
        return self.p.op(eng, lambda E: E.tensor_copy(out=out, in_=in_), rd, wr)

    def memset(self, eng, ap, val, wr):
        return self.p.op(eng, lambda E: E.memset(ap, val), (), wr)

    def dma(self, q, out, in_, rd, wr, nobar=False, **kw):
        me = self.p.op(q, lambda E: E.dma_start(out=out, in_=in_, **kw), rd, wr, dma=True)
        if nobar:
            self.p.nobar.add(me)
        return me


class Arena:
    def __init__(self, kb, ncols):
        self.t = kb.sb("arena", [128, ncols], F32)
        self.n = ncols
        self.top = 0

    def f32(self, cols):
        a = self.top
        self.top += cols
        assert self.top <= self.n, ("arena overflow", self.top, self.n)
        return self.t[:, a:a + cols]

    def bf16(self, cols):
        c = (cols + 1) // 2
        a = self.top
        self.top += c
        assert self.top <= self.n, ("arena overflow", self.top, self.n)
        return self.t[:, a:a + c].bitcast(BF16)[:, 0:cols]


TBLK = [(0, 512), (512, 512), (1024, 512), (1536, 512), (2048, 16)]
CBLK = [(i * 256, 256) for i in range(8)] + [(2048, 16)]


def build(stage="full", debug=False):
    nc = bass.Bass("TRN2", target_bir_lowering=False)
    st = contextlib.ExitStack()
    kb = KB(nc, st)
    p = kb.p
    dbg = {}

    xs = kb.din("xs", [NT, D])
    w_in = kb.din("w_in", [D, NPROJ])
    w_out = kb.din("w_out", [D, D])
    w_ff1 = kb.din("w_ff1", [D, DFF])
    w_ff2 = kb.din("w_ff2", [DFF, D])
    wc_d = kb.din("wc", [128, 12, 4])
    ln_d = kb.din("ln", [4, D])
    ident_d = kb.din("ident", [128, 128])
    sconv_d = kb.din("state_conv", [NS, 3, 1536])
    y_d = kb.dout("y", [NT, D])
    convp_d = kb.dout("conv_p", [3, 1536])
    convs_d = kb.dout("conv_s", [NS, 3, 1536])
    w1s = kb.dtmp("w1s", [D, DFF], BF16)

    A = Arena(kb, 53200)
    ident_f = A.f32(128)
    ones_f = A.f32(128)
    onesq_f = A.f32(128)
    wc = A.f32(48).rearrange("p (m j) -> p m j", j=4)
    halo = A.f32(36).rearrange("p (m j) -> p m j", j=3)
    lg = A.f32(17 * 8).rearrange("p (t j) -> p t j", j=8)
    r_ident, r_ones, r_gb, r_wc, r_halo = Rg(), Rg(), Rg(), Rg(), rgs(12)
    r_lg = rgs(17)
    kb.dma("sp", ident_f, ident_d, [], [r_ident])
    kb.dma("sp", wc, wc_d, [], [r_wc])
    kb.memset("pool", ones_f, 1.0, [r_ones])
    kb.memset("pool", onesq_f, 128.0, [r_ones])
    kb.memset("pool", halo, 0.0, r_halo)
    kb.memset("pool", lg, 0.0, r_lg)
    epsc = A.f32(1)
    kb.memset("pool", epsc, EPS, [r_ones])
    base_persist = A.top
    qkv = A.bf16(12 * NT).rearrange("p (m t) -> p m t", m=12)
    zT = A.bf16(4 * NT).rearrange("p (m t) -> p m t", m=4)
    unat = A.bf16(4 * TP).rearrange("p (m t) -> p m t", m=4)
    uperm = unat.rearrange("p m (s c) -> p m s c", s=8)
    usamp = A.bf16(4 * NS).rearrange("p (m t) -> p m t", m=4)
    r_qkv = [[Rg() for _ in range(5)] for _ in range(12)]
    r_z = [[Rg() for _ in range(5)] for _ in range(4)]
    r_u = [[Rg() for _ in range(5)] for _ in range(4)]
    base_ab = A.top

    xT = A.bf16(8 * NT).rearrange("p (k t) -> p k t", k=8)
    w_in_sb = A.bf16(8 * NPROJ).rearrange("p (k n) -> p k n", k=8)
    xt = [A.f32(1024) for _ in range(2)]
    rawb = [A.f32(515) for _ in range(2)]
    acc = [A.f32(512) for _ in range(2)]
    big = A.f32(1536)
    tb = A.f32(512)
    r_tb = Rg()
    scv = A.f32(576).rearrange("p (m b) -> p m b", m=12)
    r_xt, r_rawb, r_acc = rgs(2), rgs(2), rgs(2)
    r_big, r_scv = Rg(), Rg()
    r_xT = [[Rg(), Rg()] for _ in range(17)]
    r_win = rgs(8)
    for kc in range(8):
        kb.dma("pool", w_in_sb[:, kc, :], w_in[kc * 128:(kc + 1) * 128, :], [], [r_win[kc]])
    r_w1s = rgs(8)
    w2s = kb.dtmp("w2s", [DFF, D], BF16)
    wos = kb.dtmp("wos", [D, D], BF16)
    r_w2s, r_wos = rgs(8), Rg()


    for t in range(17):
        rows = 128 if t < 16 else NS
        xb, rx = xt[t % 2], r_xt[t % 2]
        kb.dma("sp", xb[:rows, :], xs[t * 128:t * 128 + rows, :], [], [rx])
        bks = [kb.bank(), kb.bank()]
        for kc in range(8):
            bap, brg = bks[kc // 4]
            kb.tr(bap[:, (kc % 4) * 128:(kc % 4) * 128 + rows], xb[:rows, kc * 128:(kc + 1) * 128],
                  ident_f[:rows, :rows], [rx, r_ident], [brg])
        for h in range(2):
            bap, brg = bks[h]
            src = bap.rearrange("p (k t) -> p k t", k=4)[:, :, :rows]
            dst = xT[:, 4 * h:4 * h + 4, t * 128:t * 128 + rows]
            kb.cp("act" if h == 0 else "dve", dst, src, [brg], [r_xT[t][h]])

    sc_tok = big
    kb.dma("sp", sc_tok[:48, :], sconv_d.rearrange("b j c -> (b j) c"), [], [r_big])
    bks = [kb.bank(), kb.bank()]
    for m in range(12):
        bap, brg = bks[m // 6]
        kb.tr(bap[:, (m % 6) * 48:(m % 6) * 48 + 48], sc_tok[:48, m * 128:(m + 1) * 128], ident_f[:48, :48],
              [r_big, r_ident], [brg])
    for h in range(2):
        bap, brg = bks[h]
        kb.cp("dve", scv[:, 6 * h:6 * h + 6, :], bap[:, 0:288].rearrange("p (m b) -> p m b", m=6), [brg], [r_scv])

    def colof(m):
        if m < 16:
            return m * 128
        return 2056 + (m - 16) * 128

    evi = 0
    for b, (t0, N) in enumerate(TBLK):
        tiles = list(range(4 * b, 4 * b + 4)) if N == 512 else [16]
        rd_x = [r for t in tiles for r in r_xT[t]]
        for m in range(20):
            c0 = colof(m)
            bap, brg = kb.bank()
            for kc in range(8):
                kb.mm(bap[:, :N], w_in_sb[:, kc, c0:c0 + 128], xT[:, kc, t0:t0 + N], kc == 0, kc == 7,
                      rd_x + [r_win[kc]], [brg])
            evi += 1
            ee = "act" if evi % 2 == 0 else "dve"
            if m < 12:
                j = m % 2
                rb, ac = rawb[j], acc[j]
                kb.cp(ee, rb[:, 3:3 + N], bap[:, :N], [brg], [r_rawb[j]])
                if N == 512:
                    kb.cp("dve", rb[:, 0:3], halo[:, m, :], [r_halo[m]], [r_rawb[j]])
                    kb.cp("dve", halo[:, m, :], rb[:, 512:515], [r_rawb[j]], [r_halo[m]])
                    srcs = [rb[:, jj:jj + N] for jj in range(4)]
                    rdc = [r_rawb[j], r_wc]
                else:
                    srcs = [scv[:, m, jj::3] for jj in range(3)] + [rb[:, 3:3 + N]]
                    rdc = [r_rawb[j], r_wc, r_scv]
                kb.ts("dve", ac[:, :N], srcs[0], wc[:, m, 0:1], None, ALU.mult, None, rdc, [r_acc[j]])
                for jj in range(1, 4):
                    kb.stt("dve", ac[:, :N], srcs[jj], wc[:, m, jj:jj + 1], ac[:, :N], ALU.mult, ALU.add,
                           rdc + [r_acc[j]], [r_acc[j]])
                kb.act(qkv[:, m, t0:t0 + N], ac[:, :N], AF.Silu, [r_acc[j]], [r_qkv[m][b]])
            elif m < 16:
                kb.act(zT[:, m - 12, t0:t0 + N], bap[:, :N], AF.Silu, [brg], [r_z[m - 12][b]])
            else:
                if N == 512:
                    kb.cp(ee, uperm[:, m - 16, :, b * 64:(b + 1) * 64], bap.rearrange("p (c s) -> p s c", s=8), [brg],
                          [r_u[m - 16][b]])
                else:
                    kb.cp(ee, usamp[:, m - 16, :], bap[:, :N], [brg], [r_u[m - 16][b]])
        for t in tiles:
            rows = 128 if t < 16 else NS
            bap, brg = kb.bank()
            for kc in range(8):
                kb.mm(bap[:rows, 0:8], xT[:, kc, t * 128:t * 128 + rows], w_in_sb[:, kc, 2048:2056], kc == 0,
                      kc == 7, r_xT[t] + [r_win[kc]], [brg])
            kb.cp("dve", lg[:rows, t, :], bap[:rows, 0:8], [brg], [r_lg[t]])

    tailraw = big
    for nb in range(3):
        bap, brg = kb.bank()
        for kc in range(8):
            kb.mm(bap[:19, :], xT[:, kc, 2045:2064], w_in_sb[:, kc, nb * 512:(nb + 1) * 512], kc == 0, kc == 7,
                  r_xT[15] + r_xT[16] + [r_win[kc]], [brg])
        kb.cp("act", tailraw[:19, nb * 512:(nb + 1) * 512], bap[:19, :], [brg], [r_big])
    kb.dma("sp", convp_d, tailraw[0:3, :], [r_big], [Rg()])
    kb.dma("sp", convs_d[:, 2, :], tailraw[3:19, :], [r_big], [Rg()])
    kb.dma("sp", convs_d[:, 0:2, :], sconv_d[:, 1:3, :], [], [Rg()])

    if debug:
        dq = kb.dout("dbg_qkv", [128, 12 * NT], BF16)
        dz = kb.dout("dbg_z", [128, 4 * NT], BF16)
        du = kb.dout("dbg_u", [128, 4 * TP], BF16)
        dl = kb.dout("dbg_lg", [128, 17 * 8], F32)
        allq = [r for row in r_qkv for r in row]
        kb.dma("sp", dq, qkv.rearrange("p m t -> p (m t)"), allq, [Rg()])
        kb.dma("sp", dz, zT.rearrange("p m t -> p (m t)"), [r for row in r_z for r in row], [Rg()])
        kb.dma("sp", du, unat.rearrange("p m t -> p (m t)"), [r for row in r_u for r in row], [Rg()])
        kb.dma("sp", dl, lg.rearrange("p t j -> p (t j)"), r_lg, [Rg()])

    p.barrier()
    A.top = base_ab
    mixT = A.bf16(8 * NT).rearrange("p (k t) -> p k t", k=8)
    base_c = A.top
    r_mix = [Rg() for _ in range(9)]

    p.barrier()
    A.top = base_c
    cm_d = kb.din("gconst", [128, 4, 128])
    gp_d = kb.din("gdn_par", [3, 4])
    nw_d = kb.din("gdn_nw", [128, 1])
    sg_d = kb.din("state_gdn", [NS, 4, 128, 128])
    gdnp_d = kb.dout("gdn_p", [4, 128, 128])
    gdns_d = kb.dout("gdn_s", [NS, 4, 128, 128])
    cm = A.f32(512).rearrange("p (a n) -> p a n", a=4)
    triu, maskL, maskU, strict01 = cm[:, 0, :], cm[:, 1, :], cm[:, 2, :], cm[:, 3, :]
    negones = A.f32(128)
    ident_b = A.bf16(128)
    gpar = A.f32(8).rearrange("p (a h) -> p a h", a=2)
    nwc = A.f32(1)
    c1 = A.f32(2)
    beta = A.f32(68).rearrange("p (t h) -> p t h", h=4)
    gg = A.f32(68).rearrange("p (t h) -> p t h", h=4)
    r_cm, r_gp, r_gate = Rg(), Rg(), Rg()
    r_cgate = Rg()
    kb.dma("sp", cm, cm_d, [], [r_cm, r_cgate])
    kb.dma("sp", gpar[:, 0, :], gp_d[0:1, :].partition_broadcast(128), [], [r_gp])
    kb.dma("sp", gpar[:, 1, :], gp_d[1:2, :].partition_broadcast(128), [], [r_gp])
    kb.dma("sp", nwc, nw_d, [], [r_gp])
    kb.memset("pool", negones, -1.0, [r_cm])
    kb.memset("pool", c1[:, 0:1], 1.0, [r_cm])
    kb.memset("pool", c1[:, 1:2], EPS, [r_cm])
    kb.cp("pool", ident_b, ident_f, [r_ident], [r_cm])
    one_c, eps_c = c1[:, 0:1], c1[:, 1:2]
    kb.act(gpar[:, 0, :], gpar[:, 0, :], AF.Exp, [r_gp], [r_gp])
    kb.ts("dve", gpar[:, 0, :], gpar[:, 0, :], -1.0, None, ALU.mult, None, [r_gp], [r_gp])
    kb.act(beta, lg[:, :, 0:4], AF.Exp, r_lg, [r_gate], scale=-1.0)
    kb.ts("dve", beta, beta, 1.0, None, ALU.add, None, [r_gate], [r_gate])
    p.op("dve", lambda E: E.reciprocal(out=beta, in_=beta), [r_gate], [r_gate])
    kb.tt("dve", gg, lg[:, :, 4:8], gpar[:, 1, :].unsqueeze(1).to_broadcast([128, 17, 4]), ALU.add, r_lg + [r_gp],
          [r_gate])
    kb.act(gg, gg, AF.Exp, [r_gate], [r_gate])
    kb.act(gg, gg, AF.Ln, [r_gate, r_cm], [r_gate], bias=one_c)
    kb.tt("dve", gg, gg, gpar[:, 0, :].unsqueeze(1).to_broadcast([128, 17, 4]), ALU.mult, [r_gate, r_gp], [r_gate])

    H4 = lambda n_: [128, 4, n_]
    Sst = A.f32(512).rearrange("p (h e) -> p h e", h=4)
    Sb = A.bf16(512).rearrange("p (h e) -> p h e", h=4)
    r_S, r_Sb = Rg(), Rg()
    kb.memset("dve", Sst, 0.0, [r_S])
    kb.memset("dve", Sb, 0.0, [r_Sb])
    gate = [r_cgate]
    for kc in range(8):
        kb.dma("pool", w1s[kc * 128:(kc + 1) * 128, :], w_ff1[kc * 128:(kc + 1) * 128, :], gate, [r_w1s[kc]],
               nobar=True)
    kb.dma("pool", wos, w_out, gate, [r_wos], nobar=True)
    for g in range(8):
        kb.dma("pool", w2s[g * 512:(g + 1) * 512, :], w_ff2[g * 512:(g + 1) * 512, :], gate, [r_w2s[g]], nobar=True)

    def b4(n=512):
        return A.bf16(n).rearrange("p (h e) -> p h e", h=4)

    def f4(n=512):
        return A.f32(n).rearrange("p (h e) -> p h e", h=4)
    gcs = A.f32(8)
    sml = A.f32(40).rearrange("p (a h) -> p a h", h=4)
    Gm, Dm, DTm, sqt = f4(), f4(), f4(), f4()
    Lf = sqt
    alias_base = A.top
    khtok, qhtok, qgtok = b4(), b4(), b4()
    khT, qhT = b4(), b4()
    CHB = 0
    c4 = f4
    ident_c, r_identc = ident_f, r_ident
    Mb = [c4(), c4()]
    MTb = [c4(), c4()]
    TTb = [c4(), c4()]
    TTf = [f4(), f4()]
    bvP = [f4(), f4()]
    begkP = [f4(), f4()]
    qkTP = [b4(), b4()]
    qgTP = [b4(), b4()]
    kdecP = [b4(), b4()]
    eglP = [A.f32(4), A.f32(4)]
    nwT, vnew, onb, sqt2 = f4(), b4(), b4(), f4()
    so2 = A.f32(4)
    r_gcs, r_sml = Rg(), Rg()
    R_ = {k: Rg(k) for k in ["khtok", "qhtok", "qgtok", "khT", "qhT", "Gm", "Dm", "DTm", "sqt", "M0", "M1", "MT0",
                             "MT1", "TT0", "TT1", "nwT", "vnew", "onb", "sqt2", "so2"]}
    RP = [{k: Rg(k + str(q)) for k in ["TTf", "bv", "begk", "qkT", "qgT", "kdec", "egl"]} for q in range(2)]

    def bfbank():
        bap, brg = kb.bank()
        return bap.bitcast(BF16)[:, 0:512].rearrange("p (h e) -> p h e", h=4), brg

    def fbank():
        bap, brg = kb.bank()
        return bap.rearrange("p (h e) -> p h e", h=4), brg

    def bc(col, n=128, rows=128):
        return col.unsqueeze(2).to_broadcast([rows, 4, n])

    def hb_(mat, rows=128):
        return mat.unsqueeze(1).to_broadcast([rows, 4, 128])

    def rsq(dst, src, rows, scale_in, rd, wr):
        kb.act(dst, src, AF.Ln, rd + [r_cm], wr, bias=eps_c[:rows, :], scale=scale_in)
        kb.act(dst, dst, AF.Exp, wr, wr, scale=-0.5)

    def tok_prep(t, rows, tk, f32out=False):
        views = []
        for xi in range(3):
            bv_, brg = bfbank()
            for h in range(4):
                kb.tr(bv_[:rows, h, :], qkv[:, 4 * xi + h, tk:tk + rows], ident_b, [r_cm] + r_qkv[4 * xi + h], [brg])
            views.append((bv_, brg))
        for xi in range(2):
            bv_, brg = views[xi]
            kb.act(sqt[:rows], bv_[:rows], AF.Square, [brg], [R_["sqt"]])
            p.op("dve", lambda E, xi=xi: E.tensor_reduce(out=sml[:rows, 3 + xi, :], in_=sqt[:rows], axis=AX.X,
                                                        op=ALU.add), [R_["sqt"]], [r_sml])
        rsq(sml[:rows, 5, :], sml[:rows, 3, :], rows, 128.0, [r_sml], [r_sml])
        kb.ts("dve", sml[:rows, 5, :], sml[:rows, 5, :], 1.0, None, ALU.mult, None, [r_sml], [r_sml])
        rsq(sml[:rows, 6, :], sml[:rows, 4, :], rows, 1.0, [r_sml], [r_sml])
        return views

    GS = 1

    def part1(n):
        tk = n * 128
        q_ = n % 2
        P_ = RP[q_]
        bv, begk, qkT, qgT, kdec, egl = bvP[q_], begkP[q_], qkTP[q_], qgTP[q_], kdecP[q_], eglP[q_]
        egc, edl, rnq, rnk, c2, nb = (sml[:, 0, :], sml[:, 1, :], sml[:, 5, :], sml[:, 6, :], sml[:, 7, :],
                                      sml[:, 8, :])
        bap, brg = kb.bank()
        kb.mm(bap[:, 0:4], triu, gg[:, n, :], True, True, [r_cm, r_gate], [brg])
        kb.mm(bap[:, 4:8], ones_f, gg[:, n, :], True, True, [r_ones, r_gate], [brg])
        kb.cp("dve", gcs, bap[:, 0:8], [brg], [r_gcs])
        kb.act(egc, gcs[:, 0:4], AF.Exp, [r_gcs], [r_sml])
        kb.tt("dve", edl, gcs[:, 4:8], gcs[:, 0:4], ALU.subtract, [r_gcs], [r_sml])
        kb.act(edl, edl, AF.Exp, [r_sml], [r_sml])
        kb.act(egl, gcs[:, 4:8], AF.Exp, [r_gcs], [P_["egl"]])
        kb.tt("dve", c2, beta[:, n, :], egc, ALU.mult, [r_gate, r_sml], [r_sml])
        kb.ts("dve", nb, beta[:, n, :], -1.0, None, ALU.mult, None, [r_gate], [r_sml])
        yield
        (qv, qrg), (kv, krg), (vv, vrg) = tok_prep(n, 128, tk)
        yield
        kb.tt("dve", khtok, kv, bc(rnk), ALU.mult, [krg, r_sml], [R_["khtok"]])
        kb.tt("dve", qhtok, qv, bc(rnq), ALU.mult, [qrg, r_sml], [R_["qhtok"]])
        kb.tt("dve", qgtok, qhtok, bc(egc), ALU.mult, [R_["qhtok"], r_sml], [R_["qgtok"]])
        kb.tt("dve", bv, vv, bc(beta[:, n, :]), ALU.mult, [vrg, r_gate], [P_["bv"]])
        kb.tt("dve", begk, khtok, bc(c2), ALU.mult, [R_["khtok"], r_sml], [P_["begk"]])
        kb.tt("dve", kdec, khtok, bc(edl), ALU.mult, [R_["khtok"], r_sml], [P_["kdec"]])
        yield
        for src, dst, rdst, ev in ((khtok, khT, R_["khT"], "act"), (qhtok, qhT, R_["qhT"], "dve"),
                                   (qgtok, qgT, P_["qgT"], "act")):
            bv_, brg = bfbank()
            for h in range(4):
                kb.tr(bv_[:, h, :], src[:, h, :], ident_b, [R_["khtok"], R_["qhtok"], R_["qgtok"], r_cm], [brg])
            kb.cp(ev, dst, bv_, [brg], [rdst])
        yield
        for h in range(4):
            kb.ts("dve", Gm[:, h, :], triu, gg[:, n, h:h + 1], None, ALU.mult, None, [r_cm, r_gate], [R_["Gm"]])
        dv, drg = fbank()
        dtv, dtrg = fbank()
        for h in range(4):
            kb.mm(dv[:, h, :], Gm[:, h, :], ones_f, True, False, [R_["Gm"], r_ones], [drg])
            kb.mm(dv[:, h, :], negones, Gm[:, h, :], False, True, [R_["Gm"], r_cm], [drg])
            kb.mm(dtv[:, h, :], ones_f, Gm[:, h, :], True, False, [R_["Gm"], r_ones], [dtrg])
            kb.mm(dtv[:, h, :], Gm[:, h, :], negones, False, True, [R_["Gm"], r_cm], [dtrg])
        kb.tt("dve", Dm, dv, hb_(maskL), ALU.add, [drg, r_cm], [R_["Dm"]])
        kb.act(Dm, Dm, AF.Exp, [R_["Dm"]], [R_["Dm"]])
        kb.tt("dve", DTm, dtv, hb_(maskU), ALU.add, [dtrg, r_cm], [R_["DTm"]])
        kb.act(DTm, DTm, AF.Exp, [R_["DTm"]], [R_["DTm"]])
        yield
        kkv, kkrg = fbank()
        qkv_, qkrg = fbank()
        for h in range(4):
            kb.mm(kkv[:, h, :], khT[:, h, :], khT[:, h, :], True, True, [R_["khT"]], [kkrg])
            kb.mm(qkv_[:, h, :], khT[:, h, :], qhT[:, h, :], True, True, [R_["khT"], R_["qhT"]], [qkrg])
        kb.tt("dve", Lf, kkv, Dm, ALU.mult, [kkrg, R_["Dm"]], [R_["sqt"]])
        kb.tt("dve", Lf, Lf, hb_(strict01), ALU.mult, [R_["sqt"], r_cm], [R_["sqt"]])
        kb.tt("dve", Mb[0], Lf, bc(nb), ALU.mult, [R_["sqt"], r_sml], [R_["M0"]])
        kb.tt("dve", qkT, qkv_, DTm, ALU.mult, [qkrg, R_["DTm"]], [P_["qkT"]])
        yield
        bv_, brg = fbank()
        for h in range(4):
            kb.tr(bv_[:, h, :], Mb[0][:, h, :], ident_c, [R_["M0"], r_identc], [brg])
        kb.cp("act", MTb[0], bv_, [brg], [R_["MT0"]])
        kb.tt("dve", TTb[0], MTb[0], hb_(ident_c), ALU.add, [R_["MT0"], r_identc], [R_["TT0"]])
        yield

        def square(src, dst):
            mv_, mrg = fbank()
            mtv, mtrg = fbank()
            for h in range(4):
                kb.mm(mv_[:, h, :], MTb[src][:, h, :], Mb[src][:, h, :], True, True, [R_[f"M{src}"], R_[f"MT{src}"]],
                      [mrg])
                kb.mm(mtv[:, h, :], Mb[src][:, h, :], MTb[src][:, h, :], True, True, [R_[f"M{src}"], R_[f"MT{src}"]],
                      [mtrg])
            kb.cp("act", Mb[dst], mv_, [mrg], [R_[f"M{dst}"]])
            kb.cp("dve", MTb[dst], mtv, [mtrg], [R_[f"MT{dst}"]])
        mcur = 1
        square(0, 1)
        yield
        tcur = 0
        for lvl in range(1, 7):
            mnext = 1 - mcur
            tv, trg = fbank()
            if lvl < 6:
                square(mcur, mnext)
            for h in range(4):
                kb.mm(tv[:, h, :], Mb[mcur][:, h, :], TTb[tcur][:, h, :], True, True,
                      [R_[f"TT{tcur}"], R_[f"M{mcur}"]], [trg])
            if lvl < 6:
                kb.tt("dve", TTb[1 - tcur], TTb[tcur], tv, ALU.add, [R_[f"TT{tcur}"], trg], [R_[f"TT{1 - tcur}"]])
            else:
                kb.tt("dve", TTf[q_], TTb[tcur], tv, ALU.add, [R_[f"TT{tcur}"], trg], [P_["TTf"]])
            tcur = 1 - tcur
            mcur = mnext
            yield

    def part2(n):
        tk = n * 128
        q_ = n % 2
        P_ = RP[q_]
        bv, begk, qkT, qgT, kdec, egl, TT = bvP[q_], begkP[q_], qkTP[q_], qgTP[q_], kdecP[q_], eglP[q_], TTf[q_]
        rTT = P_["TTf"]
        wv, wrg = fbank()
        for h in range(4):
            kb.mm(wv[:, h, :], begk[:, h, :], TT[:, h, :], True, True, [P_["begk"], rTT], [wrg])
        kb.act(nwT, wv, AF.Identity, [wrg], [R_["nwT"]], scale=-1.0)
        yield
        uv, urg = fbank()
        for h in range(4):
            kb.mm(uv[:, h, :], TT[:, h, :], bv[:, h, :], True, False, [rTT, P_["bv"]], [urg])
            kb.mm(uv[:, h, :], nwT[:, h, :], Sst[:, h, :], False, True, [R_["nwT"], r_S], [urg])
        kb.cp("dve", vnew, uv, [urg], [R_["vnew"]])
        yield
        ov, org = fbank()
        for h in range(4):
            kb.mm(ov[:, h, :], qgT[:, h, :], Sb[:, h, :], True, False, [P_["qgT"], r_Sb], [org])
            kb.mm(ov[:, h, :], qkT[:, h, :], vnew[:, h, :], False, True, [P_["qkT"], R_["vnew"]], [org])
        sv, srg = fbank()
        for h in range(4):
            kb.mm(sv[:, h, :], kdec[:, h, :], vnew[:, h, :], True, True, [P_["kdec"], R_["vnew"]], [srg])
        kb.tt("dve", Sst, Sst, bc(egl), ALU.mult, [r_S, P_["egl"]], [r_S])
        kb.tt("dve", Sst, Sst, sv, ALU.add, [r_S, srg], [r_S])
        kb.cp("act", Sb, Sst, [r_S], [r_Sb])
        yield
        kb.act(sqt2, ov, AF.Square, [org], [R_["sqt2"]])
        p.op("dve", lambda E: E.tensor_reduce(out=so2, in_=sqt2, axis=AX.X, op=ALU.add), [R_["sqt2"]], [R_["so2"]])
        rsq(so2, so2, 128, 1.0 / 128, [R_["so2"]], [R_["so2"]])
        kb.tt("dve", onb, ov, bc(so2), ALU.mult, [org, R_["so2"]], [R_["onb"]])
        yield
        bv_, brg = bfbank()
        for h in range(4):
            kb.tr(bv_[:, h, :], onb[:, h, :], ident_b, [R_["onb"], r_cm], [brg])
        kb.stt("dve", mixT[:, 0:4, tk:tk + 128], bv_, nwc[:, 0:1], zT[:, 0:4, tk:tk + 128], ALU.mult, ALU.mult,
               [brg, r_gp] + [r_z[m][n // 4] for m in range(4)], [r_mix[n // 4]])
        yield

    for _ in part1(0):
        pass
    for n in range(16):
        ga = part1(n + 1) if n + 1 < 16 else iter(())
        gb = part2(n)
        da = db = False
        for _ in range(6):
            if not da:
                try:
                    next(ga)
                except StopIteration:
                    da = True
        while not (da and db):
            if not db:
                try:
                    next(gb)
                except StopIteration:
                    db = True
            for _ in range(2):
                if not da:
                    try:
                        next(ga)
                    except StopIteration:
                        da = True
    kb.dma("sp", gdnp_d.rearrange("h d e -> d h e"), Sst, [r_S], [Rg()])

    p.barrier()
    A.top = alias_base
    Sall = A.f32(16 * 512).rearrange("p (b h e) -> p b h e", b=16, h=4)
    r_Sall = rgs(16)
    for b in range(NS):
        kb.dma("sp", Sall[:, b], sg_d[b].rearrange("h d e -> d h e"), [], [r_Sall[b]])
    f4s = lambda: A.f32(512).rearrange("p (h e) -> p h e", h=4)
    khs, qhs, vts, kw, qo, vacc, oacc, ksel = f4s(), f4s(), f4s(), f4s(), f4s(), f4s(), f4s(), f4s()
    kwT = A.f32(64).rearrange("p (h b) -> p h b", h=4)
    qoT = A.f32(64).rearrange("p (h b) -> p h b", h=4)
    sms = A.f32(24).rearrange("p (a h) -> p a h", h=4)
    egsel = A.f32(64).rearrange("p (b h) -> p b h", h=4)
    egbc = A.f32(64).rearrange("p (b h) -> p b h", h=4)
    r_s = {k: Rg(k) for k in ["khs", "qhs", "vts", "kw", "qo", "vacc", "oacc", "ksel", "kwT", "qoT", "sms", "egsel",
                              "egbc"]}
    rows, tk = NS, TP
    if GS:
        (qv, qrg), (kv, krg), (vv, vrg) = tok_prep(16, NS, TP)
        rnq, rnk = sml[:NS, 5, :], sml[:NS, 6, :]
        kb.tt("dve", khs[:NS], kv[:NS], bc(rnk, rows=NS), ALU.mult, [krg, r_sml], [r_s["khs"]])
        kb.tt("dve", qhs[:NS], qv[:NS], bc(rnq, rows=NS), ALU.mult, [qrg, r_sml], [r_s["qhs"]])
        kb.cp("dve", vts[:NS], vv[:NS], [vrg], [r_s["vts"]])
        eg, qkd, ckw, cqo, qkb, sos = [sms[:NS, i, :] for i in range(6)]
        bet16, g16 = beta[:NS, 16, :], gg[:NS, 16, :]
        kb.act(eg, g16, AF.Exp, [r_gate], [r_s["sms"]])
        kb.tt("dve", ksel[:NS], qhs[:NS], khs[:NS], ALU.mult, [r_s["qhs"], r_s["khs"]], [r_s["ksel"]])
        p.op("dve", lambda E: E.tensor_reduce(out=qkd, in_=ksel[:NS], axis=AX.X, op=ALU.add), [r_s["ksel"]],
             [r_s["sms"]])
        kb.tt("dve", ckw, bet16, eg, ALU.mult, [r_gate, r_s["sms"]], [r_s["sms"]])
        kb.tt("dve", qkb, qkd, bet16, ALU.mult, [r_gate, r_s["sms"]], [r_s["sms"]])
        kb.tt("dve", cqo, ckw, qkd, ALU.mult, [r_s["sms"]], [r_s["sms"]])
        kb.ts("dve", ckw, ckw, -1.0, None, ALU.mult, None, [r_s["sms"]], [r_s["sms"]])
        kb.tt("dve", kw[:NS], khs[:NS], bc(ckw, rows=NS), ALU.mult, [r_s["khs"], r_s["sms"]], [r_s["kw"]])
        kb.tt("dve", qo[:NS], qhs[:NS], bc(eg, rows=NS), ALU.mult, [r_s["qhs"], r_s["sms"]], [r_s["qo"]])
        kb.tt("dve", ksel[:NS], khs[:NS], bc(cqo, rows=NS), ALU.mult, [r_s["khs"], r_s["sms"]], [r_s["ksel"]])
        kb.tt("dve", qo[:NS], qo[:NS], ksel[:NS], ALU.subtract, [r_s["qo"], r_s["ksel"]], [r_s["qo"]])
        kb.tt("dve", vacc[:NS], vts[:NS], bc(bet16, rows=NS), ALU.mult, [r_s["vts"], r_gate], [r_s["vacc"]])
        kb.tt("dve", oacc[:NS], vts[:NS], bc(qkb, rows=NS), ALU.mult, [r_s["vts"], r_s["sms"]], [r_s["oacc"]])
        bap, brg = kb.bank()
        for h in range(4):
            kb.tr(bap[:, h * 16:(h + 1) * 16], kw[:NS, h, :], ident_f[:NS, :NS], [r_s["kw"], r_ident], [brg])
            kb.tr(bap[:, 64 + h * 16:64 + (h + 1) * 16], qo[:NS, h, :], ident_f[:NS, :NS], [r_s["qo"], r_ident], [brg])
        kwTp, qoTp = Gm, Dm
        kb.memset("dve", kwTp, 0.0, [r_s["kwT"], R_["Gm"]])
        kb.memset("dve", qoTp, 0.0, [r_s["qoT"], R_["Dm"]])
        kb.cp("dve", kwTp[:, :, 0:NS], bap[:, 0:64].rearrange("p (h b) -> p h b", h=4), [brg], [r_s["kwT"]])
        kb.cp("dve", qoTp[:, :, 0:NS], bap[:, 64:128].rearrange("p (h b) -> p h b", h=4), [brg], [r_s["qoT"]])
        kb.tt("dve", egsel[:NS], ident_f[:NS, :NS].unsqueeze(2).to_broadcast([NS, NS, 4]),
              eg.unsqueeze(1).to_broadcast([NS, NS, 4]), ALU.mult, [r_ident, r_s["sms"]], [r_s["egsel"]])
        bap, brg = kb.bank()
        kb.mm(bap[:, 0:64], ones_f[:NS, :], egsel[:NS].rearrange("p b h -> p (b h)"), True, True,
              [r_ones, r_s["egsel"]], [brg])
        kb.cp("dve", egbc, bap[:, 0:64].rearrange("p (b h) -> p b h", h=4), [brg], [r_s["egbc"]])
        for b in range(NS if GS == 1 else 0):
            vv_, vrg_ = fbank()
            ov_, org_ = fbank()
            for h in range(4):
                kb.mm(vv_[:, h, :], kwTp[:, h, :], Sall[:, b, h, :], True, True, [r_s["kwT"], r_Sall[b]], [vrg_])
                kb.mm(ov_[:, h, :], qoTp[:, h, :], Sall[:, b, h, :], True, True, [r_s["qoT"], r_Sall[b]], [org_])
            kb.stt("dve", vacc[:NS], vv_[:NS], ident_f[:NS, b:b + 1], vacc[:NS], ALU.mult, ALU.add,
                   [vrg_, r_ident, r_s["vacc"]], [r_s["vacc"]])
            kb.stt("dve", oacc[:NS], ov_[:NS], ident_f[:NS, b:b + 1], oacc[:NS], ALU.mult, ALU.add,
                   [org_, r_ident, r_s["oacc"]], [r_s["oacc"]])
        ksel2 = [ksel, kw]
        r_ks2 = [r_s["ksel"], r_s["kw"]]
        for b in range(NS if GS in (1, 3) else 0):
            ks_, rks_ = ksel2[b % 2], r_ks2[b % 2]
            kb.ts("dve", ks_[:NS], khs[:NS], ident_f[:NS, b:b + 1], None, ALU.mult, None, [r_s["khs"], r_ident],
                  [rks_])
            sv, srg = fbank()
            for h in range(4):
                kb.mm(sv[:, h, :], ks_[:NS, h, :], vacc[:NS, h, :], True, True, [rks_, r_s["vacc"]], [srg])
            for h in range(4):
                kb.stt("dve", Sall[:, b, h, :], Sall[:, b, h, :], egbc[:, b, h:h + 1], sv[:, h, :], ALU.mult, ALU.add,
                       [r_Sall[b], r_s["egbc"], srg], [r_Sall[b]])
            kb.dma("sp", gdns_d[b].rearrange("h d e -> d h e"), Sall[:, b], [r_Sall[b]], [Rg()])
        kb.act(sqt[:NS], oacc[:NS], AF.Square, [r_s["oacc"]], [R_["sqt"]])
        p.op("dve", lambda E: E.tensor_reduce(out=sos, in_=sqt[:NS], axis=AX.X, op=ALU.add), [R_["sqt"]], [r_s["sms"]])
        rsq(sos, sos, NS, 1.0 / 128, [r_s["sms"]], [r_s["sms"]])
        kb.tt("dve", oacc[:NS], oacc[:NS], bc(sos, rows=NS), ALU.mult, [r_s["oacc"], r_s["sms"]], [r_s["oacc"]])
        bap, brg = kb.bank()
        for h in range(4):
            kb.tr(bap[:, h * 16:(h + 1) * 16], oacc[:NS, h, :], ident_f[:NS, :NS], [r_s["oacc"], r_ident], [brg])
        kb.stt("dve", mixT[:, 0:4, TP:TP + NS], bap[:, 0:64].rearrange("p (h b) -> p h b", h=4), nwc[:, 0:1],
               zT[:, 0:4, TP:TP + NS], ALU.mult, ALU.mult, [brg, r_gp] + [r_z[m][4] for m in range(4)], [r_mix[4]])


    p.barrier()
    A.top = base_c
    A2 = [base_persist]

    def a2f32(cols):
        a = A2[0]
        A2[0] += cols
        assert A2[0] <= base_persist + 12 * NT // 2 + 4 * NT // 2, 'region1 overflow'
        return A.t[:, a:a + cols]
    s5a_d = kb.din("s5a", [128, 3, 16])
    s5bx_d = kb.din("s5bx", [128, 2, 16, 32])
    s5cx_d = kb.din("s5cx", [128, 2, 16, 32])
    s5d_d = kb.din("s5d", [128, 2, 4])
    w_glu = kb.din("w_glu", [512, 512])
    hs_d = kb.din("state_ssm", [NS, 2, 2048])
    ssmp_d = kb.dout("ssm_p", [2, 16, 128])
    ssms_d = kb.dout("ssm_s", [NS, 2, 2048])
    kb.bank_list = [4, 5, 6, 7]
    wglu_sb = A.bf16(4 * 512).rearrange("p (k n) -> p k n", k=4)
    bt_off = A.top
    BT = A.bf16(2 * 16 * 128).rearrange("p (r g n) -> p r g n", r=2, g=16)
    CXf = A.bf16(2 * 16 * 128).rearrange("p (r g n) -> p r g n", r=2, g=16)
    diag_d = A.bf16(4 * 128).rearrange("p (j n) -> p j n", j=4)
    pa = A.f32(48).rearrange("p (a g) -> p a g", a=3)
    s5d = A.f32(8).rearrange("p (a j) -> p a j", a=2)
    hbg = A.f32(4)
    sm = A.f32(16 * 24).rearrange("p (a g) -> p a g", g=16)
    wk = A.f32(11 * 2 * 16).rearrange("p (k c g) -> p k c g", k=11, c=2)
    hl = A.f32(32).rearrange("p (r g) -> p r g", r=2)
    carry = A.f32(2)
    cst = A.f32(2)
    r_wglu, r_BT, r_CX, r_dd, r_pa, r_sm, r_wk, r_hl, r_carry, r_cst = (Rg(), Rg(), Rg(), Rg(), Rg(), Rg(), Rg(),
                                                                      Rg(), Rg(), Rg())
    kb.dma("pool", wglu_sb, w_glu.rearrange("(k p) n -> p k n", p=128), [], [r_wglu])
    kb.dma("sp", pa, s5a_d, [], [r_pa])
    kb.dma("sp", s5d, s5d_d, [], [r_pa])
    kb.memset("pool", cst[:, 0:1], math.pi / 2, [r_cst])
    kb.memset("pool", cst[:, 1:2], 1.0, [r_cst])
    chalf = A.f32(1)
    kb.memset("pool", chalf, 0.5, [r_cst])
    cx_in = A.f32(1024).rearrange("p (r g c) -> p r g c", r=2, g=16)
    bb = A.f32(1024).rearrange("p (r g c) -> p r g c", r=2, g=16)
    lp = A.f32(9 * 2 * 16).rearrange("p (k c g) -> p k c g", k=9, c=2)
    wk8 = A.f32(8 * 2 * 16).rearrange("p (k c g) -> p k c g", k=8, c=2)
    r8 = A.f32(16)
    r_lp = Rg()
    base_loop = A.top
    bx_in = A.f32(1024).rearrange("p (r g c) -> p r g c", r=2, g=16)
    pre = A.f32(4096).rearrange("p (r g n) -> p r g n", r=2, g=16)
    tq = A.f32(512).rearrange("p (g c) -> p g c", g=16)
    r_bx, r_cxin, r_bb, r_pre, r_tq = Rg(), Rg(), Rg(), Rg(), Rg()
    kb.dma("sp", bx_in, s5bx_d, [], [r_bx])
    kb.dma("sp", cx_in, s5cx_d, [], [r_cxin])
    are, aim, ldt = pa[:, 0, :], pa[:, 1, :], pa[:, 2, :]
    S = lambda i: sm[:, i, :]
    rs = [r_sm, r_pa, r_cst]
    dtt, rr_, th, c_, s_, t1_, t2_, lbr, lbi, den, cr, ci = [S(i) for i in range(12)]
    x_, y_ = S(12), S(13)
    kb.act(dtt, ldt, AF.Exp, rs, [r_sm])
    kb.tt("dve", rr_, are, dtt, ALU.mult, rs, [r_sm])
    kb.act(rr_, rr_, AF.Exp, rs, [r_sm])
    kb.tt("dve", th, aim, dtt, ALU.mult, rs, [r_sm])
    kb.act(s_, th, AF.Sin, rs, [r_sm], scale=1.0 / 16)
    kb.act(c_, th, AF.Sin, rs, [r_sm], scale=1.0 / 16, bias=cst[:, 0:1])

    def csq(c, s):
        kb.tt("dve", t1_, c, c, ALU.mult, rs, [r_sm])
        kb.tt("dve", t2_, s, s, ALU.mult, rs, [r_sm])
        kb.stt("dve", s, c, 2.0, s, ALU.mult, ALU.mult, rs, [r_sm])
        kb.tt("dve", c, t1_, t2_, ALU.subtract, rs, [r_sm])
    for _ in range(4):
        csq(c_, s_)
    kb.tt("dve", lbr, rr_, c_, ALU.mult, rs, [r_sm])
    kb.tt("dve", lbi, rr_, s_, ALU.mult, rs, [r_sm])
    kb.ts("dve", x_, lbr, -1.0, None, ALU.add, None, rs, [r_sm])
    kb.tt("dve", t1_, are, are, ALU.mult, rs, [r_sm])
    kb.tt("dve", t2_, aim, aim, ALU.mult, rs, [r_sm])
    kb.tt("dve", den, t1_, t2_, ALU.add, rs, [r_sm])
    p.op("dve", lambda E: E.reciprocal(out=den, in_=den), rs, [r_sm])
    kb.tt("dve", t1_, x_, are, ALU.mult, rs, [r_sm])
    kb.tt("dve", t2_, lbi, aim, ALU.mult, rs, [r_sm])
    kb.tt("dve", cr, t1_, t2_, ALU.add, rs, [r_sm])
    kb.tt("dve", cr, cr, den, ALU.mult, rs, [r_sm])
    kb.tt("dve", t1_, lbi, are, ALU.mult, rs, [r_sm])
    kb.tt("dve", t2_, x_, aim, ALU.mult, rs, [r_sm])
    kb.tt("dve", ci, t1_, t2_, ALU.subtract, rs, [r_sm])
    kb.tt("dve", ci, ci, den, ALU.mult, rs, [r_sm])
    kb.cp("dve", wk[:, 0, 0, :], c_, rs, [r_wk])
    kb.ts("dve", wk[:, 0, 1, :], s_, -1.0, None, ALU.mult, None, rs, [r_wk])
    for k in range(1, 11):
        kb.cp("dve", wk[:, k, 0, :], wk[:, k - 1, 0, :], [r_wk], [r_wk])
        kb.cp("dve", wk[:, k, 1, :], wk[:, k - 1, 1, :], [r_wk], [r_wk])
        c, s = wk[:, k, 0, :], wk[:, k, 1, :]
        kb.tt("dve", t1_, c, c, ALU.mult, [r_wk, r_sm], [r_sm])
        kb.tt("dve", t2_, s, s, ALU.mult, [r_wk, r_sm], [r_sm])
        kb.stt("dve", s, c, 2.0, s, ALU.mult, ALU.mult, [r_wk], [r_wk])
        kb.tt("dve", c, t1_, t2_, ALU.subtract, [r_sm, r_wk], [r_wk])
    kb.memset("pool", lp[:, 0, 0, :], 1.0, [r_lp])
    kb.memset("pool", lp[:, 0, 1, :], 0.0, [r_lp])
    kb.cp("dve", lp[:, 1, 0, :], lbr, rs, [r_lp])
    kb.cp("dve", lp[:, 1, 1, :], lbi, rs, [r_lp])
    for k in range(2, 9):
        pr, pi_ = lp[:, k - 1, 0, :], lp[:, k - 1, 1, :]
        kb.tt("dve", t1_, pr, lbr, ALU.mult, [r_lp, r_sm], [r_sm])
        kb.tt("dve", t2_, pi_, lbi, ALU.mult, [r_lp, r_sm], [r_sm])
        kb.tt("dve", lp[:, k, 0, :], t1_, t2_, ALU.subtract, [r_sm], [r_lp])
        kb.tt("dve", t1_, pr, lbi, ALU.mult, [r_lp, r_sm], [r_sm])
        kb.tt("dve", t2_, pi_, lbr, ALU.mult, [r_lp, r_sm], [r_sm])
        kb.tt("dve", lp[:, k, 1, :], t1_, t2_, ALU.add, [r_sm], [r_lp])
    kb.tt("dve", r8, rr_, rr_, ALU.mult, rs, [r_lp])
    kb.tt("dve", r8, r8, r8, ALU.mult, [r_lp], [r_lp])
    kb.tt("dve", r8, r8, r8, ALU.mult, [r_lp], [r_lp])
    p.op("dve", lambda E: E.reciprocal(out=t1_, in_=r8), [r_lp, r_sm], [r_sm])
    kb.tt("dve", wk8[:, 0, 0, :], lp[:, 8, 0, :], t1_, ALU.mult, [r_lp, r_sm], [r_lp])
    kb.stt("dve", wk8[:, 0, 1, :], lp[:, 8, 1, :], -1.0, t1_, ALU.mult, ALU.mult, [r_lp, r_sm], [r_lp])
    for k in range(1, 8):
        kb.cp("dve", wk8[:, k, 0, :], wk8[:, k - 1, 0, :], [r_lp], [r_lp])
        kb.cp("dve", wk8[:, k, 1, :], wk8[:, k - 1, 1, :], [r_lp], [r_lp])
        c, s_2 = wk8[:, k, 0, :], wk8[:, k, 1, :]
        kb.tt("dve", t1_, c, c, ALU.mult, [r_lp, r_sm], [r_sm])
        kb.tt("dve", t2_, s_2, s_2, ALU.mult, [r_lp, r_sm], [r_sm])
        kb.stt("dve", s_2, c, 2.0, s_2, ALU.mult, ALU.mult, [r_lp], [r_lp])
        kb.tt("dve", c, t1_, t2_, ALU.subtract, [r_sm, r_lp], [r_lp])
    crb = cr.unsqueeze(2).to_broadcast([128, 16, 32])
    cib = ci.unsqueeze(2).to_broadcast([128, 16, 32])
    kb.tt("dve", bb[:, 0], bx_in[:, 0], crb, ALU.mult, [r_bx, r_sm], [r_bb])
    kb.tt("dve", tq, bx_in[:, 1], cib, ALU.mult, [r_bx, r_sm], [r_tq])
    kb.tt("dve", bb[:, 0], bb[:, 0], tq, ALU.subtract, [r_bb, r_tq], [r_bb])
    kb.tt("dve", bb[:, 1], bx_in[:, 1], crb, ALU.mult, [r_bx, r_sm], [r_bb])
    kb.tt("dve", tq, bx_in[:, 0], cib, ALU.mult, [r_bx, r_sm, r_bb], [r_tq])
    kb.tt("dve", bb[:, 1], bb[:, 1], tq, ALU.add, [r_bb, r_tq], [r_bb])
    kb.memset("dve", pre, 0.0, [r_pre])
    kb.memset("dve", CXf, 0.0, [r_CX])
    for ri in range(2):
        pre5 = pre[:, ri].rearrange("p (j l) n -> p j l n", l=4)
        bb5 = bb[:, ri].rearrange("p (j l) c -> p j l c", l=4)
        cxf5 = CXf[:, ri].rearrange("p (j l) n -> p j l n", l=4)
        cx5 = cx_in[:, ri].rearrange("p (j l) c -> p j l c", l=4)
        for l in range(4):
            kb.cp("dve", pre5[:, :, l, 32 * l:32 * l + 32], bb5[:, :, l, :], [r_bb], [r_pre])
            kb.ts("dve", cxf5[:, :, l, 32 * l:32 * l + 32], cx5[:, :, l, :], (1.0 if ri == 0 else -1.0), None,
                  ALU.mult, None, [r_cxin], [r_CX])
    for ri in range(2):
        for g4 in range(4):
            bap, brg = kb.bank()
            for l in range(4):
                kb.tr(bap[:, l * 128:(l + 1) * 128], pre[:, ri, 4 * g4 + l, :], ident_f, [r_pre, r_ident], [brg])
            kb.cp("act" if g4 % 2 else "dve", BT[:, ri, 4 * g4:4 * g4 + 4, :],
                  bap.rearrange("p (l n) -> p l n", l=4), [brg], [r_BT])
    for j in range(4):
        kb.ts("dve", diag_d[:, j, :], ident_f, s5d[:, 0, j:j + 1], None, ALU.mult, None, [r_ident, r_pa], [r_dd])
    kb.ts("dve", hbg, s5d[:, 1, :], 0.5, None, ALU.mult, None, [r_pa], [r_pa])

    GC0, GC1, GC2 = 0.044715, 0.7978845608028654, 0.5

    def gelu_chain(bap, brg, N, dst, rdst, xg, tg, rxg, rtg):
        kb.cp("act", xg[:, :N], bap[:, :N], [brg], [rxg])
        kb.tt("dve", tg[:, :N], xg[:, :N], xg[:, :N], ALU.mult, [rxg], [rtg])
        kb.ts("dve", tg[:, :N], tg[:, :N], GC0, 1.0, ALU.mult, ALU.add, [rtg], [rtg])
        kb.tt("dve", tg[:, :N], tg[:, :N], xg[:, :N], ALU.mult, [rtg, rxg], [rtg])
        kb.act(tg[:, :N], tg[:, :N], AF.Tanh, [rtg], [rtg], scale=GC1)
        kb.ts("dve", tg[:, :N], tg[:, :N], 0.5, 0.5, ALU.mult, ALU.add, [rtg], [rtg])
        kb.tt("dve", dst, tg[:, :N], xg[:, :N], ALU.mult, [rtg, rxg], rdst)

    p.barrier()
    A.top = base_loop
    hs_tok = A.f32(2048)
    h0T = A.f32(512).rearrange("p (r g b) -> p r g b", r=2, g=16)
    lh = A.f32(512).rearrange("p (r g b) -> p r g b", r=2, g=16)
    hn = A.f32(512).rearrange("p (r g b) -> p r g b", r=2, g=16)
    hnb = A.bf16(512).rearrange("p (r g b) -> p r g b", r=2, g=16)
    tq2 = A.f32(256).rearrange("p (g b) -> p g b", g=16)
    so_tok = A.f32(2048)
    xg, tg = A.f32(512), A.f32(512)
    hlT = A.f32(256).rearrange("p (r n) -> p r n", r=2)
    r_hs, r_h0T, r_lh, r_hn, r_hnb, r_tq2, r_so, r_xg, r_tg, r_hlT = (Rg(), Rg(), Rg(), Rg(), Rg(), Rg(), Rg(), Rg(),
                                                                      Rg(), Rg())
    for ri in range(2):
        kb.dma("sp", hs_tok[:NS, :], hs_d[:, ri, :], [], [r_hs])
        bap, brg = kb.bank()
        for rg in range(16):
            kb.tr(bap[:, rg * 16:(rg + 1) * 16], hs_tok[:NS, rg * 128:(rg + 1) * 128], ident_f[:NS, :NS],
                  [r_hs, r_ident], [brg])
        kb.cp("dve", h0T[:, ri], bap[:, 0:256].rearrange("p (g b) -> p g b", g=16), [brg], [r_h0T])
    lbrb = lbr.unsqueeze(2).to_broadcast([128, 16, 16])
    lbib = lbi.unsqueeze(2).to_broadcast([128, 16, 16])
    kb.tt("dve", lh[:, 0], h0T[:, 0], lbrb, ALU.mult, [r_h0T, r_sm], [r_lh])
    kb.tt("dve", tq2, h0T[:, 1], lbib, ALU.mult, [r_h0T, r_sm], [r_tq2])
    kb.tt("dve", lh[:, 0], lh[:, 0], tq2, ALU.subtract, [r_lh, r_tq2], [r_lh])
    kb.tt("dve", lh[:, 1], h0T[:, 1], lbrb, ALU.mult, [r_h0T, r_sm], [r_lh])
    kb.tt("dve", tq2, h0T[:, 0], lbib, ALU.mult, [r_h0T, r_sm, r_lh], [r_tq2])
    kb.tt("dve", lh[:, 1], lh[:, 1], tq2, ALU.add, [r_lh, r_tq2], [r_lh])
    for ri in range(2):
        bap, brg = kb.bank()
        for rg in range(16):
            kb.mm(bap[:, rg * 16:(rg + 1) * 16], BT[:, ri, rg, :], usamp[:, rg // 4, :], True, True,
                  [r_BT, r_u[rg // 4][4]], [brg])
        kb.tt("dve", hn[:, ri], lh[:, ri], bap[:, 0:256].rearrange("p (g b) -> p g b", g=16), ALU.add,
              [brg, r_lh], [r_hn])
    kb.cp("dve", hnb, hn, [r_hn], [r_hnb])
    for j in range(4):
        bap, brg = kb.bank()
        for l in range(4):
            rg = 4 * j + l
            kb.mm(bap[:, :NS], CXf[:, 0, rg, :], hnb[:, 0, rg, :], l == 0, False, [r_CX, r_hnb], [brg])
            kb.mm(bap[:, :NS], CXf[:, 1, rg, :], hnb[:, 1, rg, :], False, False, [r_CX, r_hnb], [brg])
        kb.mm(bap[:, :NS], diag_d[:, j, :], usamp[:, j, :], False, True, [r_dd, r_u[j][4]], [brg])
        gelu_chain(bap, brg, NS, usamp[:, j, :], [r_u[j][4]], xg, tg, r_xg, r_tg)
    for ri in range(2):
        for g4 in range(4):
            bap, brg = kb.bank()
            for l in range(4):
                kb.tr(bap[:NS, l * 128:(l + 1) * 128], hn[:, ri, 4 * g4 + l, :], ident_f, [r_hn, r_ident], [brg])
            kb.cp("act", so_tok[:NS, g4 * 512:(g4 + 1) * 512], bap[:NS, :], [brg], [r_so])
        kb.dma("sp", ssms_d[:, ri, :], so_tok[:NS, :], [r_so], [Rg()])

    p.barrier()
    A.top = base_loop
    PBT = A.t[:, bt_off:bt_off + 4096].bitcast(BF16).rearrange("p (l s r n) -> p l s r n", l=4, s=8, r=2)
    QCs = [A.bf16(4 * 9 * 2 * 128).rearrange("p (l t r n) -> p l t r n", l=4, t=9, r=2),
           a2f32(4608).bitcast(BF16).rearrange("p (l t r n) -> p l t r n", l=4, t=9, r=2)]
    PBb = a2f32(512).bitcast(BF16).rearrange("p (r l n) -> p r l n", r=2, l=4)
    r_PBb = Rg()
    Kbds = [A.bf16(8 * 128).rearrange("p (t n) -> p t n", t=8),
            a2f32(512).bitcast(BF16).rearrange("p (t n) -> p t n", t=8)]
    Qall = a2f32(9 * 2 * 4 * 32).rearrange("p (k r l n) -> p k r l n", k=9, r=2, l=4)
    Pall = a2f32(4 * 2 * 4 * 128).rearrange("p (k r l n) -> p k r l n", k=4, r=2, l=4)
    prod = a2f32(4 * 5 * 128).rearrange("p (a k l c) -> p a k l c", a=4, k=5, l=4)
    Zb = A.f32(2048).rearrange("p (r l c) -> p r l c", r=2, l=4)
    Eb = A.f32(2048).rearrange("p (r l c) -> p r l c", r=2, l=4)
    Wb = A.f32(2048).rearrange("p (r l c) -> p r l c", r=2, l=4)
    tmpb = A.f32(1024).rearrange("p (l c) -> p l c", l=4)
    tmp2 = A.f32(512).rearrange("p (l c) -> p l c", l=4)
    Hb = A.bf16(2 * 4 * 257).rearrange("p (r l c) -> p r l c", r=2, l=4)
    xg, tg = tmpb.rearrange("p l c -> p (l c)")[:, 0:512], tmpb.rearrange("p l c -> p (l c)")[:, 512:1024]
    r_PBT, r_Qall, r_Pall, r_prod = Rg(), Rg(), Rg(), Rg()
    r_QCs, r_Ks = rgs(2), rgs(2)
    r_Z, r_E, r_W, r_tmp, r_tmp2, r_Hb = Rg(), Rg(), Rg(), Rg(), Rg(), Rg()
    r_xg = r_tg = r_tmp
    for q_ in range(2):
        kb.memset("dve", QCs[q_], 0.0, [r_QCs[q_]])
    kb.memset("dve", Pall, 0.0, [r_Pall])
    kb.memset("dve", Hb, 0.0, [r_Hb])
    evk = [0]

    def cmul_batch(j, k0, nk, src, rsrc, dst, rdst, neg_im, pad):
        lr = lp[:, k0:k0 + nk, 0, 4 * j:4 * j + 4].unsqueeze(3).to_broadcast([128, nk, 4, 32])
        li = lp[:, k0:k0 + nk, 1, 4 * j:4 * j + 4].unsqueeze(3).to_broadcast([128, nk, 4, 32])
        sr = src[:, 0, 4 * j:4 * j + 4, :].unsqueeze(1).to_broadcast([128, nk, 4, 32])
        si = src[:, 1, 4 * j:4 * j + 4, :].unsqueeze(1).to_broadcast([128, nk, 4, 32])
        P0, P1, P2, P3 = [prod[:, a, 0:nk] for a in range(4)]
        kb.tt("dve", P0, sr, lr, ALU.mult, [rsrc, r_lp], [r_prod])
        kb.tt("dve", P1, si, li, ALU.mult, [rsrc, r_lp], [r_prod])
        kb.tt("dve", P2, si, lr, ALU.mult, [rsrc, r_lp], [r_prod])
        kb.tt("dve", P3, sr, li, ALU.mult, [rsrc, r_lp], [r_prod])
        if not pad:
            kb.tt("dve", dst[:, 0:nk, 0], P0, P1, ALU.subtract, [r_prod], [rdst])
            if neg_im:
                kb.stt("dve", dst[:, 0:nk, 1], P2, -1.0, P3, ALU.mult, ALU.subtract, [r_prod], [rdst])
            else:
                kb.tt("dve", dst[:, 0:nk, 1], P2, P3, ALU.add, [r_prod], [rdst])
            return
        for l in range(4):
            kb.tt("dve", dst[:, 0:nk, 0, l, 32 * l:32 * l + 32], P0[:, :, l, :], P1[:, :, l, :], ALU.subtract,
                  [r_prod], [rdst])
            if neg_im:
                kb.stt("dve", dst[:, 0:nk, 1, l, 32 * l:32 * l + 32], P2[:, :, l, :], -1.0, P3[:, :, l, :],
                       ALU.mult, ALU.subtract, [r_prod], [rdst])
            else:
                kb.tt("dve", dst[:, 0:nk, 1, l, 32 * l:32 * l + 32], P2[:, :, l, :], P3[:, :, l, :], ALU.add,
                      [r_prod], [rdst])

    def s5_setup(j):
        q_ = j % 2
        QC, Kbd, r_QC, r_K = QCs[q_], Kbds[q_], r_QCs[q_], r_Ks[q_]
        cmul_batch(j, 0, 5, cx_in, r_cxin, Qall, r_Qall, True, False)
        yield
        cmul_batch(j, 5, 4, cx_in, r_cxin, Qall[:, 5:9], r_Qall, True, False)
        yield
        for l in range(4):
            for ri in range(2):
                kb.cp("act", QC[:, l, 0:8, ri, 32 * l:32 * l + 32], Qall[:, 1:9, ri, l, :], [r_Qall], [r_QC])
                kb.cp("act", QC[:, l, 8, ri, 32 * l:32 * l + 32], Qall[:, 0, ri, l, :], [r_Qall], [r_QC])
        yield
        for half in (1, 0):
            cmul_batch(j, 4 * half, 4, bb, r_bb, Pall, r_Pall, False, True)
            yield
            for kk in range(4):
                sidx = 7 - (4 * half + kk)
                for ri in range(2):
                    bap, brg = kb.bank()
                    for l in range(4):
                        kb.tr(bap[:, l * 128:(l + 1) * 128], Pall[:, kk, ri, l, :], ident_f, [r_Pall, r_ident],
                              [brg])
                    evk[0] += 1
                    kb.cp("act", PBT[:, :, sidx, ri, :],
                          bap.rearrange("p (l n) -> p l n", l=4), [brg], [r_PBT])
                if kk % 2 == 1:
                    yield
        kb.cp("dve", PBb, Pall[:, 0], [r_Pall], [r_PBb])
        for k4 in range(2):
            bap, brg = kb.bank()
            for kk in range(4):
                k = 4 * k4 + kk
                n8 = 0
                for l in range(4):
                    for ri in range(2):
                        kb.mm(bap[:, kk * 128:(kk + 1) * 128], PBb[:, ri, l, :], QC[:, l, (k - 1) % 9, ri, :],
                              n8 == 0, n8 == 7, [r_PBb, r_QC], [brg])
                        n8 += 1
            kb.cp("act", Kbd[:, 4 * k4:4 * k4 + 4, :], bap.rearrange("p (k n) -> p k n", k=4), [brg], [r_K])
        kb.tt("dve", Kbd[:, 0, :], Kbd[:, 0, :], diag_d[:, j, :], ALU.add, [r_K, r_dd], [r_K])
        yield

    def s5_main(j):
        q_ = j % 2
        QC, Kbd, r_QC, r_K = QCs[q_], Kbds[q_], r_QCs[q_], r_Ks[q_]
        uj = uperm[:, j]
        ru = [r_u[j][tb] for tb in range(4)]
        for l in range(4):
            for ri in range(2):
                bap, brg = kb.bank()
                for sidx in range(8):
                    kb.mm(bap[:, 0:256], PBT[:, l, sidx, ri, :], uj[:, sidx, :], sidx == 0, sidx == 7, [r_PBT] + ru,
                          [brg])
                evk[0] += 1
                kb.cp("act", Zb[:, ri, l, :], bap[:, 0:256], [brg], [r_Z])
        yield
        Er, Ei = Eb[:, 0], Eb[:, 1]
        kb.cp("dve", Er[:, :, 0:1], wk8[:, 0, 0, 4 * j:4 * j + 4].unsqueeze(2), [r_lp], [r_E])
        kb.cp("dve", Ei[:, :, 0:1], wk8[:, 0, 1, 4 * j:4 * j + 4].unsqueeze(2), [r_lp], [r_E])
        for k in range(8):
            n = 1 << k
            ck = wk8[:, k, 0, 4 * j:4 * j + 4].unsqueeze(2).to_broadcast([128, 4, n])
            sk = wk8[:, k, 1, 4 * j:4 * j + 4].unsqueeze(2).to_broadcast([128, 4, n])
            kb.tt("dve", tmpb[:, :, 0:n], Ei[:, :, 0:n], sk, ALU.mult, [r_E, r_lp], [r_tmp])
            kb.tt("dve", tmp2[:, :, 0:n], Er[:, :, 0:n], ck, ALU.mult, [r_E, r_lp], [r_tmp2])
            kb.tt("dve", Er[:, :, n:2 * n], tmp2[:, :, 0:n], tmpb[:, :, 0:n], ALU.subtract, [r_tmp, r_tmp2], [r_E])
            kb.tt("dve", tmpb[:, :, 0:n], Ei[:, :, 0:n], ck, ALU.mult, [r_E, r_lp], [r_tmp])
            kb.tt("dve", tmp2[:, :, 0:n], Er[:, :, 0:n], sk, ALU.mult, [r_E, r_lp], [r_tmp2])
            kb.tt("dve", Ei[:, :, n:2 * n], tmp2[:, :, 0:n], tmpb[:, :, 0:n], ALU.add, [r_tmp, r_tmp2], [r_E])
            if k % 2 == 1:
                yield
        kb.tt("dve", Wb[:, 0], Er, Zb[:, 0], ALU.mult, [r_E, r_Z], [r_W])
        kb.tt("dve", tmpb, Ei, Zb[:, 1], ALU.mult, [r_E, r_Z], [r_tmp])
        kb.tt("dve", Wb[:, 0], Wb[:, 0], tmpb, ALU.subtract, [r_W, r_tmp], [r_W])
        kb.tt("dve", Wb[:, 1], Er, Zb[:, 1], ALU.mult, [r_E, r_Z], [r_W])
        kb.tt("dve", tmpb, Ei, Zb[:, 0], ALU.mult, [r_E, r_Z], [r_tmp])
        kb.tt("dve", Wb[:, 1], Wb[:, 1], tmpb, ALU.add, [r_W, r_tmp], [r_W])
        yield
        for ri in range(2):
            for l in range(4):
                rg = 4 * j + l
                rb = r8[:, rg:rg + 1].to_broadcast([128, 256])
                p.op("dve", lambda E, ri=ri, l=l, rb=rb: E.tensor_tensor_scan(
                    out=Zb[:, ri, l, :], data0=rb, data1=Wb[:, ri, l, :], initial=0.0, op0=ALU.mult, op1=ALU.add),
                    [r_W, r_lp, r_Z], [r_Z])
        yield
        Gr, Gi = Zb[:, 0], Zb[:, 1]
        kb.tt("dve", Wb[:, 0], Er, Gr, ALU.mult, [r_E, r_Z, r_W], [r_W])
        kb.tt("dve", tmpb, Ei, Gi, ALU.mult, [r_E, r_Z], [r_tmp])
        kb.tt("dve", Hb[:, 0, :, 1:257], Wb[:, 0], tmpb, ALU.add, [r_W, r_tmp], [r_Hb])
        kb.tt("dve", hl[:, 0, 4 * j:4 * j + 4].unsqueeze(2), Wb[:, 0, :, 255:256], tmpb[:, :, 255:256], ALU.add,
              [r_W, r_tmp], [r_hl])
        kb.tt("dve", Wb[:, 1], Er, Gi, ALU.mult, [r_E, r_Z, r_W], [r_W])
        kb.tt("dve", tmpb, Ei, Gr, ALU.mult, [r_E, r_Z, r_Hb], [r_tmp])
        kb.tt("dve", Hb[:, 1, :, 1:257], Wb[:, 1], tmpb, ALU.subtract, [r_W, r_tmp], [r_Hb])
        kb.tt("dve", hl[:, 1, 4 * j:4 * j + 4].unsqueeze(2), Wb[:, 1, :, 255:256], tmpb[:, :, 255:256], ALU.subtract,
              [r_W, r_tmp], [r_hl])
        yield
        for tl in range(7, -1, -1):
            bap, brg = kb.fixed_bank(tl // 2)
            yo = bap[:, (tl % 2) * 256:(tl % 2) * 256 + 256]
            for sidx in range(tl + 1):
                kb.mm(yo, Kbd[:, tl - sidx, :], uj[:, sidx, :], sidx == 0, False, [r_K] + ru, [brg])
            for l in range(4):
                for ri in range(2):
                    kb.mm(yo, QC[:, l, tl, ri, :], Hb[:, ri, l, 0:256], False, (l == 3 and ri == 1), [r_QC, r_Hb],
                          [brg])
            if tl % 2 == 0:
                b_ = tl // 2
                dst = uj[:, 2 * b_:2 * b_ + 2, :]
                kb.act(xg, bap, AF.Identity, [brg], [r_xg], scale=0.5)
                kb.act(tg, xg, AF.Square, [r_xg], [r_tg])
                kb.act(tg, tg, AF.Identity, [r_tg, r_cst], [r_tg], scale=4.0 * GC0, bias=cst[:, 1:2])
                kb.tt("dve", tg, tg, xg, ALU.mult, [r_tg, r_xg], [r_tg])
                kb.act(tg, tg, AF.Tanh, [r_tg], [r_tg], scale=2.0 * GC1)
                kb.stt("dve", dst, tg.rearrange("p (s c) -> p s c", s=2), 1.0, xg.rearrange("p (s c) -> p s c", s=2),
                       ALU.add, ALU.mult, [r_tg, r_xg], ru)
                yield

    for _ in s5_setup(0):
        pass
    for j in range(4):
        ga = s5_setup(j + 1) if j + 1 < 4 else iter(())
        gb = s5_main(j)
        da = db = False
        while not (da and db):
            if not db:
                try:
                    next(gb)
                except StopIteration:
                    db = True
            for _ in range(2):
                if not da:
                    try:
                        next(ga)
                    except StopIteration:
                        da = True
    bap, brg = kb.bank()
    for ri in range(2):
        kb.tr(bap[:16, ri * 128:(ri + 1) * 128], hl[:, ri, :], ident_f, [r_hl, r_ident], [brg])
    kb.cp("act", hlT[:16].rearrange("p r n -> p (r n)"), bap[:16, 0:256], [brg], [r_hlT])
    for ri in range(2):
        kb.dma("sp", ssmp_d[ri], hlT[:16, ri, :], [r_hlT], [Rg()])
    ru_all = lambda k: [r_u[k][tb] for tb in range(4)]
    for pb in range(5):
        N = 512 if pb < 4 else NS

        def ysl(k, pb=pb):
            if pb < 4:
                return uperm[:, k, 2 * pb:2 * pb + 2, :].rearrange("p s c -> p (s c)")
            return usamp[:, k, :]
        for e in range(4):
            bap, brg = kb.bank()
            for kc in range(4):
                kb.mm(bap[:, :N], wglu_sb[:, kc, e * 128:(e + 1) * 128], ysl(kc), kc == 0, kc == 3,
                      [r_wglu] + (ru_all(kc) if pb < 4 else [r_u[kc][4]]), [brg])
            kb.act(tg[:, :N], bap[:, :N], AF.Tanh, [brg, r_pa], [r_tg], scale=0.5, bias=hbg[:, e:e + 1])
            kb.act(tg[:, :N], tg[:, :N], AF.Identity, [r_tg, r_cst], [r_tg], scale=0.5, bias=chalf)
            if pb < 4:
                dst = mixT[:, 4 + e, 0:TP].rearrange("p (c s) -> p s c", s=8)[:, 2 * pb:2 * pb + 2, :]
                kb.tt("dve", dst, tg.rearrange("p (s c) -> p s c", s=2), uperm[:, e, 2 * pb:2 * pb + 2, :], ALU.mult,
                      [r_tg] + ru_all(e), r_mix[0:4])
            else:
                kb.tt("dve", mixT[:, 4 + e, TP:TP + NS], tg[:, :N], usamp[:, e, :], ALU.mult, [r_tg, r_u[e][4]],
                      [r_mix[4]])
    kb.bank_list = None

    p.barrier()
    A.top = base_persist
    w_out_sb = A.bf16(8 * D).rearrange("p (k n) -> p k n", k=8)
    w2_sb = A.bf16(32 * D).rearrange("p (k n) -> p k n", k=32)
    assert A.top <= base_ab
    A.top = base_c
    gb = A.f32(4096).rearrange("p (a d) -> p a d", a=4)
    for i in range(4):
        kb.dma("sp", gb[:, i, :], ln_d[i:i + 1, :].partition_broadcast(128), [], [r_gb])
    xtc = [A.f32(1024) for _ in range(2)]
    x1 = [A.f32(1024) for _ in range(4)]
    rr = [A.f32(1024) for _ in range(2)]
    x1T = A.bf16(8 * 256).rearrange("p (k t) -> p k t", k=8)
    hT = A.bf16(32 * 256).rearrange("p (k t) -> p k t", k=32)
    w1b = [A.bf16(8 * 512).rearrange("p (k n) -> p k n", k=8) for _ in range(2)]
    rtmp = [A.bf16(256), A.bf16(256)]
    stats = A.f32(64)
    r_wout, r_w2 = rgs(8), rgs(32)
    r_xtc, r_x1, r_rr, r_w1b = rgs(2), rgs(4), rgs(2), rgs(2)
    r_x1T, r_hT, r_rtmp, r_stats = [rgs(2), rgs(2)], rgs(32), rgs(2), rgs(2)
    kb.dma("sp", w_out_sb, wos.rearrange("(k p) n -> p k n", p=128), [r_wos], r_wout)
    for g in range(8):
        kb.dma("sp", w2_sb[:, 4 * g:4 * g + 4, :],
               w2s[g * 512:(g + 1) * 512, :].rearrange("(k p) n -> p k n", p=128), [r_w2s[g]], r_w2[4 * g:4 * g + 4])

    def layernorm(src, rows, gi, dst, rsrc, rdst, sidx):
        stt_ = stats[:, sidx * 32:(sidx + 1) * 32]
        bst = stt_[:, 0:12].rearrange("p (c s) -> p c s", c=2)
        mv = stt_[:, 12:14]
        rstd = stt_[:, 14:15]
        nmr = stt_[:, 15:16]
        rs = r_stats[sidx]
        for c in range(2):
            p.op("dve", lambda E, c=c: E.bn_stats(out=bst[:rows, c, :], in_=src[:rows, c * 512:(c + 1) * 512]),
                 [rsrc], [rs])
        p.op("dve", lambda E: E.bn_aggr(out=mv[:rows, :], in_=stt_[:rows, 0:12]), [rs], [rs])
        kb.act(rstd[:rows, :], mv[:rows, 1:2], AF.Sqrt, [rs], [rs], bias=epsc[:rows, :], scale=1.0)
        p.op("dve", lambda E: E.reciprocal(out=rstd[:rows, :], in_=rstd[:rows, :]), [rs], [rs])
        kb.stt("dve", nmr[:rows, :], mv[:rows, 0:1], -1.0, rstd[:rows, :], ALU.mult, ALU.mult, [rs], [rs])
        kb.act(src[:rows, :], src[:rows, :], AF.Identity, [rsrc, rs], [rsrc], bias=nmr[:rows, :],
               scale=rstd[:rows, :])
        kb.tt("dve", src[:rows, :], src[:rows, :], gb[:rows, gi, :], ALU.mult, [rsrc, r_gb], [rsrc])
        kb.tt("dve", dst[:rows, :], src[:rows, :], gb[:rows, gi + 1, :], ALU.add, [rsrc, r_gb], [rdst])

    blocks = []
    tile_i = 0
    for cb, (t0, N) in enumerate(CBLK):
        tiles = []
        for tt_ in range((N + 127) // 128):
            tiles.append(dict(j=tile_i % 2, jx=tile_i % 4, rows=min(128, N - tt_ * 128), tk=t0 + tt_ * 128, tt=tt_))
            tile_i += 1
        blocks.append(dict(N=N, tiles=tiles))
    w1i = [0]

    def front_a(blk):
        for T in blk["tiles"]:
            j, jx, rows, tk = T["j"], T["jx"], T["rows"], T["tk"]
            kb.dma("sp", xtc[j][:rows, :], xs[tk:tk + rows, :], [], [r_xtc[j]])
            for hh in range(2):
                bap, brg = kb.bank()
                for kc in range(8):
                    kb.mm(bap[:rows, :], mixT[:, kc, tk:tk + rows], w_out_sb[:, kc, hh * 512:(hh + 1) * 512],
                          kc == 0, kc == 7, r_mix + [r_wout[kc]], [brg])
                kb.stt("dve", rr[j][:rows, hh * 512:(hh + 1) * 512], xtc[j][:rows, hh * 512:(hh + 1) * 512], ALPHA,
                       bap[:rows, :], ALU.mult, ALU.add, [brg, r_xtc[j]], [r_rr[j]])
            layernorm(rr[j], rows, 0, x1[jx], r_rr[j], r_x1[jx], j)

    def front_b(blk):
        for T in blk["tiles"]:
            jx, rows, tt_ = T["jx"], T["rows"], T["tt"]
            bks = [kb.bank(), kb.bank()]
            for kc in range(8):
                bap, brg = bks[kc // 4]
                kb.tr(bap[:, (kc % 4) * 128:(kc % 4) * 128 + rows], x1[jx][:rows, kc * 128:(kc + 1) * 128],
                      ident_f[:rows, :rows], [r_x1[jx], r_ident], [brg])
            for h in range(2):
                bap, brg = bks[h]
                src = bap.rearrange("p (k t) -> p k t", k=4)[:, :, :rows]
                dst = x1T[:, 4 * h:4 * h + 4, tt_ * 128:tt_ * 128 + rows]
                kb.cp("act" if h == 0 else "dve", dst, src, [brg], [r_x1T[tt_][h]])

    def ffn1(blk):
        N = blk["N"]
        rd_x1T = [r for q in r_x1T[:len(blk["tiles"])] for r in q]
        for g in range(8):
            wb, rwb = w1b[w1i[0] % 2], r_w1b[w1i[0] % 2]
            w1i[0] += 1
            kb.dma("sp", wb, w1s[:, g * 512:(g + 1) * 512].rearrange("(k p) n -> p k n", p=128), r_w1s, [rwb])
            for f4 in range(4):
                fc = 4 * g + f4
                bap, brg = kb.bank()
                for kc in range(8):
                    kb.mm(bap[:, :N], wb[:, kc, f4 * 128:(f4 + 1) * 128], x1T[:, kc, :N], kc == 0, kc == 7,
                          rd_x1T + [rwb], [brg])
                rt_, rrt_ = rtmp[fc % 2], r_rtmp[fc % 2]
                kb.act(rt_[:, :N], bap[:, :N], AF.Relu, [brg], [rrt_])
                kb.tt("dve", hT[:, fc, :N], rt_[:, :N], rt_[:, :N], ALU.mult, [rrt_], [r_hT[fc]])

    def back(blk):
        for T in blk["tiles"]:
            j, jx, rows, tk, tt_ = T["j"], T["jx"], T["rows"], T["tk"], T["tt"]
            for hh in range(2):
                bap, brg = kb.bank()
                for fc in range(32):
                    kb.mm(bap[:rows, :], hT[:, fc, tt_ * 128:tt_ * 128 + rows], w2_sb[:, fc, hh * 512:(hh + 1) * 512],
                          fc == 0, fc == 31, [r_hT[fc], r_w2[fc]], [brg])
                kb.stt("dve", rr[j][:rows, hh * 512:(hh + 1) * 512], x1[jx][:rows, hh * 512:(hh + 1) * 512], ALPHA,
                       bap[:rows, :], ALU.mult, ALU.add, [brg, r_x1[jx]], [r_rr[j]])
            layernorm(rr[j], rows, 2, x1[jx], r_rr[j], r_x1[jx], j)
            kb.dma("pool", y_d[tk:tk + rows, :], x1[jx][:rows, :], [r_x1[jx]], [Rg()])

    front_a(blocks[0])
    front_b(blocks[0])
    for cb, blk in enumerate(blocks):
        ffn1(blk)
        if cb + 1 < len(blocks):
            front_a(blocks[cb + 1])
        back(blk)
        if cb + 1 < len(blocks):
            front_b(blocks[cb + 1])

    p.emit(nc, st)
    return nc, kb, st


_CACHE = {}


def _get_nc(debug=False):
    key = ("nc", debug)
    if key not in _CACHE:
        _CACHE[key] = build(debug=debug)
    return _CACHE[key]


def host_inputs(inp, c):
    f = np.float32
    xs = np.concatenate([inp["x_prompt"][c], inp["x_sample"][NS * c:NS * (c + 1), 0, :]], axis=0)
    m = {
        "xs": np.ascontiguousarray(xs, dtype=f),
        "w_in": np.ascontiguousarray(inp["w_in"][0], dtype=f),
        "w_out": np.ascontiguousarray(inp["w_out"][0], dtype=f),
        "w_ff1": np.ascontiguousarray(inp["w_ff1"][0], dtype=f),
        "w_ff2": np.ascontiguousarray(inp["w_ff2"][0], dtype=f),
        "wc": np.ascontiguousarray(inp["w_conv"][0].reshape(4, 12, 128).transpose(2, 1, 0), dtype=f),
        "ln": np.ascontiguousarray(np.stack([inp["ln1_g"][0], inp["ln1_b"][0], inp["ln2_g"][0], inp["ln2_b"][0]]),
                                   dtype=f),
        "ident": np.eye(128, dtype=f),
        "state_conv": np.ascontiguousarray(inp["state_conv"][0, NS * c:NS * (c + 1)], dtype=f),
        "w_glu": np.ascontiguousarray(inp["w_glu"][0], dtype=f),
    }

    def sl(a):
        return a.reshape(16, 2, 64).transpose(1, 2, 0).reshape(128, 16)
    m["s5a"] = np.ascontiguousarray(np.stack([sl(inp["s5_a_re"][0]), sl(inp["s5_a_im"][0]),
                                              sl(np.repeat(inp["s5_log_dt"][0][:, None], 64, axis=1))], axis=1),
                                    dtype=f)

    def padx(a):
        a = a.reshape(16, 2, 64, 16)
        o = np.zeros((2, 64, 16, 2, 16), f)
        for h in range(2):
            o[h, :, :, h, :] = a[:, h].transpose(1, 0, 2)
        return o.reshape(128, 16, 32)
    m["s5bx"] = np.ascontiguousarray(np.stack([padx(inp["s5_b_re"][0]), padx(inp["s5_b_im"][0])], axis=1))
    m["s5cx"] = np.ascontiguousarray(np.stack([padx(inp["s5_c_re"][0].transpose(0, 2, 1)),
                                               padx(inp["s5_c_im"][0].transpose(0, 2, 1))], axis=1))
    m["s5d"] = np.ascontiguousarray(np.stack([inp["s5_d"][0].reshape(4, 128).T, inp["b_glu"][0].reshape(4, 128).T]),
                                    dtype=f).transpose(1, 0, 2).copy()
    tri = np.triu(np.ones((128, 128), f))
    neg = np.float32(-1e30)
    maskL = np.where(np.tril(np.ones((128, 128), bool)), f(0), neg).astype(f)
    m["gconst"] = np.ascontiguousarray(np.stack([tri, maskL, maskL.T, np.tril(np.ones((128, 128), f), -1)], axis=1))
    m["gdn_par"] = np.ascontiguousarray(np.stack([inp["gdn_a_log"][0], inp["gdn_dt_bias"][0], np.zeros(4, f)]),
                                        dtype=f)
    m["gdn_nw"] = np.ascontiguousarray(inp["gdn_norm_w"][0].reshape(128, 1), dtype=f)
    m["state_gdn"] = np.ascontiguousarray(inp["state_gdn"][0, NS * c:NS * (c + 1)], dtype=f)
    m["state_ssm"] = np.ascontiguousarray(np.stack([inp["state_ssm_re"][0, NS * c:NS * (c + 1)].reshape(NS, 2048),
                                                    inp["state_ssm_im"][0, NS * c:NS * (c + 1)].reshape(NS, 2048)],
                                                   axis=1), dtype=f)
    return m


def run(inp, debug=False, trace=False):
    nc, kb, st = _get_nc(debug)
    in_maps = []
    for c in range(NCORES):
        m = host_inputs(inp, c)
        in_maps.append({k: m[k] for k in kb.ins})
    res = run_bass_kernel_spmd(nc, in_maps, core_ids=list(range(NCORES)), trace=trace)
    return res


def kernel(**inp):
    inp = {k: np.asarray(v) for k, v in inp.items()}
    res = run(inp)
    R = res.results
    f = np.float32
    y_p = np.stack([R[c]["y"][:TP] for c in range(NCORES)]).astype(f)
    y_s = np.concatenate([R[c]["y"][TP:] for c in range(NCORES)])[:, None, :].astype(f)
    conv_p = np.stack([R[c]["conv_p"] for c in range(NCORES)])[None].astype(f)
    conv_s = np.concatenate([R[c]["conv_s"] for c in range(NCORES)])[None].astype(f)
    gdn_p = np.stack([R[c]["gdn_p"] for c in range(NCORES)])[None].astype(f)
    re_p = np.stack([R[c]["ssm_p"][0].reshape(32, 64) for c in range(NCORES)])[None].astype(f)
    im_p = np.stack([R[c]["ssm_p"][1].reshape(32, 64) for c in range(NCORES)])[None].astype(f)
    gdn_s = np.concatenate([R[c]["gdn_s"] for c in range(NCORES)])[None].astype(f)
    re_s = np.concatenate([R[c]["ssm_s"][:, 0].reshape(NS, 32, 64) for c in range(NCORES)])[None].astype(f)
    im_s = np.concatenate([R[c]["ssm_s"][:, 1].reshape(NS, 32, 64) for c in range(NCORES)])[None].astype(f)
    return (y_p, y_s, gdn_p, conv_p, re_p, im_p, gdn_s, conv_s, re_s, im_s)
```

```python
import contextlib
import math
import os
import numpy as np
import concourse.bass as bass
import concourse.mybir as mybir
from concourse.bass_utils import run_bass_kernel_spmd

F32 = mybir.dt.float32
BF16 = mybir.dt.bfloat16
AF = mybir.ActivationFunctionType
ALU = mybir.AluOpType
AX = mybir.AxisListType


class Rg:
    __slots__ = ("w", "r", "name")

    def __init__(self, name=""):
        self.w = None
        self.r = {}
        self.name = name


def rgs(n, name=""):
    return [Rg(f"{name}{i}") for i in range(n)]


class Prog:
    COMPUTE = ("pe", "act", "dve", "pool")
    NDMASEM = 6

    def __init__(self):
        self.ops = {e: [] for e in ("pe", "act", "dve", "pool", "sp")}
        self.dma_count = {}
        self.floor = set()
        self.nobar = set()

    def barrier(self):
        fl = set()
        for e in self.COMPUTE:
            if self.ops[e]:
                for i in range(len(self.ops[e]) - 1, -1, -1):
                    if not self.ops[e][i]["dma"]:
                        fl.add((e, i))
                        break
        for q, n in self.dma_count.items():
            for k in range(max(0, n - self.NDMASEM), n):
                if ("dma:" + q, k) not in self.nobar:
                    fl.add(("dma:" + q, k))
        self.floor = fl

    def op(self, eng, fn, reads=(), writes=(), dma=False):
        deps = set(self.floor)
        for r in reads:
            if r.w is not None:
                deps.add(r.w)
        for w in writes:
            if w.w is not None:
                deps.add(w.w)
            for e, i in w.r.items():
                deps.add((e, i))
        idx = len(self.ops[eng])
        if dma:
            k = self.dma_count.get(eng, 0)
            self.dma_count[eng] = k + 1
            me = ("dma:" + eng, k)
        else:
            me = (eng, idx)
            nd = set()
            for d in deps:
                if d[0] == eng:
                    if eng == "pe":
                        continue
                    isr = any((r.w == d) for r in reads) or any((w.w == d) for w in writes)
                    if not isr:
                        continue
                nd.add(d)
            deps = nd
        self.ops[eng].append(dict(fn=fn, deps=deps, dma=dma, me=me))
        for r in reads:
            r.r[me[0]] = me[1]
        for w in writes:
            w.w = me
            w.r = {}
        return me

    def emit(self, nc, stack):
        sems = {e: stack.enter_context(nc.semaphore("s_" + e)) for e in self.COMPUTE}
        dsems = {}
        for q in self.dma_count:
            dsems[q] = [stack.enter_context(nc.semaphore(f"d_{q}{i}")) for i in range(self.NDMASEM)]
        sig = {e: set() for e in self.COMPUTE}
        for e, lst in self.ops.items():
            for o in lst:
                for d in o["deps"]:
                    if d[0] in sig:
                        sig[d[0]].add(d[1])
        rank = {}
        for e in self.COMPUTE:
            rank[e] = {idx: i + 1 for i, idx in enumerate(sorted(sig[e]))}
        K = self.NDMASEM

        def dep_sem(d):
            s, i = d
            if s.startswith("dma:"):
                q = s[4:]
                return dsems[q][i % K], 16 * (i // K + 1), ("d", q, i % K)
            return sems[s], rank[s][i], ("c", s)

        block = stack.enter_context(nc.Block())
        getters = {"pe": block.tensor, "act": block.scalar, "dve": block.vector, "pool": block.gpsimd,
                   "sp": block.sync}

        def make(eng):
            lst = self.ops[eng]

            def body(E):
                waited = {}
                for o in lst:
                    need = {}
                    for d in o["deps"]:
                        sh, val, key = dep_sem(d)
                        if need.get(key, (None, 0))[1] < val:
                            need[key] = (sh, val)
                    if o["dma"]:
                        q, k = o["me"][0][4:], o["me"][1]
                        if k >= K:
                            key = ("d", q, k % K)
                            val = 16 * (k // K)
                            if need.get(key, (None, 0))[1] < val:
                                need[key] = (dsems[q][k % K], val)
                    for key, (sh, val) in need.items():
                        if waited.get(key, 0) >= val:
                            continue
                        E.wait_ge(sh, val)
                        waited[key] = val
                    ins = o["fn"](E)
                    if o["dma"]:
                        q, k = o["me"][0][4:], o["me"][1]
                        ins.then_inc(dsems[q][k % K], 16)
                    elif o["me"][1] in sig.get(eng, ()):
                        ins.then_inc(sems[eng], 1)
                n = self.dma_count.get(eng, 0)
                for j in range(min(K, n)):
                    cnt = (n - 1 - j) // K + 1
                    E.wait_ge(dsems[eng][j], 16 * cnt)
            return body

        for eng in ("sp", "pe", "act", "dve", "pool"):
            if self.ops[eng] or self.dma_count.get(eng, 0):
                getters[eng](make(eng))


NCORES = 8
D = 1024
TP = 2048
NS = 16
NT = TP + NS
NPROJ = 2568
DFF = 4096
ALPHA = 2.0 ** 0.25
EPS = 1e-6


class KB:
    def __init__(self, nc, stack):
        self.nc = nc
        self.st = stack
        self.p = Prog()
        self.ps = stack.enter_context(nc.psum_tensor("ps_all", [128, 4096], F32))
        self.bank_rg = rgs(8, "bank")
        self.bank_i = 0
        self.ins = {}
        self.outs = {}

    def sb(self, name, shape, dtype=F32):
        return self.st.enter_context(self.nc.sbuf_tensor(name, list(shape), dtype))

    def din(self, name, shape, dtype=F32):
        t = self.nc.dram_tensor(name, list(shape), dtype, kind="ExternalInput").ap()
        self.ins[name] = t
        return t

    def dout(self, name, shape, dtype=F32):
        t = self.nc.dram_tensor(name, list(shape), dtype, kind="ExternalOutput").ap()
        self.outs[name] = t
        return t

    def dtmp(self, name, shape, dtype):
        return self.nc.dram_tensor(name, list(shape), dtype, kind="Internal").ap()

    def bank(self):
        lst = getattr(self, "bank_list", None) or list(range(8))
        i = lst[self.bank_i % len(lst)]
        self.bank_i += 1
        return self.ps[:, i * 512:(i + 1) * 512], self.bank_rg[i]

    def fixed_bank(self, i):
        return self.ps[:, i * 512:(i + 1) * 512], self.bank_rg[i]

    def mm(self, out, lhsT, rhs, start, stop, rd, wr):
        return self.p.op("pe", lambda E: E.matmul(out, lhsT=lhsT, rhs=rhs, start=start, stop=stop), rd, wr)

    def tr(self, out, in_, ident, rd, wr):
        return self.p.op("pe", lambda E: E.transpose(out=out, in_=in_, identity=ident), rd, wr)

    def act(self, out, in_, func, rd, wr, bias=None, scale=None):
        kw = {}
        if bias is not None:
            kw["bias"] = bias
        if scale is not None:
            kw["scale"] = scale
        return self.p.op("act", lambda E: E.activation(out=out, in_=in_, func=func, **kw), rd, wr)

    def ts(self, eng, out, in0, s1, s2, op0, op1, rd, wr):
        if s2 is None:
            return self.p.op(eng, lambda E: E.tensor_scalar(out=out, in0=in0, scalar1=s1, scalar2=None, op0=op0),
                             rd, wr)
        return self.p.op(eng, lambda E: E.tensor_scalar(out=out, in0=in0, scalar1=s1, scalar2=s2, op0=op0, op1=op1),
                         rd, wr)

    def tt(self, eng, out, in0, in1, op, rd, wr):
        return self.p.op(eng, lambda E: E.tensor_tensor(out=out, in0=in0, in1=in1, op=op), rd, wr)

    def stt(self, eng, out, in0, scalar, in1, op0, op1, rd, wr):
        return self.p.op(eng, lambda E: E.scalar_tensor_tensor(out=out, in0=in0, scalar=scalar, in1=in1, op0=op0,
                                                               op1=op1), rd, wr)

    def cp(self, eng, out, in_, rd, wr):
        if eng == "act":
            return self.p.op("act", lambda E: E.copy(out=out, in_=in_), rd, wr)
        return self.p.op(eng, lambda E: E.tensor_copy(out=out, in_=in_), rd, wr)

    def memset(self, eng, ap, val, wr):
        return self.p.op(eng, lambda E: E.memset(ap, val), (), wr)

    def dma(self, q, out, in_, rd, wr, nobar=False, **kw):
        me = self.p.op(q, lambda E: E.dma_start(out=out, in_=in_, **kw), rd, wr, dma=True)
        if nobar:
            self.p.nobar.add(me)
        return me


class Arena:
    def __init__(self, kb, ncols):
        self.t = kb.sb("arena", [128, ncols], F32)
        self.n = ncols
        self.top = 0

    def f32(self, cols):
        a = self.top
        self.top += cols
        assert self.top <= self.n, ("arena overflow", self.top, self.n)
        return self.t[:, a:a + cols]

    def bf16(self, cols):
        c = (cols + 1) // 2
        a = self.top
        self.top += c
        assert self.top <= self.n, ("arena overflow", self.top, self.n)
        return self.t[:, a:a + c].bitcast(BF16)[:, 0:cols]


TBLK = [(0, 512), (512, 512), (1024, 512), (1536, 512), (2048, 16)]
CBLK = [(i * 256, 256) for i in range(8)] + [(2048, 16)]


def build(stage="full", debug=False):
    nc = bass.Bass("TRN2", target_bir_lowering=False)
    st = contextlib.ExitStack()
    kb = KB(nc, st)
    p = kb.p
    dbg = {}

    xs = kb.din("xs", [NT, D])
    w_in = kb.din("w_in", [D, NPROJ])
    w_out = kb.din("w_out", [D, D])
    w_ff1 = kb.din("w_ff1", [D, DFF])
    w_ff2 = kb.din("w_ff2", [DFF, D])
    wc_d = kb.din("wc", [128, 12, 4])
    ln_d = kb.din("ln", [4, D])
    ident_d = kb.din("ident", [128, 128])
    sconv_d = kb.din("state_conv", [NS, 3, 1536])
    y_d = kb.dout("y", [NT, D])
    convp_d = kb.dout("conv_p", [3, 1536])
    convs_d = kb.dout("conv_s", [NS, 3, 1536])
    w1s = kb.dtmp("w1s", [D, DFF], BF16)

    A = Arena(kb, 53200)
    ident_f = A.f32(128)
    ones_f = A.f32(128)
    onesq_f = A.f32(128)
    wc = A.f32(48).rearrange("p (m j) -> p m j", j=4)
    halo = A.f32(36).rearrange("p (m j) -> p m j", j=3)
    lg = A.f32(17 * 8).rearrange("p (t j) -> p t j", j=8)
    r_ident, r_ones, r_gb, r_wc, r_halo = Rg(), Rg(), Rg(), Rg(), rgs(12)
    r_lg = rgs(17)
    kb.dma("sp", ident_f, ident_d, [], [r_ident])
    kb.dma("sp", wc, wc_d, [], [r_wc])
    kb.memset("pool", ones_f, 1.0, [r_ones])
    kb.memset("pool", onesq_f, 128.0, [r_ones])
    kb.memset("pool", halo, 0.0, r_halo)
    kb.memset("pool", lg, 0.0, r_lg)
    epsc = A.f32(1)
    kb.memset("pool", epsc, EPS, [r_ones])
    base_persist = A.top
    qkv = A.bf16(12 * NT).rearrange("p (m t) -> p m t", m=12)
    zT = A.bf16(4 * NT).rearrange("p (m t) -> p m t", m=4)
    unat = A.bf16(4 * TP).rearrange("p (m t) -> p m t", m=4)
    uperm = unat.rearrange("p m (s c) -> p m s c", s=8)
    usamp = A.bf16(4 * NS).rearrange("p (m t) -> p m t", m=4)
    r_qkv = [[Rg() for _ in range(5)] for _ in range(12)]
    r_z = [[Rg() for _ in range(5)] for _ in range(4)]
    r_u = [[Rg() for _ in range(5)] for _ in range(4)]
    base_ab = A.top

    xT = A.bf16(8 * NT).rearrange("p (k t) -> p k t", k=8)
    w_in_sb = A.bf16(8 * NPROJ).rearrange("p (k n) -> p k n", k=8)
    xt = [A.f32(1024) for _ in range(2)]
    rawb = [A.f32(515) for _ in range(2)]
    acc = [A.f32(512) for _ in range(2)]
    big = A.f32(1536)
    tb = A.f32(512)
    r_tb = Rg()
    scv = A.f32(576).rearrange("p (m b) -> p m b", m=12)
    r_xt, r_rawb, r_acc = rgs(2), rgs(2), rgs(2)
    r_big, r_scv = Rg(), Rg()
    r_xT = [[Rg(), Rg()] for _ in range(17)]
    r_win = rgs(8)
    for kc in range(8):
        kb.dma("pool", w_in_sb[:, kc, :], w_in[kc * 128:(kc + 1) * 128, :], [], [r_win[kc]])
    r_w1s = rgs(8)
    w2s = kb.dtmp("w2s", [DFF, D], BF16)
    wos = kb.dtmp("wos", [D, D], BF16)
    r_w2s, r_wos = rgs(8), Rg()


    for t in range(17):
        rows = 128 if t < 16 else NS
        xb, rx = xt[t % 2], r_xt[t % 2]
        kb.dma("sp", xb[:rows, :], xs[t * 128:t * 128 + rows, :], [], [rx])
        bks = [kb.bank(), kb.bank()]
        for kc in range(8):
            bap, brg = bks[kc // 4]
            kb.tr(bap[:, (kc % 4) * 128:(kc % 4) * 128 + rows], xb[:rows, kc * 128:(kc + 1) * 128],
                  ident_f[:rows, :rows], [rx, r_ident], [brg])
        for h in range(2):
            bap, brg = bks[h]
            src = bap.rearrange("p (k t) -> p k t", k=4)[:, :, :rows]
            dst = xT[:, 4 * h:4 * h + 4, t * 128:t * 128 + rows]
            kb.cp("act" if h == 0 else "dve", dst, src, [brg], [r_xT[t][h]])

    sc_tok = big
    kb.dma("sp", sc_tok[:48, :], sconv_d.rearrange("b j c -> (b j) c"), [], [r_big])
    bks = [kb.bank(), kb.bank()]
    for m in range(12):
        bap, brg = bks[m // 6]
        kb.tr(bap[:, (m % 6) * 48:(m % 6) * 48 + 48], sc_tok[:48, m * 128:(m + 1) * 128], ident_f[:48, :48],
              [r_big, r_ident], [brg])
    for h in range(2):
        bap, brg = bks[h]
        kb.cp("dve", scv[:, 6 * h:6 * h + 6, :], bap[:, 0:288].rearrange("p (m b) -> p m b", m=6), [brg], [r_scv])

    def colof(m):
        if m < 16:
            return m * 128
        return 2056 + (m - 16) * 128

    evi = 0
    for b, (t0, N) in enumerate(TBLK):
        tiles = list(range(4 * b, 4 * b + 4)) if N == 512 else [16]
        rd_x = [r for t in tiles for r in r_xT[t]]
        for m in range(20):
            c0 = colof(m)
            bap, brg = kb.bank()
            for kc in range(8):
                kb.mm(bap[:, :N], w_in_sb[:, kc, c0:c0 + 128], xT[:, kc, t0:t0 + N], kc == 0, kc == 7,
                      rd_x + [r_win[kc]], [brg])
            evi += 1
            ee = "act" if evi % 2 == 0 else "dve"
            if m < 12:
                j = m % 2
                rb, ac = rawb[j], acc[j]
                kb.cp(ee, rb[:, 3:3 + N], bap[:, :N], [brg], [r_rawb[j]])
                if N == 512:
                    kb.cp("dve", rb[:, 0:3], halo[:, m, :], [r_halo[m]], [r_rawb[j]])
                    kb.cp("dve", halo[:, m, :], rb[:, 512:515], [r_rawb[j]], [r_halo[m]])
                    srcs = [rb[:, jj:jj + N] for jj in range(4)]
                    rdc = [r_rawb[j], r_wc]
                else:
                    srcs = [scv[:, m, jj::3] for jj in range(3)] + [rb[:, 3:3 + N]]
                    rdc = [r_rawb[j], r_wc, r_scv]
                kb.ts("dve", ac[:, :N], srcs[0], wc[:, m, 0:1], None, ALU.mult, None, rdc, [r_acc[j]])
                for jj in range(1, 4):
                    kb.stt("dve", ac[:, :N], srcs[jj], wc[:, m, jj:jj + 1], ac[:, :N], ALU.mult, ALU.add,
                           rdc + [r_acc[j]], [r_acc[j]])
                kb.act(qkv[:, m, t0:t0 + N], ac[:, :N], AF.Silu, [r_acc[j]], [r_qkv[m][b]])
            elif m < 16:
                kb.act(zT[:, m - 12, t0:t0 + N], bap[:, :N], AF.Silu, [brg], [r_z[m - 12][b]])
            else:
                if N == 512:
                    kb.cp(ee, uperm[:, m - 16, :, b * 64:(b + 1) * 64], bap.rearrange("p (c s) -> p s c", s=8), [brg],
                          [r_u[m - 16][b]])
                else:
                    kb.cp(ee, usamp[:, m - 16, :], bap[:, :N], [brg], [r_u[m - 16][b]])
        for t in tiles:
            rows = 128 if t < 16 else NS
            bap, brg = kb.bank()
            for kc in range(8):
                kb.mm(bap[:rows, 0:8], xT[:, kc, t * 128:t * 128 + rows], w_in_sb[:, kc, 2048:2056], kc == 0,
                      kc == 7, r_xT[t] + [r_win[kc]], [brg])
            kb.cp("dve", lg[:rows, t, :], bap[:rows, 0:8], [brg], [r_lg[t]])

    tailraw = big
    for nb in range(3):
        bap, brg = kb.bank()
        for kc in range(8):
            kb.mm(bap[:19, :], xT[:, kc, 2045:2064], w_in_sb[:, kc, nb * 512:(nb + 1) * 512], kc == 0, kc == 7,
                  r_xT[15] + r_xT[16] + [r_win[kc]], [brg])
        kb.cp("act", tailraw[:19, nb * 512:(nb + 1) * 512], bap[:19, :], [brg], [r_big])
    kb.dma("sp", convp_d, tailraw[0:3, :], [r_big], [Rg()])
    kb.dma("sp", convs_d[:, 2, :], tailraw[3:19, :], [r_big], [Rg()])
    kb.dma("sp", convs_d[:, 0:2, :], sconv_d[:, 1:3, :], [], [Rg()])

    if debug:
        dq = kb.dout("dbg_qkv", [128, 12 * NT], BF16)
        dz = kb.dout("dbg_z", [128, 4 * NT], BF16)
        du = kb.dout("dbg_u", [128, 4 * TP], BF16)
        dl = kb.dout("dbg_lg", [128, 17 * 8], F32)
        allq = [r for row in r_qkv for r in row]
        kb.dma("sp", dq, qkv.rearrange("p m t -> p (m t)"), allq, [Rg()])
        kb.dma("sp", dz, zT.rearrange("p m t -> p (m t)"), [r for row in r_z for r in row], [Rg()])
        kb.dma("sp", du, unat.rearrange("p m t -> p (m t)"), [r for row in r_u for r in row], [Rg()])
        kb.dma("sp", dl, lg.rearrange("p t j -> p (t j)"), r_lg, [Rg()])

    p.barrier()
    A.top = base_ab
    mixT = A.bf16(8 * NT).rearrange("p (k t) -> p k t", k=8)
    base_c = A.top
    r_mix = [Rg() for _ in range(9)]

    p.barrier()
    A.top = base_c
    cm_d = kb.din("gconst", [128, 4, 128])
    gp_d = kb.din("gdn_par", [3, 4])
    nw_d = kb.din("gdn_nw", [128, 1])
    sg_d = kb.din("state_gdn", [NS, 4, 128, 128])
    gdnp_d = kb.dout("gdn_p", [4, 128, 128])
    gdns_d = kb.dout("gdn_s", [NS, 4, 128, 128])
    cm = A.f32(512).rearrange("p (a n) -> p a n", a=4)
    triu, maskL, maskU, strict01 = cm[:, 0, :], cm[:, 1, :], cm[:, 2, :], cm[:, 3, :]
    negones = A.f32(128)
    ident_b = A.bf16(128)
    gpar = A.f32(8).rearrange("p (a h) -> p a h", a=2)
    nwc = A.f32(1)
    c1 = A.f32(2)
    beta = A.f32(68).rearrange("p (t h) -> p t h", h=4)
    gg = A.f32(68).rearrange("p (t h) -> p t h", h=4)
    r_cm, r_gp, r_gate = Rg(), Rg(), Rg()
    r_cgate = Rg()
    kb.dma("sp", cm, cm_d, [], [r_cm, r_cgate])
    kb.dma("sp", gpar[:, 0, :], gp_d[0:1, :].partition_broadcast(128), [], [r_gp])
    kb.dma("sp", gpar[:, 1, :], gp_d[1:2, :].partition_broadcast(128), [], [r_gp])
    kb.dma("sp", nwc, nw_d, [], [r_gp])
    kb.memset("pool", negones, -1.0, [r_cm])
    kb.memset("pool", c1[:, 0:1], 1.0, [r_cm])
    kb.memset("pool", c1[:, 1:2], EPS, [r_cm])
    kb.cp("pool", ident_b, ident_f, [r_ident], [r_cm])
    one_c, eps_c = c1[:, 0:1], c1[:, 1:2]
    kb.act(gpar[:, 0, :], gpar[:, 0, :], AF.Exp, [r_gp], [r_gp])
    kb.ts("dve", gpar[:, 0, :], gpar[:, 0, :], -1.0, None, ALU.mult, None, [r_gp], [r_gp])
    kb.act(beta, lg[:, :, 0:4], AF.Exp, r_lg, [r_gate], scale=-1.0)
    kb.ts("dve", beta, beta, 1.0, None, ALU.add, None, [r_gate], [r_gate])
    p.op("dve", lambda E: E.reciprocal(out=beta, in_=beta), [r_gate], [r_gate])
    kb.tt("dve", gg, lg[:, :, 4:8], gpar[:, 1, :].unsqueeze(1).to_broadcast([128, 17, 4]), ALU.add, r_lg + [r_gp],
          [r_gate])
    kb.act(gg, gg, AF.Exp, [r_gate], [r_gate])
    kb.act(gg, gg, AF.Ln, [r_gate, r_cm], [r_gate], bias=one_c)
    kb.tt("dve", gg, gg, gpar[:, 0, :].unsqueeze(1).to_broadcast([128, 17, 4]), ALU.mult, [r_gate, r_gp], [r_gate])

    H4 = lambda n_: [128, 4, n_]
    Sst = A.f32(512).rearrange("p (h e) -> p h e", h=4)
    Sb = A.bf16(512).rearrange("p (h e) -> p h e", h=4)
    r_S, r_Sb = Rg(), Rg()
    kb.memset("dve", Sst, 0.0, [r_S])
    kb.memset("dve", Sb, 0.0, [r_Sb])
    gate = [r_cgate]
    for kc in range(8):
        kb.dma("pool", w1s[kc * 128:(kc + 1) * 128, :], w_ff1[kc * 128:(kc + 1) * 128, :], gate, [r_w1s[kc]],
               nobar=True)
    kb.dma("pool", wos, w_out, gate, [r_wos], nobar=True)
    for g in range(8):
        kb.dma("pool", w2s[g * 512:(g + 1) * 512, :], w_ff2[g * 512:(g + 1) * 512, :], gate, [r_w2s[g]], nobar=True)

    def b4(n=512):
        return A.bf16(n).rearrange("p (h e) -> p h e", h=4)

    def f4(n=512):
        return A.f32(n).rearrange("p (h e) -> p h e", h=4)
    gcs = A.f32(8)
    sml = A.f32(40).rearrange("p (a h) -> p a h", h=4)
    Gm, Dm, DTm, sqt = f4(), f4(), f4(), f4()
    Lf = sqt
    alias_base = A.top
    khtok, qhtok, qgtok = b4(), b4(), b4()
    khT, qhT = b4(), b4()
    CHB = 0
    c4 = f4
    ident_c, r_identc = ident_f, r_ident
    Mb = [c4(), c4()]
    MTb = [c4(), c4()]
    TTb = [c4(), c4()]
    TTf = [f4(), f4()]
    bvP = [f4(), f4()]
    begkP = [f4(), f4()]
    qkTP = [b4(), b4()]
    qgTP = [b4(), b4()]
    kdecP = [b4(), b4()]
    eglP = [A.f32(4), A.f32(4)]
    nwT, vnew, onb, sqt2 = f4(), b4(), b4(), f4()
    so2 = A.f32(4)
    r_gcs, r_sml = Rg(), Rg()
    R_ = {k: Rg(k) for k in ["khtok", "qhtok", "qgtok", "khT", "qhT", "Gm", "Dm", "DTm", "sqt", "M0", "M1", "MT0",
                             "MT1", "TT0", "TT1", "nwT", "vnew", "onb", "sqt2", "so2"]}
    RP = [{k: Rg(k + str(q)) for k in ["TTf", "bv", "begk", "qkT", "qgT", "kdec", "egl"]} for q in range(2)]

    def bfbank():
        bap, brg = kb.bank()
        return bap.bitcast(BF16)[:, 0:512].rearrange("p (h e) -> p h e", h=4), brg

    def fbank():
        bap, brg = kb.bank()
        return bap.rearrange("p (h e) -> p h e", h=4), brg

    def bc(col, n=128, rows=128):
        return col.unsqueeze(2).to_broadcast([rows, 4, n])

    def hb_(mat, rows=128):
        return mat.unsqueeze(1).to_broadcast([rows, 4, 128])

    def rsq(dst, src, rows, scale_in, rd, wr):
        kb.act(dst, src, AF.Ln, rd + [r_cm], wr, bias=eps_c[:rows, :], scale=scale_in)
        kb.act(dst, dst, AF.Exp, wr, wr, scale=-0.5)

    def tok_prep(t, rows, tk, f32out=False):
        views = []
        for xi in range(3):
            bv_, brg = bfbank()
            for h in range(4):
                kb.tr(bv_[:rows, h, :], qkv[:, 4 * xi + h, tk:tk + rows], ident_b, [r_cm] + r_qkv[4 * xi + h], [brg])
            views.append((bv_, brg))
        for xi in range(2):
            bv_, brg = views[xi]
            kb.act(sqt[:rows], bv_[:rows], AF.Square, [brg], [R_["sqt"]])
            p.op("dve", lambda E, xi=xi: E.tensor_reduce(out=sml[:rows, 3 + xi, :], in_=sqt[:rows], axis=AX.X,
                                                        op=ALU.add), [R_["sqt"]], [r_sml])
        rsq(sml[:rows, 5, :], sml[:rows, 3, :], rows, 128.0, [r_sml], [r_sml])
        kb.ts("dve", sml[:rows, 5, :], sml[:rows, 5, :], 1.0, None, ALU.mult, None, [r_sml], [r_sml])
        rsq(sml[:rows, 6, :], sml[:rows, 4, :], rows, 1.0, [r_sml], [r_sml])
        return views

    GS = 1

    def part1(n):
        tk = n * 128
        q_ = n % 2
        P_ = RP[q_]
        bv, begk, qkT, qgT, kdec, egl = bvP[q_], begkP[q_], qkTP[q_], qgTP[q_], kdecP[q_], eglP[q_]
        egc, edl, rnq, rnk, c2, nb = (sml[:, 0, :], sml[:, 1, :], sml[:, 5, :], sml[:, 6, :], sml[:, 7, :],
                                      sml[:, 8, :])
        bap, brg = kb.bank()
        kb.mm(bap[:, 0:4], triu, gg[:, n, :], True, True, [r_cm, r_gate], [brg])
        kb.mm(bap[:, 4:8], ones_f, gg[:, n, :], True, True, [r_ones, r_gate], [brg])
        kb.cp("dve", gcs, bap[:, 0:8], [brg], [r_gcs])
        kb.act(egc, gcs[:, 0:4], AF.Exp, [r_gcs], [r_sml])
        kb.tt("dve", edl, gcs[:, 4:8], gcs[:, 0:4], ALU.subtract, [r_gcs], [r_sml])
        kb.act(edl, edl, AF.Exp, [r_sml], [r_sml])
        kb.act(egl, gcs[:, 4:8], AF.Exp, [r_gcs], [P_["egl"]])
        kb.tt("dve", c2, beta[:, n, :], egc, ALU.mult, [r_gate, r_sml], [r_sml])
        kb.ts("dve", nb, beta[:, n, :], -1.0, None, ALU.mult, None, [r_gate], [r_sml])
        yield
        (qv, qrg), (kv, krg), (vv, vrg) = tok_prep(n, 128, tk)
        yield
        kb.tt("dve", khtok, kv, bc(rnk), ALU.mult, [krg, r_sml], [R_["khtok"]])
        kb.tt("dve", qhtok, qv, bc(rnq), ALU.mult, [qrg, r_sml], [R_["qhtok"]])
        kb.tt("dve", qgtok, qhtok, bc(egc), ALU.mult, [R_["qhtok"], r_sml], [R_["qgtok"]])
        kb.tt("dve", bv, vv, bc(beta[:, n, :]), ALU.mult, [vrg, r_gate], [P_["bv"]])
        kb.tt("dve", begk, khtok, bc(c2), ALU.mult, [R_["khtok"], r_sml], [P_["begk"]])
        kb.tt("dve", kdec, khtok, bc(edl), ALU.mult, [R_["khtok"], r_sml], [P_["kdec"]])
        yield
        for src, dst, rdst, ev in ((khtok, khT, R_["khT"], "act"), (qhtok, qhT, R_["qhT"], "dve"),
                                   (qgtok, qgT, P_["qgT"], "act")):
            bv_, brg = bfbank()
            for h in range(4):
                kb.tr(bv_[:, h, :], src[:, h, :], ident_b, [R_["khtok"], R_["qhtok"], R_["qgtok"], r_cm], [brg])
            kb.cp(ev, dst, bv_, [brg], [rdst])
        yield
        for h in range(4):
            kb.ts("dve", Gm[:, h, :], triu, gg[:, n, h:h + 1], None, ALU.mult, None, [r_cm, r_gate], [R_["Gm"]])
        dv, drg = fbank()
        dtv, dtrg = fbank()
        for h in range(4):
            kb.mm(dv[:, h, :], Gm[:, h, :], ones_f, True, False, [R_["Gm"], r_ones], [drg])
            kb.mm(dv[:, h, :], negones, Gm[:, h, :], False, True, [R_["Gm"], r_cm], [drg])
            kb.mm(dtv[:, h, :], ones_f, Gm[:, h, :], True, False, [R_["Gm"], r_ones], [dtrg])
            kb.mm(dtv[:, h, :], Gm[:, h, :], negones, False, True, [R_["Gm"], r_cm], [dtrg])
        kb.tt("dve", Dm, dv, hb_(maskL), ALU.add, [drg, r_cm], [R_["Dm"]])
        kb.act(Dm, Dm, AF.Exp, [R_["Dm"]], [R_["Dm"]])
        kb.tt("dve", DTm, dtv, hb_(maskU), ALU.add, [dtrg, r_cm], [R_["DTm"]])
        kb.act(DTm, DTm, AF.Exp, [R_["DTm"]], [R_["DTm"]])
        yield
        kkv, kkrg = fbank()
        qkv_, qkrg = fbank()
        for h in range(4):
            kb.mm(kkv[:, h, :], khT[:, h, :], khT[:, h, :], True, True, [R_["khT"]], [kkrg])
            kb.mm(qkv_[:, h, :], khT[:, h, :], qhT[:, h, :], True, True, [R_["khT"], R_["qhT"]], [qkrg])
        kb.tt("dve", Lf, kkv, Dm, ALU.mult, [kkrg, R_["Dm"]], [R_["sqt"]])
        kb.tt("dve", Lf, Lf, hb_(strict01), ALU.mult, [R_["sqt"], r_cm], [R_["sqt"]])
        kb.tt("dve", Mb[0], Lf, bc(nb), ALU.mult, [R_["sqt"], r_sml], [R_["M0"]])
        kb.tt("dve", qkT, qkv_, DTm, ALU.mult, [qkrg, R_["DTm"]], [P_["qkT"]])
        yield
        bv_, brg = fbank()
        for h in range(4):
            kb.tr(bv_[:, h, :], Mb[0][:, h, :], ident_c, [R_["M0"], r_identc], [brg])
        kb.cp("act", MTb[0], bv_, [brg], [R_["MT0"]])
        kb.tt("dve", TTb[0], MTb[0], hb_(ident_c), ALU.add, [R_["MT0"], r_identc], [R_["TT0"]])
        yield

        def square(src, dst):
            mv_, mrg = fbank()
            mtv, mtrg = fbank()
            for h in range(4):
                kb.mm(mv_[:, h, :], MTb[src][:, h, :], Mb[src][:, h, :], True, True, [R_[f"M{src}"], R_[f"MT{src}"]],
                      [mrg])
                kb.mm(mtv[:, h, :], Mb[src][:, h, :], MTb[src][:, h, :], True, True, [R_[f"M{src}"], R_[f"MT{src}"]],
                      [mtrg])
            kb.cp("act", Mb[dst], mv_, [mrg], [R_[f"M{dst}"]])
            kb.cp("dve", MTb[dst], mtv, [mtrg], [R_[f"MT{dst}"]])
        mcur = 1
        square(0, 1)
        yield
        tcur = 0
        for lvl in range(1, 7):
            mnext = 1 - mcur
            tv, trg = fbank()
            if lvl < 6:
                square(mcur, mnext)
            for h in range(4):
                kb.mm(tv[:, h, :], Mb[mcur][:, h, :], TTb[tcur][:, h, :], True, True,
                      [R_[f"TT{tcur}"], R_[f"M{mcur}"]], [trg])
            if lvl < 6:
                kb.tt("dve", TTb[1 - tcur], TTb[tcur], tv, ALU.add, [R_[f"TT{tcur}"], trg], [R_[f"TT{1 - tcur}"]])
            else:
                kb.tt("dve", TTf[q_], TTb[tcur], tv, ALU.add, [R_[f"TT{tcur}"], trg], [P_["TTf"]])
            tcur = 1 - tcur
            mcur = mnext
            yield

    def part2(n):
        tk = n * 128
        q_ = n % 2
        P_ = RP[q_]
        bv, begk, qkT, qgT, kdec, egl, TT = bvP[q_], begkP[q_], qkTP[q_], qgTP[q_], kdecP[q_], eglP[q_], TTf[q_]
        rTT = P_["TTf"]
        wv, wrg = fbank()
        for h in range(4):
            kb.mm(wv[:, h, :], begk[:, h, :], TT[:, h, :], True, True, [P_["begk"], rTT], [wrg])
        kb.act(nwT, wv, AF.Identity, [wrg], [R_["nwT"]], scale=-1.0)
        yield
        uv, urg = fbank()
        for h in range(4):
            kb.mm(uv[:, h, :], TT[:, h, :], bv[:, h, :], True, False, [rTT, P_["bv"]], [urg])
            kb.mm(uv[:, h, :], nwT[:, h, :], Sst[:, h, :], False, True, [R_["nwT"], r_S], [urg])
        kb.cp("dve", vnew, uv, [urg], [R_["vnew"]])
        yield
        ov, org = fbank()
        for h in range(4):
            kb.mm(ov[:, h, :], qgT[:, h, :], Sb[:, h, :], True, False, [P_["qgT"], r_Sb], [org])
            kb.mm(ov[:, h, :], qkT[:, h, :], vnew[:, h, :], False, True, [P_["qkT"], R_["vnew"]], [org])
        sv, srg = fbank()
        for h in range(4):
            kb.mm(sv[:, h, :], kdec[:, h, :], vnew[:, h, :], True, True, [P_["kdec"], R_["vnew"]], [srg])
        kb.tt("dve", Sst, Sst, bc(egl), ALU.mult, [r_S, P_["egl"]], [r_S])
        kb.tt("dve", Sst, Sst, sv, ALU.add, [r_S, srg], [r_S])
        kb.cp("act", Sb, Sst, [r_S], [r_Sb])
        yield
        kb.act(sqt2, ov, AF.Square, [org], [R_["sqt2"]])
        p.op("dve", lambda E: E.tensor_reduce(out=so2, in_=sqt2, axis=AX.X, op=ALU.add), [R_["sqt2"]], [R_["so2"]])
        rsq(so2, so2, 128, 1.0 / 128, [R_["so2"]], [R_["so2"]])
        kb.tt("dve", onb, ov, bc(so2), ALU.mult, [org, R_["so2"]], [R_["onb"]])
        yield
        bv_, brg = bfbank()
        for h in range(4):
            kb.tr(bv_[:, h, :], onb[:, h, :], ident_b, [R_["onb"], r_cm], [brg])
        kb.stt("dve", mixT[:, 0:4, tk:tk + 128], bv_, nwc[:, 0:1], zT[:, 0:4, tk:tk + 128], ALU.mult, ALU.mult,
               [brg, r_gp] + [r_z[m][n // 4] for m in range(4)], [r_mix[n // 4]])
        yield

    for _ in part1(0):
        pass
    for n in range(16):
        ga = part1(n + 1) if n + 1 < 16 else iter(())
        gb = part2(n)
        da = db = False
        for _ in range(5):
            if not da:
                try:
                    next(ga)
                except StopIteration:
                    da = True
        while not (da and db):
            if not db:
                try:
                    next(gb)
                except StopIteration:
                    db = True
            if not da:
                try:
                    next(ga)
                except StopIteration:
                    da = True
    kb.dma("sp", gdnp_d.rearrange("h d e -> d h e"), Sst, [r_S], [Rg()])

    p.barrier()
    A.top = alias_base
    Sall = A.f32(16 * 512).rearrange("p (b h e) -> p b h e", b=16, h=4)
    r_Sall = rgs(16)
    for b in range(NS):
        kb.dma("sp", Sall[:, b], sg_d[b].rearrange("h d e -> d h e"), [], [r_Sall[b]])
    f4s = lambda: A.f32(512).rearrange("p (h e) -> p h e", h=4)
    khs, qhs, vts, kw, qo, vacc, oacc, ksel = f4s(), f4s(), f4s(), f4s(), f4s(), f4s(), f4s(), f4s()
    kwT = A.f32(64).rearrange("p (h b) -> p h b", h=4)
    qoT = A.f32(64).rearrange("p (h b) -> p h b", h=4)
    sms = A.f32(24).rearrange("p (a h) -> p a h", h=4)
    egsel = A.f32(64).rearrange("p (b h) -> p b h", h=4)
    egbc = A.f32(64).rearrange("p (b h) -> p b h", h=4)
    r_s = {k: Rg(k) for k in ["khs", "qhs", "vts", "kw", "qo", "vacc", "oacc", "ksel", "kwT", "qoT", "sms", "egsel",
                              "egbc"]}
    rows, tk = NS, TP
    if GS:
        (qv, qrg), (kv, krg), (vv, vrg) = tok_prep(16, NS, TP)
        rnq, rnk = sml[:NS, 5, :], sml[:NS, 6, :]
        kb.tt("dve", khs[:NS], kv[:NS], bc(rnk, rows=NS), ALU.mult, [krg, r_sml], [r_s["khs"]])
        kb.tt("dve", qhs[:NS], qv[:NS], bc(rnq, rows=NS), ALU.mult, [qrg, r_sml], [r_s["qhs"]])
        kb.cp("dve", vts[:NS], vv[:NS], [vrg], [r_s["vts"]])
        eg, qkd, ckw, cqo, qkb, sos = [sms[:NS, i, :] for i in range(6)]
        bet16, g16 = beta[:NS, 16, :], gg[:NS, 16, :]
        kb.act(eg, g16, AF.Exp, [r_gate], [r_s["sms"]])
        kb.tt("dve", ksel[:NS], qhs[:NS], khs[:NS], ALU.mult, [r_s["qhs"], r_s["khs"]], [r_s["ksel"]])
        p.op("dve", lambda E: E.tensor_reduce(out=qkd, in_=ksel[:NS], axis=AX.X, op=ALU.add), [r_s["ksel"]],
             [r_s["sms"]])
        kb.tt("dve", ckw, bet16, eg, ALU.mult, [r_gate, r_s["sms"]], [r_s["sms"]])
        kb.tt("dve", qkb, qkd, bet16, ALU.mult, [r_gate, r_s["sms"]], [r_s["sms"]])
        kb.tt("dve", cqo, ckw, qkd, ALU.mult, [r_s["sms"]], [r_s["sms"]])
        kb.ts("dve", ckw, ckw, -1.0, None, ALU.mult, None, [r_s["sms"]], [r_s["sms"]])
        kb.tt("dve", kw[:NS], khs[:NS], bc(ckw, rows=NS), ALU.mult, [r_s["khs"], r_s["sms"]], [r_s["kw"]])
        kb.tt("dve", qo[:NS], qhs[:NS], bc(eg, rows=NS), ALU.mult, [r_s["qhs"], r_s["sms"]], [r_s["qo"]])
        kb.tt("dve", ksel[:NS], khs[:NS], bc(cqo, rows=NS), ALU.mult, [r_s["khs"], r_s["sms"]], [r_s["ksel"]])
        kb.tt("dve", qo[:NS], qo[:NS], ksel[:NS], ALU.subtract, [r_s["qo"], r_s["ksel"]], [r_s["qo"]])
        kb.tt("dve", vacc[:NS], vts[:NS], bc(bet16, rows=NS), ALU.mult, [r_s["vts"], r_gate], [r_s["vacc"]])
        kb.tt("dve", oacc[:NS], vts[:NS], bc(qkb, rows=NS), ALU.mult, [r_s["vts"], r_s["sms"]], [r_s["oacc"]])
        bap, brg = kb.bank()
        for h in range(4):
            kb.tr(bap[:, h * 16:(h + 1) * 16], kw[:NS, h, :], ident_f[:NS, :NS], [r_s["kw"], r_ident], [brg])
            kb.tr(bap[:, 64 + h * 16:64 + (h + 1) * 16], qo[:NS, h, :], ident_f[:NS, :NS], [r_s["qo"], r_ident], [brg])
        kwTp, qoTp = Gm, Dm
        kb.memset("dve", kwTp, 0.0, [r_s["kwT"], R_["Gm"]])
        kb.memset("dve", qoTp, 0.0, [r_s["qoT"], R_["Dm"]])
        kb.cp("dve", kwTp[:, :, 0:NS], bap[:, 0:64].rearrange("p (h b) -> p h b", h=4), [brg], [r_s["kwT"]])
        kb.cp("dve", qoTp[:, :, 0:NS], bap[:, 64:128].rearrange("p (h b) -> p h b", h=4), [brg], [r_s["qoT"]])
        kb.tt("dve", egsel[:NS], ident_f[:NS, :NS].unsqueeze(2).to_broadcast([NS, NS, 4]),
              eg.unsqueeze(1).to_broadcast([NS, NS, 4]), ALU.mult, [r_ident, r_s["sms"]], [r_s["egsel"]])
        bap, brg = kb.bank()
        kb.mm(bap[:, 0:64], ones_f[:NS, :], egsel[:NS].rearrange("p b h -> p (b h)"), True, True,
              [r_ones, r_s["egsel"]], [brg])
        kb.cp("dve", egbc, bap[:, 0:64].rearrange("p (b h) -> p b h", h=4), [brg], [r_s["egbc"]])
        for b in range(NS if GS == 1 else 0):
            vv_, vrg_ = fbank()
            ov_, org_ = fbank()
            for h in range(4):
                kb.mm(vv_[:, h, :], kwTp[:, h, :], Sall[:, b, h, :], True, True, [r_s["kwT"], r_Sall[b]], [vrg_])
                kb.mm(ov_[:, h, :], qoTp[:, h, :], Sall[:, b, h, :], True, True, [r_s["qoT"], r_Sall[b]], [org_])
            kb.stt("dve", vacc[:NS], vv_[:NS], ident_f[:NS, b:b + 1], vacc[:NS], ALU.mult, ALU.add,
                   [vrg_, r_ident, r_s["vacc"]], [r_s["vacc"]])
            kb.stt("dve", oacc[:NS], ov_[:NS], ident_f[:NS, b:b + 1], oacc[:NS], ALU.mult, ALU.add,
                   [org_, r_ident, r_s["oacc"]], [r_s["oacc"]])
        ksel2 = [ksel, kw]
        r_ks2 = [r_s["ksel"], r_s["kw"]]
        for b in range(NS if GS in (1, 3) else 0):
            ks_, rks_ = ksel2[b % 2], r_ks2[b % 2]
            kb.ts("dve", ks_[:NS], khs[:NS], ident_f[:NS, b:b + 1], None, ALU.mult, None, [r_s["khs"], r_ident],
                  [rks_])
            sv, srg = fbank()
            for h in range(4):
                kb.mm(sv[:, h, :], ks_[:NS, h, :], vacc[:NS, h, :], True, True, [rks_, r_s["vacc"]], [srg])
            for h in range(4):
                kb.stt("dve", Sall[:, b, h, :], Sall[:, b, h, :], egbc[:, b, h:h + 1], sv[:, h, :], ALU.mult, ALU.add,
                       [r_Sall[b], r_s["egbc"], srg], [r_Sall[b]])
            kb.dma("sp", gdns_d[b].rearrange("h d e -> d h e"), Sall[:, b], [r_Sall[b]], [Rg()])
        kb.act(sqt[:NS], oacc[:NS], AF.Square, [r_s["oacc"]], [R_["sqt"]])
        p.op("dve", lambda E: E.tensor_reduce(out=sos, in_=sqt[:NS], axis=AX.X, op=ALU.add), [R_["sqt"]], [r_s["sms"]])
        rsq(sos, sos, NS, 1.0 / 128, [r_s["sms"]], [r_s["sms"]])
        kb.tt("dve", oacc[:NS], oacc[:NS], bc(sos, rows=NS), ALU.mult, [r_s["oacc"], r_s["sms"]], [r_s["oacc"]])
        bap, brg = kb.bank()
        for h in range(4):
            kb.tr(bap[:, h * 16:(h + 1) * 16], oacc[:NS, h, :], ident_f[:NS, :NS], [r_s["oacc"], r_ident], [brg])
        kb.stt("dve", mixT[:, 0:4, TP:TP + NS], bap[:, 0:64].rearrange("p (h b) -> p h b", h=4), nwc[:, 0:1],
               zT[:, 0:4, TP:TP + NS], ALU.mult, ALU.mult, [brg, r_gp] + [r_z[m][4] for m in range(4)], [r_mix[4]])


    p.barrier()
    A.top = base_c
    A2 = [base_persist]

    def a2f32(cols):
        a = A2[0]
        A2[0] += cols
        assert A2[0] <= base_persist + 12 * NT // 2 + 4 * NT // 2, 'region1 overflow'
        return A.t[:, a:a + cols]
    s5a_d = kb.din("s5a", [128, 3, 16])
    s5bx_d = kb.din("s5bx", [128, 2, 16, 32])
    s5cx_d = kb.din("s5cx", [128, 2, 16, 32])
    s5d_d = kb.din("s5d", [128, 2, 4])
    w_glu = kb.din("w_glu", [512, 512])
    hs_d = kb.din("state_ssm", [NS, 2, 2048])
    ssmp_d = kb.dout("ssm_p", [2, 16, 128])
    ssms_d = kb.dout("ssm_s", [NS, 2, 2048])
    kb.bank_list = [4, 5, 6, 7]
    wglu_sb = A.bf16(4 * 512).rearrange("p (k n) -> p k n", k=4)
    bt_off = A.top
    BT = A.bf16(2 * 16 * 128).rearrange("p (r g n) -> p r g n", r=2, g=16)
    CXf = A.bf16(2 * 16 * 128).rearrange("p (r g n) -> p r g n", r=2, g=16)
    diag_d = A.bf16(4 * 128).rearrange("p (j n) -> p j n", j=4)
    pa = A.f32(48).rearrange("p (a g) -> p a g", a=3)
    s5d = A.f32(8).rearrange("p (a j) -> p a j", a=2)
    hbg = A.f32(4)
    sm = A.f32(16 * 24).rearrange("p (a g) -> p a g", g=16)
    wk = A.f32(11 * 2 * 16).rearrange("p (k c g) -> p k c g", k=11, c=2)
    hl = A.f32(32).rearrange("p (r g) -> p r g", r=2)
    carry = A.f32(2)
    cst = A.f32(2)
    r_wglu, r_BT, r_CX, r_dd, r_pa, r_sm, r_wk, r_hl, r_carry, r_cst = (Rg(), Rg(), Rg(), Rg(), Rg(), Rg(), Rg(),
                                                                      Rg(), Rg(), Rg())
    kb.dma("pool", wglu_sb, w_glu.rearrange("(k p) n -> p k n", p=128), [], [r_wglu])
    kb.dma("sp", pa, s5a_d, [], [r_pa])
    kb.dma("sp", s5d, s5d_d, [], [r_pa])
    kb.memset("pool", cst[:, 0:1], math.pi / 2, [r_cst])
    kb.memset("pool", cst[:, 1:2], 1.0, [r_cst])
    chalf = A.f32(1)
    kb.memset("pool", chalf, 0.5, [r_cst])
    cx_in = A.f32(1024).rearrange("p (r g c) -> p r g c", r=2, g=16)
    bb = A.f32(1024).rearrange("p (r g c) -> p r g c", r=2, g=16)
    lp = A.f32(9 * 2 * 16).rearrange("p (k c g) -> p k c g", k=9, c=2)
    wk8 = A.f32(8 * 2 * 16).rearrange("p (k c g) -> p k c g", k=8, c=2)
    r8 = A.f32(16)
    r_lp = Rg()
    base_loop = A.top
    bx_in = A.f32(1024).rearrange("p (r g c) -> p r g c", r=2, g=16)
    pre = A.f32(4096).rearrange("p (r g n) -> p r g n", r=2, g=16)
    tq = A.f32(512).rearrange("p (g c) -> p g c", g=16)
    r_bx, r_cxin, r_bb, r_pre, r_tq = Rg(), Rg(), Rg(), Rg(), Rg()
    kb.dma("sp", bx_in, s5bx_d, [], [r_bx])
    kb.dma("sp", cx_in, s5cx_d, [], [r_cxin])
    are, aim, ldt = pa[:, 0, :], pa[:, 1, :], pa[:, 2, :]
    S = lambda i: sm[:, i, :]
    rs = [r_sm, r_pa, r_cst]
    dtt, rr_, th, c_, s_, t1_, t2_, lbr, lbi, den, cr, ci = [S(i) for i in range(12)]
    x_, y_ = S(12), S(13)
    kb.act(dtt, ldt, AF.Exp, rs, [r_sm])
    kb.tt("dve", rr_, are, dtt, ALU.mult, rs, [r_sm])
    kb.act(rr_, rr_, AF.Exp, rs, [r_sm])
    kb.tt("dve", th, aim, dtt, ALU.mult, rs, [r_sm])
    kb.act(s_, th, AF.Sin, rs, [r_sm], scale=1.0 / 16)
    kb.act(c_, th, AF.Sin, rs, [r_sm], scale=1.0 / 16, bias=cst[:, 0:1])

    def csq(c, s):
        kb.tt("dve", t1_, c, c, ALU.mult, rs, [r_sm])
        kb.tt("dve", t2_, s, s, ALU.mult, rs, [r_sm])
        kb.stt("dve", s, c, 2.0, s, ALU.mult, ALU.mult, rs, [r_sm])
        kb.tt("dve", c, t1_, t2_, ALU.subtract, rs, [r_sm])
    for _ in range(4):
        csq(c_, s_)
    kb.tt("dve", lbr, rr_, c_, ALU.mult, rs, [r_sm])
    kb.tt("dve", lbi, rr_, s_, ALU.mult, rs, [r_sm])
    kb.ts("dve", x_, lbr, -1.0, None, ALU.add, None, rs, [r_sm])
    kb.tt("dve", t1_, are, are, ALU.mult, rs, [r_sm])
    kb.tt("dve", t2_, aim, aim, ALU.mult, rs, [r_sm])
    kb.tt("dve", den, t1_, t2_, ALU.add, rs, [r_sm])
    p.op("dve", lambda E: E.reciprocal(out=den, in_=den), rs, [r_sm])
    kb.tt("dve", t1_, x_, are, ALU.mult, rs, [r_sm])
    kb.tt("dve", t2_, lbi, aim, ALU.mult, rs, [r_sm])
    kb.tt("dve", cr, t1_, t2_, ALU.add, rs, [r_sm])
    kb.tt("dve", cr, cr, den, ALU.mult, rs, [r_sm])
    kb.tt("dve", t1_, lbi, are, ALU.mult, rs, [r_sm])
    kb.tt("dve", t2_, x_, aim, ALU.mult, rs, [r_sm])
    kb.tt("dve", ci, t1_, t2_, ALU.subtract, rs, [r_sm])
    kb.tt("dve", ci, ci, den, ALU.mult, rs, [r_sm])
    kb.cp("dve", wk[:, 0, 0, :], c_, rs, [r_wk])
    kb.ts("dve", wk[:, 0, 1, :], s_, -1.0, None, ALU.mult, None, rs, [r_wk])
    for k in range(1, 11):
        kb.cp("dve", wk[:, k, 0, :], wk[:, k - 1, 0, :], [r_wk], [r_wk])
        kb.cp("dve", wk[:, k, 1, :], wk[:, k - 1, 1, :], [r_wk], [r_wk])
        c, s = wk[:, k, 0, :], wk[:, k, 1, :]
        kb.tt("dve", t1_, c, c, ALU.mult, [r_wk, r_sm], [r_sm])
        kb.tt("dve", t2_, s, s, ALU.mult, [r_wk, r_sm], [r_sm])
        kb.stt("dve", s, c, 2.0, s, ALU.mult, ALU.mult, [r_wk], [r_wk])
        kb.tt("dve", c, t1_, t2_, ALU.subtract, [r_sm, r_wk], [r_wk])
    kb.memset("pool", lp[:, 0, 0, :], 1.0, [r_lp])
    kb.memset("pool", lp[:, 0, 1, :], 0.0, [r_lp])
    kb.cp("dve", lp[:, 1, 0, :], lbr, rs, [r_lp])
    kb.cp("dve", lp[:, 1, 1, :], lbi, rs, [r_lp])
    for k in range(2, 9):
        pr, pi_ = lp[:, k - 1, 0, :], lp[:, k - 1, 1, :]
        kb.tt("dve", t1_, pr, lbr, ALU.mult, [r_lp, r_sm], [r_sm])
        kb.tt("dve", t2_, pi_, lbi, ALU.mult, [r_lp, r_sm], [r_sm])
        kb.tt("dve", lp[:, k, 0, :], t1_, t2_, ALU.subtract, [r_sm], [r_lp])
        kb.tt("dve", t1_, pr, lbi, ALU.mult, [r_lp, r_sm], [r_sm])
        kb.tt("dve", t2_, pi_, lbr, ALU.mult, [r_lp, r_sm], [r_sm])
        kb.tt("dve", lp[:, k, 1, :], t1_, t2_, ALU.add, [r_sm], [r_lp])
    kb.tt("dve", r8, rr_, rr_, ALU.mult, rs, [r_lp])
    kb.tt("dve", r8, r8, r8, ALU.mult, [r_lp], [r_lp])
    kb.tt("dve", r8, r8, r8, ALU.mult, [r_lp], [r_lp])
    p.op("dve", lambda E: E.reciprocal(out=t1_, in_=r8), [r_lp, r_sm], [r_sm])
    kb.tt("dve", wk8[:, 0, 0, :], lp[:, 8, 0, :], t1_, ALU.mult, [r_lp, r_sm], [r_lp])
    kb.stt("dve", wk8[:, 0, 1, :], lp[:, 8, 1, :], -1.0, t1_, ALU.mult, ALU.mult, [r_lp, r_sm], [r_lp])
    for k in range(1, 8):
        kb.cp("dve", wk8[:, k, 0, :], wk8[:, k - 1, 0, :], [r_lp], [r_lp])
        kb.cp("dve", wk8[:, k, 1, :], wk8[:, k - 1, 1, :], [r_lp], [r_lp])
        c, s_2 = wk8[:, k, 0, :], wk8[:, k, 1, :]
        kb.tt("dve", t1_, c, c, ALU.mult, [r_lp, r_sm], [r_sm])
        kb.tt("dve", t2_, s_2, s_2, ALU.mult, [r_lp, r_sm], [r_sm])
        kb.stt("dve", s_2, c, 2.0, s_2, ALU.mult, ALU.mult, [r_lp], [r_lp])
        kb.tt("dve", c, t1_, t2_, ALU.subtract, [r_sm, r_lp], [r_lp])
    crb = cr.unsqueeze(2).to_broadcast([128, 16, 32])
    cib = ci.unsqueeze(2).to_broadcast([128, 16, 32])
    kb.tt("dve", bb[:, 0], bx_in[:, 0], crb, ALU.mult, [r_bx, r_sm], [r_bb])
    kb.tt("dve", tq, bx_in[:, 1], cib, ALU.mult, [r_bx, r_sm], [r_tq])
    kb.tt("dve", bb[:, 0], bb[:, 0], tq, ALU.subtract, [r_bb, r_tq], [r_bb])
    kb.tt("dve", bb[:, 1], bx_in[:, 1], crb, ALU.mult, [r_bx, r_sm], [r_bb])
    kb.tt("dve", tq, bx_in[:, 0], cib, ALU.mult, [r_bx, r_sm, r_bb], [r_tq])
    kb.tt("dve", bb[:, 1], bb[:, 1], tq, ALU.add, [r_bb, r_tq], [r_bb])
    kb.memset("dve", pre, 0.0, [r_pre])
    kb.memset("dve", CXf, 0.0, [r_CX])
    for ri in range(2):
        pre5 = pre[:, ri].rearrange("p (j l) n -> p j l n", l=4)
        bb5 = bb[:, ri].rearrange("p (j l) c -> p j l c", l=4)
        cxf5 = CXf[:, ri].rearrange("p (j l) n -> p j l n", l=4)
        cx5 = cx_in[:, ri].rearrange("p (j l) c -> p j l c", l=4)
        for l in range(4):
            kb.cp("dve", pre5[:, :, l, 32 * l:32 * l + 32], bb5[:, :, l, :], [r_bb], [r_pre])
            kb.ts("dve", cxf5[:, :, l, 32 * l:32 * l + 32], cx5[:, :, l, :], (1.0 if ri == 0 else -1.0), None,
                  ALU.mult, None, [r_cxin], [r_CX])
    for ri in range(2):
        for g4 in range(4):
            bap, brg = kb.bank()
            for l in range(4):
                kb.tr(bap[:, l * 128:(l + 1) * 128], pre[:, ri, 4 * g4 + l, :], ident_f, [r_pre, r_ident], [brg])
            kb.cp("act" if g4 % 2 else "dve", BT[:, ri, 4 * g4:4 * g4 + 4, :],
                  bap.rearrange("p (l n) -> p l n", l=4), [brg], [r_BT])
    for j in range(4):
        kb.ts("dve", diag_d[:, j, :], ident_f, s5d[:, 0, j:j + 1], None, ALU.mult, None, [r_ident, r_pa], [r_dd])
    kb.ts("dve", hbg, s5d[:, 1, :], 0.5, None, ALU.mult, None, [r_pa], [r_pa])

    GC0, GC1, GC2 = 0.044715, 0.7978845608028654, 0.5

    def gelu_chain(bap, brg, N, dst, rdst, xg, tg, rxg, rtg):
        kb.cp("act", xg[:, :N], bap[:, :N], [brg], [rxg])
        kb.tt("dve", tg[:, :N], xg[:, :N], xg[:, :N], ALU.mult, [rxg], [rtg])
        kb.ts("dve", tg[:, :N], tg[:, :N], GC0, 1.0, ALU.mult, ALU.add, [rtg], [rtg])
        kb.tt("dve", tg[:, :N], tg[:, :N], xg[:, :N], ALU.mult, [rtg, rxg], [rtg])
        kb.act(tg[:, :N], tg[:, :N], AF.Tanh, [rtg], [rtg], scale=GC1)
        kb.ts("dve", tg[:, :N], tg[:, :N], 0.5, 0.5, ALU.mult, ALU.add, [rtg], [rtg])
        kb.tt("dve", dst, tg[:, :N], xg[:, :N], ALU.mult, [rtg, rxg], rdst)

    p.barrier()
    A.top = base_loop
    hs_tok = A.f32(2048)
    h0T = A.f32(512).rearrange("p (r g b) -> p r g b", r=2, g=16)
    lh = A.f32(512).rearrange("p (r g b) -> p r g b", r=2, g=16)
    hn = A.f32(512).rearrange("p (r g b) -> p r g b", r=2, g=16)
    hnb = A.bf16(512).rearrange("p (r g b) -> p r g b", r=2, g=16)
    tq2 = A.f32(256).rearrange("p (g b) -> p g b", g=16)
    so_tok = A.f32(2048)
    xg, tg = A.f32(512), A.f32(512)
    hlT = A.f32(256).rearrange("p (r n) -> p r n", r=2)
    r_hs, r_h0T, r_lh, r_hn, r_hnb, r_tq2, r_so, r_xg, r_tg, r_hlT = (Rg(), Rg(), Rg(), Rg(), Rg(), Rg(), Rg(), Rg(),
                                                                      Rg(), Rg())
    for ri in range(2):
        kb.dma("sp", hs_tok[:NS, :], hs_d[:, ri, :], [], [r_hs])
        bap, brg = kb.bank()
        for rg in range(16):
            kb.tr(bap[:, rg * 16:(rg + 1) * 16], hs_tok[:NS, rg * 128:(rg + 1) * 128], ident_f[:NS, :NS],
                  [r_hs, r_ident], [brg])
        kb.cp("dve", h0T[:, ri], bap[:, 0:256].rearrange("p (g b) -> p g b", g=16), [brg], [r_h0T])
    lbrb = lbr.unsqueeze(2).to_broadcast([128, 16, 16])
    lbib = lbi.unsqueeze(2).to_broadcast([128, 16, 16])
    kb.tt("dve", lh[:, 0], h0T[:, 0], lbrb, ALU.mult, [r_h0T, r_sm], [r_lh])
    kb.tt("dve", tq2, h0T[:, 1], lbib, ALU.mult, [r_h0T, r_sm], [r_tq2])
    kb.tt("dve", lh[:, 0], lh[:, 0], tq2, ALU.subtract, [r_lh, r_tq2], [r_lh])
    kb.tt("dve", lh[:, 1], h0T[:, 1], lbrb, ALU.mult, [r_h0T, r_sm], [r_lh])
    kb.tt("dve", tq2, h0T[:, 0], lbib, ALU.mult, [r_h0T, r_sm, r_lh], [r_tq2])
    kb.tt("dve", lh[:, 1], lh[:, 1], tq2, ALU.add, [r_lh, r_tq2], [r_lh])
    for ri in range(2):
        bap, brg = kb.bank()
        for rg in range(16):
            kb.mm(bap[:, rg * 16:(rg + 1) * 16], BT[:, ri, rg, :], usamp[:, rg // 4, :], True, True,
                  [r_BT, r_u[rg // 4][4]], [brg])
        kb.tt("dve", hn[:, ri], lh[:, ri], bap[:, 0:256].rearrange("p (g b) -> p g b", g=16), ALU.add,
              [brg, r_lh], [r_hn])
    kb.cp("dve", hnb, hn, [r_hn], [r_hnb])
    for j in range(4):
        bap, brg = kb.bank()
        for l in range(4):
            rg = 4 * j + l
            kb.mm(bap[:, :NS], CXf[:, 0, rg, :], hnb[:, 0, rg, :], l == 0, False, [r_CX, r_hnb], [brg])
            kb.mm(bap[:, :NS], CXf[:, 1, rg, :], hnb[:, 1, rg, :], False, False, [r_CX, r_hnb], [brg])
        kb.mm(bap[:, :NS], diag_d[:, j, :], usamp[:, j, :], False, True, [r_dd, r_u[j][4]], [brg])
        gelu_chain(bap, brg, NS, usamp[:, j, :], [r_u[j][4]], xg, tg, r_xg, r_tg)
    for ri in range(2):
        for g4 in range(4):
            bap, brg = kb.bank()
            for l in range(4):
                kb.tr(bap[:NS, l * 128:(l + 1) * 128], hn[:, ri, 4 * g4 + l, :], ident_f, [r_hn, r_ident], [brg])
            kb.cp("act", so_tok[:NS, g4 * 512:(g4 + 1) * 512], bap[:NS, :], [brg], [r_so])
        kb.dma("sp", ssms_d[:, ri, :], so_tok[:NS, :], [r_so], [Rg()])

    p.barrier()
    A.top = base_loop
    PBT = A.t[:, bt_off:bt_off + 4096].bitcast(BF16).rearrange("p (l s r n) -> p l s r n", l=4, s=8, r=2)
    QCs = [A.bf16(4 * 9 * 2 * 128).rearrange("p (l t r n) -> p l t r n", l=4, t=9, r=2),
           a2f32(4608).bitcast(BF16).rearrange("p (l t r n) -> p l t r n", l=4, t=9, r=2)]
    PBb = a2f32(512).bitcast(BF16).rearrange("p (r l n) -> p r l n", r=2, l=4)
    r_PBb = Rg()
    Kbds = [A.bf16(8 * 128).rearrange("p (t n) -> p t n", t=8),
            a2f32(512).bitcast(BF16).rearrange("p (t n) -> p t n", t=8)]
    Qall = a2f32(9 * 2 * 4 * 32).rearrange("p (k r l n) -> p k r l n", k=9, r=2, l=4)
    Pall = a2f32(4 * 2 * 4 * 128).rearrange("p (k r l n) -> p k r l n", k=4, r=2, l=4)
    prod = a2f32(4 * 5 * 128).rearrange("p (a k l c) -> p a k l c", a=4, k=5, l=4)
    Zb = A.f32(2048).rearrange("p (r l c) -> p r l c", r=2, l=4)
    Eb = A.f32(2048).rearrange("p (r l c) -> p r l c", r=2, l=4)
    Wb = A.f32(2048).rearrange("p (r l c) -> p r l c", r=2, l=4)
    tmpb = A.f32(1024).rearrange("p (l c) -> p l c", l=4)
    tmp2 = A.f32(512).rearrange("p (l c) -> p l c", l=4)
    Hb = A.bf16(2 * 4 * 257).rearrange("p (r l c) -> p r l c", r=2, l=4)
    xg, tg = tmpb.rearrange("p l c -> p (l c)")[:, 0:512], tmpb.rearrange("p l c -> p (l c)")[:, 512:1024]
    r_PBT, r_Qall, r_Pall, r_prod = Rg(), Rg(), Rg(), Rg()
    r_QCs, r_Ks = rgs(2), rgs(2)
    r_Z, r_E, r_W, r_tmp, r_tmp2, r_Hb = Rg(), Rg(), Rg(), Rg(), Rg(), Rg()
    r_xg = r_tg = r_tmp
    for q_ in range(2):
        kb.memset("dve", QCs[q_], 0.0, [r_QCs[q_]])
    kb.memset("dve", Pall, 0.0, [r_Pall])
    kb.memset("dve", Hb, 0.0, [r_Hb])
    evk = [0]

    def cmul_batch(j, k0, nk, src, rsrc, dst, rdst, neg_im, pad):
        lr = lp[:, k0:k0 + nk, 0, 4 * j:4 * j + 4].unsqueeze(3).to_broadcast([128, nk, 4, 32])
        li = lp[:, k0:k0 + nk, 1, 4 * j:4 * j + 4].unsqueeze(3).to_broadcast([128, nk, 4, 32])
        sr = src[:, 0, 4 * j:4 * j + 4, :].unsqueeze(1).to_broadcast([128, nk, 4, 32])
        si = src[:, 1, 4 * j:4 * j + 4, :].unsqueeze(1).to_broadcast([128, nk, 4, 32])
        P0, P1, P2, P3 = [prod[:, a, 0:nk] for a in range(4)]
        kb.tt("dve", P0, sr, lr, ALU.mult, [rsrc, r_lp], [r_prod])
        kb.tt("dve", P1, si, li, ALU.mult, [rsrc, r_lp], [r_prod])
        kb.tt("dve", P2, si, lr, ALU.mult, [rsrc, r_lp], [r_prod])
        kb.tt("dve", P3, sr, li, ALU.mult, [rsrc, r_lp], [r_prod])
        if not pad:
            kb.tt("dve", dst[:, 0:nk, 0], P0, P1, ALU.subtract, [r_prod], [rdst])
            if neg_im:
                kb.stt("dve", dst[:, 0:nk, 1], P2, -1.0, P3, ALU.mult, ALU.subtract, [r_prod], [rdst])
            else:
                kb.tt("dve", dst[:, 0:nk, 1], P2, P3, ALU.add, [r_prod], [rdst])
            return
        for l in range(4):
            kb.tt("dve", dst[:, 0:nk, 0, l, 32 * l:32 * l + 32], P0[:, :, l, :], P1[:, :, l, :], ALU.subtract,
                  [r_prod], [rdst])
            if neg_im:
                kb.stt("dve", dst[:, 0:nk, 1, l, 32 * l:32 * l + 32], P2[:, :, l, :], -1.0, P3[:, :, l, :],
                       ALU.mult, ALU.subtract, [r_prod], [rdst])
            else:
                kb.tt("dve", dst[:, 0:nk, 1, l, 32 * l:32 * l + 32], P2[:, :, l, :], P3[:, :, l, :], ALU.add,
                      [r_prod], [rdst])

    def s5_setup(j):
        q_ = j % 2
        QC, Kbd, r_QC, r_K = QCs[q_], Kbds[q_], r_QCs[q_], r_Ks[q_]
        cmul_batch(j, 0, 5, cx_in, r_cxin, Qall, r_Qall, True, False)
        yield
        cmul_batch(j, 5, 4, cx_in, r_cxin, Qall[:, 5:9], r_Qall, True, False)
        yield
        for l in range(4):
            for ri in range(2):
                kb.cp("act", QC[:, l, 0:8, ri, 32 * l:32 * l + 32], Qall[:, 1:9, ri, l, :], [r_Qall], [r_QC])
                kb.cp("act", QC[:, l, 8, ri, 32 * l:32 * l + 32], Qall[:, 0, ri, l, :], [r_Qall], [r_QC])
        yield
        for half in (1, 0):
            cmul_batch(j, 4 * half, 4, bb, r_bb, Pall, r_Pall, False, True)
            yield
            for kk in range(4):
                sidx = 7 - (4 * half + kk)
                for ri in range(2):
                    bap, brg = kb.bank()
                    for l in range(4):
                        kb.tr(bap[:, l * 128:(l + 1) * 128], Pall[:, kk, ri, l, :], ident_f, [r_Pall, r_ident],
                              [brg])
                    evk[0] += 1
                    kb.cp("act", PBT[:, :, sidx, ri, :],
                          bap.rearrange("p (l n) -> p l n", l=4), [brg], [r_PBT])
                if kk % 2 == 1:
                    yield
        kb.cp("dve", PBb, Pall[:, 0], [r_Pall], [r_PBb])
        for k4 in range(2):
            bap, brg = kb.bank()
            for kk in range(4):
                k = 4 * k4 + kk
                n8 = 0
                for l in range(4):
                    for ri in range(2):
                        kb.mm(bap[:, kk * 128:(kk + 1) * 128], PBb[:, ri, l, :], QC[:, l, (k - 1) % 9, ri, :],
                              n8 == 0, n8 == 7, [r_PBb, r_QC], [brg])
                        n8 += 1
            kb.cp("act", Kbd[:, 4 * k4:4 * k4 + 4, :], bap.rearrange("p (k n) -> p k n", k=4), [brg], [r_K])
        kb.tt("dve", Kbd[:, 0, :], Kbd[:, 0, :], diag_d[:, j, :], ALU.add, [r_K, r_dd], [r_K])
        yield

    def s5_main(j):
        q_ = j % 2
        QC, Kbd, r_QC, r_K = QCs[q_], Kbds[q_], r_QCs[q_], r_Ks[q_]
        uj = uperm[:, j]
        ru = [r_u[j][tb] for tb in range(4)]
        for l in range(4):
            for ri in range(2):
                bap, brg = kb.bank()
                for sidx in range(8):
                    kb.mm(bap[:, 0:256], PBT[:, l, sidx, ri, :], uj[:, sidx, :], sidx == 0, sidx == 7, [r_PBT] + ru,
                          [brg])
                evk[0] += 1
                kb.cp("act", Zb[:, ri, l, :], bap[:, 0:256], [brg], [r_Z])
        yield
        Er, Ei = Eb[:, 0], Eb[:, 1]
        kb.cp("dve", Er[:, :, 0:1], wk8[:, 0, 0, 4 * j:4 * j + 4].unsqueeze(2), [r_lp], [r_E])
        kb.cp("dve", Ei[:, :, 0:1], wk8[:, 0, 1, 4 * j:4 * j + 4].unsqueeze(2), [r_lp], [r_E])
        for k in range(8):
            n = 1 << k
            ck = wk8[:, k, 0, 4 * j:4 * j + 4].unsqueeze(2).to_broadcast([128, 4, n])
            sk = wk8[:, k, 1, 4 * j:4 * j + 4].unsqueeze(2).to_broadcast([128, 4, n])
            kb.tt("dve", tmpb[:, :, 0:n], Ei[:, :, 0:n], sk, ALU.mult, [r_E, r_lp], [r_tmp])
            kb.tt("dve", tmp2[:, :, 0:n], Er[:, :, 0:n], ck, ALU.mult, [r_E, r_lp], [r_tmp2])
            kb.tt("dve", Er[:, :, n:2 * n], tmp2[:, :, 0:n], tmpb[:, :, 0:n], ALU.subtract, [r_tmp, r_tmp2], [r_E])
            kb.tt("dve", tmpb[:, :, 0:n], Ei[:, :, 0:n], ck, ALU.mult, [r_E, r_lp], [r_tmp])
            kb.tt("dve", tmp2[:, :, 0:n], Er[:, :, 0:n], sk, ALU.mult, [r_E, r_lp], [r_tmp2])
            kb.tt("dve", Ei[:, :, n:2 * n], tmp2[:, :, 0:n], tmpb[:, :, 0:n], ALU.add, [r_tmp, r_tmp2], [r_E])
            if k % 2 == 1:
                yield
        kb.tt("dve", Wb[:, 0], Er, Zb[:, 0], ALU.mult, [r_E, r_Z], [r_W])
        kb.tt("dve", tmpb, Ei, Zb[:, 1], ALU.mult, [r_E, r_Z], [r_tmp])
        kb.tt("dve", Wb[:, 0], Wb[:, 0], tmpb, ALU.subtract, [r_W, r_tmp], [r_W])
        kb.tt("dve", Wb[:, 1], Er, Zb[:, 1], ALU.mult, [r_E, r_Z], [r_W])
        kb.tt("dve", tmpb, Ei, Zb[:, 0], ALU.mult, [r_E, r_Z], [r_tmp])
        kb.tt("dve", Wb[:, 1], Wb[:, 1], tmpb, ALU.add, [r_W, r_tmp], [r_W])
        yield
        for ri in range(2):
            for l in range(4):
                rg = 4 * j + l
                rb = r8[:, rg:rg + 1].to_broadcast([128, 256])
                p.op("dve", lambda E, ri=ri, l=l, rb=rb: E.tensor_tensor_scan(
                    out=Zb[:, ri, l, :], data0=rb, data1=Wb[:, ri, l, :], initial=0.0, op0=ALU.mult, op1=ALU.add),
                    [r_W, r_lp, r_Z], [r_Z])
        yield
        Gr, Gi = Zb[:, 0], Zb[:, 1]
        kb.tt("dve", Wb[:, 0], Er, Gr, ALU.mult, [r_E, r_Z, r_W], [r_W])
        kb.tt("dve", tmpb, Ei, Gi, ALU.mult, [r_E, r_Z], [r_tmp])
        kb.tt("dve", Hb[:, 0, :, 1:257], Wb[:, 0], tmpb, ALU.add, [r_W, r_tmp], [r_Hb])
        kb.tt("dve", hl[:, 0, 4 * j:4 * j + 4].unsqueeze(2), Wb[:, 0, :, 255:256], tmpb[:, :, 255:256], ALU.add,
              [r_W, r_tmp], [r_hl])
        kb.tt("dve", Wb[:, 1], Er, Gi, ALU.mult, [r_E, r_Z, r_W], [r_W])
        kb.tt("dve", tmpb, Ei, Gr, ALU.mult, [r_E, r_Z, r_Hb], [r_tmp])
        kb.tt("dve", Hb[:, 1, :, 1:257], Wb[:, 1], tmpb, ALU.subtract, [r_W, r_tmp], [r_Hb])
        kb.tt("dve", hl[:, 1, 4 * j:4 * j + 4].unsqueeze(2), Wb[:, 1, :, 255:256], tmpb[:, :, 255:256], ALU.subtract,
              [r_W, r_tmp], [r_hl])
        yield
        for tl in range(7, -1, -1):
            bap, brg = kb.fixed_bank(tl // 2)
            yo = bap[:, (tl % 2) * 256:(tl % 2) * 256 + 256]
            for sidx in range(tl + 1):
                kb.mm(yo, Kbd[:, tl - sidx, :], uj[:, sidx, :], sidx == 0, False, [r_K] + ru, [brg])
            for l in range(4):
                for ri in range(2):
                    kb.mm(yo, QC[:, l, tl, ri, :], Hb[:, ri, l, 0:256], False, (l == 3 and ri == 1), [r_QC, r_Hb],
                          [brg])
            if tl % 2 == 0:
                b_ = tl // 2
                dst = uj[:, 2 * b_:2 * b_ + 2, :]
                kb.act(xg, bap, AF.Identity, [brg], [r_xg], scale=0.5)
                kb.act(tg, xg, AF.Square, [r_xg], [r_tg])
                kb.act(tg, tg, AF.Identity, [r_tg, r_cst], [r_tg], scale=4.0 * GC0, bias=cst[:, 1:2])
                kb.tt("dve", tg, tg, xg, ALU.mult, [r_tg, r_xg], [r_tg])
                kb.act(tg, tg, AF.Tanh, [r_tg], [r_tg], scale=2.0 * GC1)
                kb.stt("dve", dst, tg.rearrange("p (s c) -> p s c", s=2), 1.0, xg.rearrange("p (s c) -> p s c", s=2),
                       ALU.add, ALU.mult, [r_tg, r_xg], ru)
                yield

    for _ in s5_setup(0):
        pass
    for j in range(4):
        ga = s5_setup(j + 1) if j + 1 < 4 else iter(())
        gb = s5_main(j)
        da = db = False
        while not (da and db):
            if not db:
                try:
                    next(gb)
                except StopIteration:
                    db = True
            if not da:
                try:
                    next(ga)
                except StopIteration:
                    da = True
    bap, brg = kb.bank()
    for ri in range(2):
        kb.tr(bap[:16, ri * 128:(ri + 1) * 128], hl[:, ri, :], ident_f, [r_hl, r_ident], [brg])
    kb.cp("act", hlT[:16].rearrange("p r n -> p (r n)"), bap[:16, 0:256], [brg], [r_hlT])
    for ri in range(2):
        kb.dma("sp", ssmp_d[ri], hlT[:16, ri, :], [r_hlT], [Rg()])
    ru_all = lambda k: [r_u[k][tb] for tb in range(4)]
    for pb in range(5):
        N = 512 if pb < 4 else NS

        def ysl(k, pb=pb):
            if pb < 4:
                return uperm[:, k, 2 * pb:2 * pb + 2, :].rearrange("p s c -> p (s c)")
            return usamp[:, k, :]
        for e in range(4):
            bap, brg = kb.bank()
            for kc in range(4):
                kb.mm(bap[:, :N], wglu_sb[:, kc, e * 128:(e + 1) * 128], ysl(kc), kc == 0, kc == 3,
                      [r_wglu] + (ru_all(kc) if pb < 4 else [r_u[kc][4]]), [brg])
            kb.act(tg[:, :N], bap[:, :N], AF.Tanh, [brg, r_pa], [r_tg], scale=0.5, bias=hbg[:, e:e + 1])
            kb.act(tg[:, :N], tg[:, :N], AF.Identity, [r_tg, r_cst], [r_tg], scale=0.5, bias=chalf)
            if pb < 4:
                dst = mixT[:, 4 + e, 0:TP].rearrange("p (c s) -> p s c", s=8)[:, 2 * pb:2 * pb + 2, :]
                kb.tt("dve", dst, tg.rearrange("p (s c) -> p s c", s=2), uperm[:, e, 2 * pb:2 * pb + 2, :], ALU.mult,
                      [r_tg] + ru_all(e), r_mix[0:4])
            else:
                kb.tt("dve", mixT[:, 4 + e, TP:TP + NS], tg[:, :N], usamp[:, e, :], ALU.mult, [r_tg, r_u[e][4]],
                      [r_mix[4]])
    kb.bank_list = None

    p.barrier()
    A.top = base_persist
    w_out_sb = A.bf16(8 * D).rearrange("p (k n) -> p k n", k=8)
    w2_sb = A.bf16(32 * D).rearrange("p (k n) -> p k n", k=32)
    assert A.top <= base_ab
    A.top = base_c
    gb = A.f32(4096).rearrange("p (a d) -> p a d", a=4)
    for i in range(4):
        kb.dma("sp", gb[:, i, :], ln_d[i:i + 1, :].partition_broadcast(128), [], [r_gb])
    xtc = [A.f32(1024) for _ in range(2)]
    x1 = [A.f32(1024) for _ in range(4)]
    rr = [A.f32(1024) for _ in range(2)]
    x1T = A.bf16(8 * 256).rearrange("p (k t) -> p k t", k=8)
    hT = A.bf16(32 * 256).rearrange("p (k t) -> p k t", k=32)
    w1b = [A.bf16(8 * 512).rearrange("p (k n) -> p k n", k=8) for _ in range(2)]
    rtmp = [A.bf16(256), A.bf16(256)]
    stats = A.f32(64)
    r_wout, r_w2 = rgs(8), rgs(32)
    r_xtc, r_x1, r_rr, r_w1b = rgs(2), rgs(4), rgs(2), rgs(2)
    r_x1T, r_hT, r_rtmp, r_stats = [rgs(2), rgs(2)], rgs(32), rgs(2), rgs(2)
    kb.dma("sp", w_out_sb, wos.rearrange("(k p) n -> p k n", p=128), [r_wos], r_wout)
    for g in range(8):
        kb.dma("sp", w2_sb[:, 4 * g:4 * g + 4, :],
               w2s[g * 512:(g + 1) * 512, :].rearrange("(k p) n -> p k n", p=128), [r_w2s[g]], r_w2[4 * g:4 * g + 4])

    def layernorm(src, rows, gi, dst, rsrc, rdst, sidx):
        stt_ = stats[:, sidx * 32:(sidx + 1) * 32]
        bst = stt_[:, 0:12].rearrange("p (c s) -> p c s", c=2)
        mv = stt_[:, 12:14]
        rstd = stt_[:, 14:15]
        nmr = stt_[:, 15:16]
        rs = r_stats[sidx]
        for c in range(2):
            p.op("dve", lambda E, c=c: E.bn_stats(out=bst[:rows, c, :], in_=src[:rows, c * 512:(c + 1) * 512]),
                 [rsrc], [rs])
        p.op("dve", lambda E: E.bn_aggr(out=mv[:rows, :], in_=stt_[:rows, 0:12]), [rs], [rs])
        kb.act(rstd[:rows, :], mv[:rows, 1:2], AF.Sqrt, [rs], [rs], bias=epsc[:rows, :], scale=1.0)
        p.op("dve", lambda E: E.reciprocal(out=rstd[:rows, :], in_=rstd[:rows, :]), [rs], [rs])
        kb.stt("dve", nmr[:rows, :], mv[:rows, 0:1], -1.0, rstd[:rows, :], ALU.mult, ALU.mult, [rs], [rs])
        kb.act(src[:rows, :], src[:rows, :], AF.Identity, [rsrc, rs], [rsrc], bias=nmr[:rows, :],
               scale=rstd[:rows, :])
        kb.tt("dve", src[:rows, :], src[:rows, :], gb[:rows, gi, :], ALU.mult, [rsrc, r_gb], [rsrc])
        kb.tt("dve", dst[:rows, :], src[:rows, :], gb[:rows, gi + 1, :], ALU.add, [rsrc, r_gb], [rdst])

    blocks = []
    tile_i = 0
    for cb, (t0, N) in enumerate(CBLK):
        tiles = []
        for tt_ in range((N + 127) // 128):
            tiles.append(dict(j=tile_i % 2, jx=tile_i % 4, rows=min(128, N - tt_ * 128), tk=t0 + tt_ * 128, tt=tt_))
            tile_i += 1
        blocks.append(dict(N=N, tiles=tiles))
    w1i = [0]

    def front_a(blk):
        for T in blk["tiles"]:
            j, jx, rows, tk = T["j"], T["jx"], T["rows"], T["tk"]
            kb.dma("sp", xtc[j][:rows, :], xs[tk:tk + rows, :], [], [r_xtc[j]])
            for hh in range(2):
                bap, brg = kb.bank()
                for kc in range(8):
                    kb.mm(bap[:rows, :], mixT[:, kc, tk:tk + rows], w_out_sb[:, kc, hh * 512:(hh + 1) * 512],
                          kc == 0, kc == 7, r_mix + [r_wout[kc]], [brg])
                kb.stt("dve", rr[j][:rows, hh * 512:(hh + 1) * 512], xtc[j][:rows, hh * 512:(hh + 1) * 512], ALPHA,
                       bap[:rows, :], ALU.mult, ALU.add, [brg, r_xtc[j]], [r_rr[j]])
            layernorm(rr[j], rows, 0, x1[jx], r_rr[j], r_x1[jx], j)

    def front_b(blk):
        for T in blk["tiles"]:
            jx, rows, tt_ = T["jx"], T["rows"], T["tt"]
            bks = [kb.bank(), kb.bank()]
            for kc in range(8):
                bap, brg = bks[kc // 4]
                kb.tr(bap[:, (kc % 4) * 128:(kc % 4) * 128 + rows], x1[jx][:rows, kc * 128:(kc + 1) * 128],
                      ident_f[:rows, :rows], [r_x1[jx], r_ident], [brg])
            for h in range(2):
                bap, brg = bks[h]
                src = bap.rearrange("p (k t) -> p k t", k=4)[:, :, :rows]
                dst = x1T[:, 4 * h:4 * h + 4, tt_ * 128:tt_ * 128 + rows]
                kb.cp("act" if h == 0 else "dve", dst, src, [brg], [r_x1T[tt_][h]])

    def ffn1(blk):
        N = blk["N"]
        rd_x1T = [r for q in r_x1T[:len(blk["tiles"])] for r in q]
        for g in range(8):
            wb, rwb = w1b[w1i[0] % 2], r_w1b[w1i[0] % 2]
            w1i[0] += 1
            kb.dma("sp", wb, w1s[:, g * 512:(g + 1) * 512].rearrange("(k p) n -> p k n", p=128), r_w1s, [rwb])
            for f4 in range(4):
                fc = 4 * g + f4
                bap, brg = kb.bank()
                for kc in range(8):
                    kb.mm(bap[:, :N], wb[:, kc, f4 * 128:(f4 + 1) * 128], x1T[:, kc, :N], kc == 0, kc == 7,
                          rd_x1T + [rwb], [brg])
                rt_, rrt_ = rtmp[fc % 2], r_rtmp[fc % 2]
                kb.act(rt_[:, :N], bap[:, :N], AF.Relu, [brg], [rrt_])
                kb.tt("dve", hT[:, fc, :N], rt_[:, :N], rt_[:, :N], ALU.mult, [rrt_], [r_hT[fc]])

    def back(blk):
        for T in blk["tiles"]:
            j, jx, rows, tk, tt_ = T["j"], T["jx"], T["rows"], T["tk"], T["tt"]
            for hh in range(2):
                bap, brg = kb.bank()
                for fc in range(32):
                    kb.mm(bap[:rows, :], hT[:, fc, tt_ * 128:tt_ * 128 + rows], w2_sb[:, fc, hh * 512:(hh + 1) * 512],
                          fc == 0, fc == 31, [r_hT[fc], r_w2[fc]], [brg])
                kb.stt("dve", rr[j][:rows, hh * 512:(hh + 1) * 512], x1[jx][:rows, hh * 512:(hh + 1) * 512], ALPHA,
                       bap[:rows, :], ALU.mult, ALU.add, [brg, r_x1[jx]], [r_rr[j]])
            layernorm(rr[j], rows, 2, x1[jx], r_rr[j], r_x1[jx], j)
            kb.dma("pool", y_d[tk:tk + rows, :], x1[jx][:rows, :], [r_x1[jx]], [Rg()])

    front_a(blocks[0])
    front_b(blocks[0])
    for cb, blk in enumerate(blocks):
        ffn1(blk)
        if cb + 1 < len(blocks):
            front_a(blocks[cb + 1])
        back(blk)
        if cb + 1 < len(blocks):
            front_b(blocks[cb + 1])

    p.emit(nc, st)
    return nc, kb, st


_CACHE = {}


def _get_nc(debug=False):
    key = ("nc", debug)
    if key not in _CACHE:
        _CACHE[key] = build(debug=debug)
    return _CACHE[key]


def host_inputs(inp, c):
    f = np.float32
    xs = np.concatenate([inp["x_prompt"][c], inp["x_sample"][NS * c:NS * (c + 1), 0, :]], axis=0)
    m = {
        "xs": np.ascontiguousarray(xs, dtype=f),
        "w_in": np.ascontiguousarray(inp["w_in"][0], dtype=f),
        "w_out": np.ascontiguousarray(inp["w_out"][0], dtype=f),
        "w_ff1": np.ascontiguousarray(inp["w_ff1"][0], dtype=f),
        "w_ff2": np.ascontiguousarray(inp["w_ff2"][0], dtype=f),
        "wc": np.ascontiguousarray(inp["w_conv"][0].reshape(4, 12, 128).transpose(2, 1, 0), dtype=f),
        "ln": np.ascontiguousarray(np.stack([inp["ln1_g"][0], inp["ln1_b"][0], inp["ln2_g"][0], inp["ln2_b"][0]]),
                                   dtype=f),
        "ident": np.eye(128, dtype=f),
        "state_conv": np.ascontiguousarray(inp["state_conv"][0, NS * c:NS * (c + 1)], dtype=f),
        "w_glu": np.ascontiguousarray(inp["w_glu"][0], dtype=f),
    }

    def sl(a):
        return a.reshape(16, 2, 64).transpose(1, 2, 0).reshape(128, 16)
    m["s5a"] = np.ascontiguousarray(np.stack([sl(inp["s5_a_re"][0]), sl(inp["s5_a_im"][0]),
                                              sl(np.repeat(inp["s5_log_dt"][0][:, None], 64, axis=1))], axis=1),
                                    dtype=f)

    def padx(a):
        a = a.reshape(16, 2, 64, 16)
        o = np.zeros((2, 64, 16, 2, 16), f)
        for h in range(2):
            o[h, :, :, h, :] = a[:, h].transpose(1, 0, 2)
        return o.reshape(128, 16, 32)
    m["s5bx"] = np.ascontiguousarray(np.stack([padx(inp["s5_b_re"][0]), padx(inp["s5_b_im"][0])], axis=1))
    m["s5cx"] = np.ascontiguousarray(np.stack([padx(inp["s5_c_re"][0].transpose(0, 2, 1)),
                                               padx(inp["s5_c_im"][0].transpose(0, 2, 1))], axis=1))
    m["s5d"] = np.ascontiguousarray(np.stack([inp["s5_d"][0].reshape(4, 128).T, inp["b_glu"][0].reshape(4, 128).T]),
                                    dtype=f).transpose(1, 0, 2).copy()
    tri = np.triu(np.ones((128, 128), f))
    neg = np.float32(-1e30)
    maskL = np.where(np.tril(np.ones((128, 128), bool)), f(0), neg).astype(f)
    m["gconst"] = np.ascontiguousarray(np.stack([tri, maskL, maskL.T, np.tril(np.ones((128, 128), f), -1)], axis=1))
    m["gdn_par"] = np.ascontiguousarray(np.stack([inp["gdn_a_log"][0], inp["gdn_dt_bias"][0], np.zeros(4, f)]),
                                        dtype=f)
    m["gdn_nw"] = np.ascontiguousarray(inp["gdn_norm_w"][0].reshape(128, 1), dtype=f)
    m["state_gdn"] = np.ascontiguousarray(inp["state_gdn"][0, NS * c:NS * (c + 1)], dtype=f)
    m["state_ssm"] = np.ascontiguousarray(np.stack([inp["state_ssm_re"][0, NS * c:NS * (c + 1)].reshape(NS, 2048),
                                                    inp["state_ssm_im"][0, NS * c:NS * (c + 1)].reshape(NS, 2048)],
                                                   axis=1), dtype=f)
    return m


def run(inp, debug=False, trace=False):
    nc, kb, st = _get_nc(debug)
    in_maps = []
    for c in range(NCORES):
        m = host_inputs(inp, c)
        in_maps.append({k: m[k] for k in kb.ins})
    res = run_bass_kernel_spmd(nc, in_maps, core_ids=list(range(NCORES)), trace=trace)
    return res


def kernel(**inp):
    inp = {k: np.asarray(v) for k, v in inp.items()}
    res = run(inp)
    R = res.results
    f = np.float32
    y_p = np.stack([R[c]["y"][:TP] for c in range(NCORES)]).astype(f)
    y_s = np.concatenate([R[c]["y"][TP:] for c in range(NCORES)])[:, None, :].astype(f)
    conv_p = np.stack([R[c]["conv_p"] for c in range(NCORES)])[None].astype(f)
    conv_s = np.concatenate([R[c]["conv_s"] for c in range(NCORES)])[None].astype(f)
    gdn_p = np.stack([R[c]["gdn_p"] for c in range(NCORES)])[None].astype(f)
    re_p = np.stack([R[c]["ssm_p"][0].reshape(32, 64) for c in range(NCORES)])[None].astype(f)
    im_p = np.stack([R[c]["ssm_p"][1].reshape(32, 64) for c in range(NCORES)])[None].astype(f)
    gdn_s = np.concatenate([R[c]["gdn_s"] for c in range(NCORES)])[None].astype(f)
    re_s = np.concatenate([R[c]["ssm_s"][:, 0].reshape(NS, 32, 64) for c in range(NCORES)])[None].astype(f)
    im_s = np.concatenate([R[c]["ssm_s"][:, 1].reshape(NS, 32, 64) for c in range(NCORES)])[None].astype(f)
    return (y_p, y_s, gdn_p, conv_p, re_p, im_p, gdn_s, conv_s, re_s, im_s)
```
